# Optimizing a Trainium2 kernel written in Bass

```python
import math
import jax, jax.numpy as jnp
from jax import lax
import numpy as np

D_MODEL = 1024
BATCH = 32
SEQ = 256
DEPTH = 4
DEC_BATCH = 8
DEC_SEQ = 1024
PAST_LEN = 512

GRID_W = 64
ROPE_BASE = 10000.0
Q_BLOCK = 128
EPS = 1e-6
N_EVEN = (DEPTH + 1) // 2
N_ODD = DEPTH // 2
H_A = 4
DK_A = 64
DV_A = 128
GLA_LR = 16
GLA_TAU = 16.0
GLA_CHUNK = 16
H_B = 4
Q_RANK = 256
KV_RANK = 256
NOPE_B = 128
ROPE_B = 64
V_B = 128
MLA_SCALE = (NOPE_B + ROPE_B) ** -0.5
H_C = 8
DH_C = 64
DIFF_SCALE = DH_C ** -0.5
D_FF = 2816
CONV_W = 3
EVEN_SIZES = (H_A * DK_A, H_A * DK_A, H_A * DV_A, H_A * DV_A, GLA_LR, GLA_LR, Q_RANK, KV_RANK, ROPE_B)
IN_EVEN = sum(EVEN_SIZES)
MIX_EVEN = H_A * DV_A + H_B * V_B
MIX_ODD = H_C * 2 * DH_C
IN_ODD = 3 * MIX_ODD

kernel_name = 'hybrid_flow_gla_mla_diffattn_step'


def rmsnorm(x, g):
    xf = x.astype(jnp.float32)
    y = xf * lax.rsqrt(jnp.mean(xf * xf, axis=-1, keepdims=True) + EPS)
    return (y * g.astype(jnp.float32)).astype(x.dtype)


def modulate(h, shift, scale):
    return h * (1.0 + scale) + shift


def rope_angles(rows, rot_dim):
    t = jnp.arange(rows * GRID_W)
    row = (t // GRID_W).astype(jnp.float32)
    col = (t % GRID_W).astype(jnp.float32)
    half = rot_dim // 2
    inv = ROPE_BASE ** (-jnp.arange(0, half, 2, dtype=jnp.float32) / half)
    return row[:, None] * inv, col[:, None] * inv


def _rot_half(x, ang):
    t, q = ang.shape
    shape = (t,) + (1,) * (x.ndim - 3) + (q,)
    cos = jnp.cos(ang).reshape(shape).astype(x.dtype)
    sin = jnp.sin(ang).reshape(shape).astype(x.dtype)
    x1, x2 = x[..., :q], x[..., q:]
    return jnp.concatenate([x1 * cos - x2 * sin, x2 * cos + x1 * sin], axis=-1)


def axial_rope(x, ang):
    ang_row, ang_col = ang
    half = x.shape[-1] // 2
    return jnp.concatenate([_rot_half(x[..., :half], ang_row), _rot_half(x[..., half:], ang_col)], axis=-1)


def map_query_blocks(fn, q):
    b, t = q.shape[:2]
    nb = t // Q_BLOCK
    qb = jnp.moveaxis(q.reshape((b, nb, Q_BLOCK) + q.shape[2:]), 1, 0)
    ob = lax.map(fn, qb)
    return jnp.moveaxis(ob, 0, 1).reshape((b, t) + ob.shape[3:])


def softmax_attend(q, k, v, scale):
    def blk(qb):
        s = jnp.einsum('bqhd,bkhd->bhqk', qb, k).astype(jnp.float32) * scale
        p = jax.nn.softmax(s, axis=-1).astype(v.dtype)
        return jnp.einsum('bhqk,bkhv->bqhv', p, v)
    return map_query_blocks(blk, q)


def diff_attend(q, k, v, lam, scale):
    def blk(qb):
        s = jnp.einsum('bqhnd,bkhnd->bnhqk', qb, k).astype(jnp.float32) * scale
        p = jax.nn.softmax(s, axis=-1)
        a = (p[:, 0] - lam * p[:, 1]).astype(v.dtype)
        return jnp.einsum('bhqk,bkhv->bqhv', a, v)
    return map_query_blocks(blk, q)


def gla_scan(q, k, v, log_a, s0):
    b, t, h, dk = q.shape
    dv = v.shape[-1]
    n = t // GLA_CHUNK
    f32 = jnp.float32

    def chunks(z):
        return z.reshape((b, n, GLA_CHUNK) + z.shape[2:]).astype(f32)

    qc, kc, vc = chunks(q), chunks(k), chunks(v)
    bcum = jnp.cumsum(chunks(log_a), axis=2)
    idx = jnp.arange(GLA_CHUNK)
    lower = (idx[:, None] >= idx[None, :])[None, None, :, :, None, None]
    rel = jnp.where(lower, bcum[:, :, :, None] - bcum[:, :, None, :], -jnp.inf)
    scores = jnp.einsum('bnthk,bnshk,bntshk->bntsh', qc, kc, jnp.exp(rel))
    o_intra = jnp.einsum('bntsh,bnshv->bnthv', scores, vc)
    b_last = bcum[:, :, -1]
    q_dec = qc * jnp.exp(bcum)
    k_dec = kc * jnp.exp(b_last[:, :, None] - bcum)

    def step(state, inp):
        qd, kd, vv, bl = inp
        o = jnp.einsum('bchk,bhkv->bchv', qd, state)
        state = jnp.exp(bl)[..., None] * state + jnp.einsum('bchk,bchv->bhkv', kd, vv)
        return state, o

    def lead(z):
        return jnp.moveaxis(z, 1, 0)

    s_fin, o_inter = lax.scan(step, s0.astype(f32), (lead(q_dec), lead(k_dec), lead(vc), lead(b_last)))
    o = o_intra + jnp.moveaxis(o_inter, 0, 1)
    return o.reshape(b, t, h, dv).astype(v.dtype), s_fin


def even_project(h, w_in, w_dec, b_dec, q_norm, kv_norm, w_uq):
    b, t = h.shape[:2]
    split_at = np.cumsum(EVEN_SIZES)[:-1].tolist()
    qa, ka, va, ga, lr_f, lr_b, cq, ckv, kr = jnp.split(h @ w_in, split_at, axis=-1)
    qa = qa.reshape(b, t, H_A, DK_A) * (DK_A ** -0.5)
    ka = ka.reshape(b, t, H_A, DK_A)
    va = va.reshape(b, t, H_A, DV_A)
    la_f = jax.nn.log_sigmoid((lr_f @ w_dec[0] + b_dec[0]).astype(jnp.float32)).reshape(b, t, H_A, DK_A) / GLA_TAU
    la_b = jax.nn.log_sigmoid((lr_b @ w_dec[1] + b_dec[1]).astype(jnp.float32)).reshape(b, t, H_A, DK_A) / GLA_TAU
    qb = (rmsnorm(cq, q_norm) @ w_uq).reshape(b, t, H_B, NOPE_B + ROPE_B)
    ckv = rmsnorm(ckv, kv_norm)
    return qa, ka, va, ga, la_f, la_b, qb, ckv, kr


def mla_expand(ckv, kr, w_ukv):
    b, s = ckv.shape[:2]
    kv = (ckv @ w_ukv).reshape(b, s, H_B, NOPE_B + V_B)
    k_nope, v = kv[..., :NOPE_B], kv[..., NOPE_B:]
    k = jnp.concatenate([k_nope, jnp.broadcast_to(kr[:, :, None, :], (b, s, H_B, ROPE_B))], axis=-1)
    return k, v


def gla_output(o_f, o_b_rev, ga, gla_gain):
    b, t = o_f.shape[:2]
    o = o_f + jnp.flip(o_b_rev, axis=1)
    o = rmsnorm(o, gla_gain) * jax.nn.silu(ga.reshape(b, t, H_A, DV_A))
    return o.reshape(b, t, H_A * DV_A)


def even_mixer_context(h, w_in, w_out, w_dec, b_dec, gla_gain, q_norm, kv_norm, w_uq, w_ukv):
    b, t = h.shape[:2]
    qa, ka, va, ga, la_f, la_b, qb, ckv, kr = even_project(h, w_in, w_dec, b_dec, q_norm, kv_norm, w_uq)
    s0 = jnp.zeros((b, H_A, DK_A, DV_A), jnp.float32)
    o_f, s_f = gla_scan(qa, ka, va, la_f, s0)
    fl = lambda z: jnp.flip(z, axis=1)
    o_b, s_b = gla_scan(fl(qa), fl(ka), fl(va), fl(la_b), s0)
    o_gla = gla_output(o_f, o_b, ga, gla_gain)
    k, v = mla_expand(ckv, kr, w_ukv)
    o_mla = softmax_attend(qb, k, v, MLA_SCALE).reshape(b, t, H_B * V_B)
    out = jnp.concatenate([o_gla, o_mla], axis=-1) @ w_out
    return out, jnp.stack([s_f, s_b], axis=1), ckv, kr


def even_mixer_latent(h, st_gla, ckv_ctx, kr_ctx, ang, w_in, w_out, w_dec, b_dec, gla_gain, q_norm, kv_norm, w_uq, w_ukv):
    b, t = h.shape[:2]
    qa, ka, va, ga, la_f, la_b, qb, ckv, kr = even_project(h, w_in, w_dec, b_dec, q_norm, kv_norm, w_uq)
    o_f, _ = gla_scan(qa, ka, va, la_f, st_gla[:, 0])
    fl = lambda z: jnp.flip(z, axis=1)
    o_b, _ = gla_scan(fl(qa), fl(ka), fl(va), fl(la_b), st_gla[:, 1])
    o_gla = gla_output(o_f, o_b, ga, gla_gain)
    qb = jnp.concatenate([qb[..., :NOPE_B], axial_rope(qb[..., NOPE_B:], ang)], axis=-1)
    k_lat, v_lat = mla_expand(ckv, axial_rope(kr, ang), w_ukv)
    k_ctx, v_ctx = mla_expand(ckv_ctx, kr_ctx, w_ukv)
    k = jnp.concatenate([k_ctx, k_lat], axis=1)
    v = jnp.concatenate([v_ctx, v_lat], axis=1)
    o_mla = softmax_attend(qb, k, v, MLA_SCALE).reshape(b, t, H_B * V_B)
    return jnp.concatenate([o_gla, o_mla], axis=-1) @ w_out


def diff_lambda_value(lam_p, lam_init):
    lp = lam_p.astype(jnp.float32)
    return jnp.exp(jnp.sum(lp[0] * lp[1])) - jnp.exp(jnp.sum(lp[2] * lp[3])) + lam_init


def diff_project(h, w_in):
    b, t = h.shape[:2]
    q, k, v = jnp.split(h @ w_in, 3, axis=-1)
    return (q.reshape(b, t, H_C, 2, DH_C), k.reshape(b, t, H_C, 2, DH_C), v.reshape(b, t, H_C, 2 * DH_C))


def diff_output(o, w_out, d_gain, lam_init):
    b, t = o.shape[:2]
    o = rmsnorm(o, d_gain) * (1.0 - lam_init)
    return o.reshape(b, t, MIX_ODD) @ w_out


def diff_mixer_context(h, w_in, w_out, d_gain, lam, lam_init):
    b, t = h.shape[:2]
    q, k, v = diff_project(h, w_in)
    o = diff_attend(q, k, v, lam, DIFF_SCALE)
    return diff_output(o, w_out, d_gain, lam_init), k.reshape(b, t, H_C, 2 * DH_C), v


def diff_mixer_latent(h, k_ctx, v_ctx, ang, w_in, w_out, d_gain, lam, lam_init):
    b = h.shape[0]
    s = k_ctx.shape[1]
    q, k, v = diff_project(h, w_in)
    q = axial_rope(q, ang)
    k = axial_rope(k, ang)
    k_all = jnp.concatenate([k_ctx.reshape(b, s, H_C, 2, DH_C), k], axis=1)
    v_all = jnp.concatenate([v_ctx, v], axis=1)
    o = diff_attend(q, k_all, v_all, lam, DIFF_SCALE)
    return diff_output(o, w_out, d_gain, lam_init)


def conv_ffn(h, w_up, conv_w, conv_b, w_down):
    u = h @ w_up
    p = jnp.pad(u, ((0, 0), (1, 1), (0, 0)))
    u = p[:, :-2] * conv_w[0] + p[:, 1:-1] * conv_w[1] + p[:, 2:] * conv_w[2] + conv_b
    a, g = jnp.split(u, 2, axis=-1)
    return (jax.nn.silu(a) * g) @ w_down


def setup_inputs(seed: int = 0) -> dict:
    key = jax.random.key(seed)
    ks = iter(jax.random.split(key, 32))

    def nrm(shape, scale):
        return jax.random.normal(next(ks), shape, jnp.float32) * scale

    def gain(shape):
        return 1.0 + nrm(shape, 0.05)

    return {
        'x_prompt': nrm((BATCH, SEQ, D_MODEL), 1.0),
        'x_sample': nrm((DEC_BATCH, DEC_SEQ, D_MODEL), 1.0),
        'state_gla': nrm((DEC_BATCH, N_EVEN, 2, H_A, DK_A, DV_A), 1.0),
        'cache_mla_ckv': nrm((DEC_BATCH, N_EVEN, PAST_LEN, KV_RANK), 1.0),
        'cache_mla_krope': nrm((DEC_BATCH, N_EVEN, PAST_LEN, ROPE_B), 1.0),
        'cache_diff_k': nrm((DEC_BATCH, N_ODD, PAST_LEN, H_C, 2 * DH_C), 1.0),
        'cache_diff_v': nrm((DEC_BATCH, N_ODD, PAST_LEN, H_C, 2 * DH_C), 1.0),
        'c': nrm((DEC_BATCH, D_MODEL), 1.0),
        'c_ctx': nrm((D_MODEL,), 1.0),
        'w_ada': nrm((DEPTH, D_MODEL, 6 * D_MODEL), 0.5 * D_MODEL ** -0.5),
        'b_ada': nrm((DEPTH, 6 * D_MODEL), 0.02),
        'norm_gain': gain((DEPTH, 2, D_MODEL)),
        'final_gain': gain((D_MODEL,)),
        'w_in_even': nrm((N_EVEN, D_MODEL, IN_EVEN), D_MODEL ** -0.5),
        'w_out_even': nrm((N_EVEN, MIX_EVEN, D_MODEL), MIX_EVEN ** -0.5),
        'gla_w_decay': nrm((N_EVEN, 2, GLA_LR, H_A * DK_A), GLA_LR ** -0.5),
        'gla_b_decay': nrm((N_EVEN, 2, H_A * DK_A), 0.1),
        'gla_norm': gain((N_EVEN, DV_A)),
        'mla_q_norm': gain((N_EVEN, Q_RANK)),
        'mla_kv_norm': gain((N_EVEN, KV_RANK)),
        'mla_w_uq': nrm((N_EVEN, Q_RANK, H_B * (NOPE_B + ROPE_B)), Q_RANK ** -0.5),
        'mla_w_ukv': nrm((N_EVEN, KV_RANK, H_B * (NOPE_B + V_B)), KV_RANK ** -0.5),
        'w_in_odd': nrm((N_ODD, D_MODEL, IN_ODD), D_MODEL ** -0.5),
        'w_out_odd': nrm((N_ODD, MIX_ODD, D_MODEL), MIX_ODD ** -0.5),
        'diff_lambda': nrm((N_ODD, 4, DH_C), 0.1),
        'diff_norm': gain((N_ODD, 2 * DH_C)),
        'ffn_w_up': nrm((DEPTH, D_MODEL, 2 * D_FF), D_MODEL ** -0.5),
        'ffn_conv_w': nrm((DEPTH, CONV_W, 2 * D_FF), CONV_W ** -0.5),
        'ffn_conv_b': nrm((DEPTH, 2 * D_FF), 0.02),
        'ffn_w_down': nrm((DEPTH, D_FF, D_MODEL), D_FF ** -0.5),
    }


def reference(x_prompt, x_sample, state_gla, cache_mla_ckv, cache_mla_krope, cache_diff_k, cache_diff_v, c,
              c_ctx, w_ada, b_ada, norm_gain, final_gain, w_in_even, w_out_even, gla_w_decay, gla_b_decay,
              gla_norm, mla_q_norm, mla_kv_norm, mla_w_uq, mla_w_ukv, w_in_odd, w_out_odd, diff_lambda,
              diff_norm, ffn_w_up, ffn_conv_w, ffn_conv_b, ffn_w_down):
    rows = x_sample.shape[1] // GRID_W
    ang_b = rope_angles(rows, ROPE_B)
    ang_c = rope_angles(rows, DH_C)
    s_ctx = jax.nn.silu(c_ctx)[None, :]
    s_lat = jax.nn.silu(c)
    xp, xs = x_prompt, x_sample
    gla_states, mla_ckvs, mla_krs, diff_ks, diff_vs = [], [], [], [], []
    for l in range(DEPTH):
        mp = jnp.split((s_ctx @ w_ada[l] + b_ada[l])[:, None, :], 6, axis=-1)
        ms = jnp.split((s_lat @ w_ada[l] + b_ada[l])[:, None, :], 6, axis=-1)
        hp = modulate(rmsnorm(xp, norm_gain[l, 0]), mp[0], mp[1])
        hs = modulate(rmsnorm(xs, norm_gain[l, 0]), ms[0], ms[1])
        i = l // 2
        if l % 2 == 0:
            ep = (w_in_even[i], w_out_even[i], gla_w_decay[i], gla_b_decay[i], gla_norm[i],
                  mla_q_norm[i], mla_kv_norm[i], mla_w_uq[i], mla_w_ukv[i])
            op, st, ckv, kr = even_mixer_context(hp, *ep)
            os_ = even_mixer_latent(hs, state_gla[:, i], cache_mla_ckv[:, i], cache_mla_krope[:, i], ang_b, *ep)
            gla_states.append(st)
            mla_ckvs.append(ckv)
            mla_krs.append(kr)
        else:
            lam_init = 0.8 - 0.6 * math.exp(-0.3 * l)
            lam = diff_lambda_value(diff_lambda[i], lam_init)
            op, k_new, v_new = diff_mixer_context(hp, w_in_odd[i], w_out_odd[i], diff_norm[i], lam, lam_init)
            os_ = diff_mixer_latent(hs, cache_diff_k[:, i], cache_diff_v[:, i], ang_c, w_in_odd[i], w_out_odd[i],
                                    diff_norm[i], lam, lam_init)
            diff_ks.append(k_new)
            diff_vs.append(v_new)
        xp = xp + mp[2] * op
        xs = xs + ms[2] * os_
        hp = modulate(rmsnorm(xp, norm_gain[l, 1]), mp[3], mp[4])
        hs = modulate(rmsnorm(xs, norm_gain[l, 1]), ms[3], ms[4])
        xp = xp + mp[5] * conv_ffn(hp, ffn_w_up[l], ffn_conv_w[l], ffn_conv_b[l], ffn_w_down[l])
        xs = xs + ms[5] * conv_ffn(hs, ffn_w_up[l], ffn_conv_w[l], ffn_conv_b[l], ffn_w_down[l])
    y_prompt = rmsnorm(xp, final_gain)
    y_sample = rmsnorm(xs, final_gain)
    return (y_prompt, y_sample, jnp.stack(gla_states, axis=1), jnp.stack(mla_ckvs, axis=1),
            jnp.stack(mla_krs, axis=1), jnp.stack(diff_ks, axis=1), jnp.stack(diff_vs, axis=1))
```

```python
import math
import os
import contextlib
import numpy as np
import concourse.bass as bass
import concourse.mybir as mybir
from concourse.bass_utils import run_bass_kernel_spmd

F32 = mybir.dt.float32
BF16 = mybir.dt.bfloat16
AF = mybir.ActivationFunctionType
ALU = mybir.AluOpType

NCORES = 8
D = 1024
TT = 1024
DFF = 2816
NJ = 22
EPS = 1e-6
DEPTH = 4
MLA_SCALE = 192 ** -0.5
DIFF_SCALE = 64 ** -0.5
NSLOT = 4
SLOTF = 4096
ARENA_WORDS = 53000
ADA_INTERLEAVE = True


def _blk(W, cols):
    K = W.shape[0]
    nk = K // 128
    sub = W[:, cols]
    return np.ascontiguousarray(sub.reshape(nk, 128, len(cols)).transpose(1, 0, 2).reshape(128, nk * len(cols)))


_P64 = np.concatenate([np.arange(16, 32), np.arange(0, 16), np.arange(48, 64), np.arange(32, 48)])


def _perm_cols(cols):
    cols = np.asarray(cols)
    g = cols.reshape(-1, 64)
    return g[:, _P64].reshape(-1)


def weight_layout():
    L = []
    ar = np.arange

    def add(key, F, fn):
        L.append((key, F, fn))

    for l in range(DEPTH):
        for b in range(12):
            add(("ada", l, b), 4096, lambda inp, l=l, b=b: _blk(inp["w_ada"][l], b * 512 + ar(512)))
    for l in range(DEPTH):
        i = l // 2
        if l % 2 == 0:
            kr = 2080 + ar(64)
            c5 = np.concatenate([1536 + ar(32), kr, _perm_cols(kr)])
            add(("e5", i), 8 * 160, lambda inp, i=i, c5=c5: _blk(inp["w_in_even"][i], c5))
            add(("e3", i), 4096, lambda inp, i=i: _blk(inp["w_in_even"][i], 512 + ar(512)))
            add(("e2", i), 4096, lambda inp, i=i: _blk(inp["w_in_even"][i], 1024 + ar(512)))
            add(("e1", i), 4096, lambda inp, i=i: _blk(inp["w_in_even"][i], ar(512)))
            add(("e4", i), 4096, lambda inp, i=i: _blk(inp["w_in_even"][i], 1568 + ar(512)))
            nope = np.concatenate([h * 192 + ar(128) for h in range(4)])
            rope = np.concatenate([h * 192 + 128 + ar(64) for h in range(4)])
            cuq = np.concatenate([nope, rope, _perm_cols(rope)])
            add(("uq", i), 2 * 1024, lambda inp, i=i, cuq=cuq: _blk(inp["mla_w_uq"][i], cuq))
            kn = np.concatenate([h * 256 + ar(128) for h in range(4)])
            vv = np.concatenate([h * 256 + 128 + ar(128) for h in range(4)])
            ckv = np.concatenate([kn, vv])
            add(("ukv", i), 2 * 1024, lambda inp, i=i, ckv=ckv: _blk(inp["mla_w_ukv"][i], ckv))
            for b in range(2):
                add(("oe", i, b), 4096, lambda inp, i=i, b=b: _blk(inp["w_out_even"][i], b * 512 + ar(512)))
        else:
            for nm, base in (("oq", 0), ("ok", 1024), ("ov", 2048)):
                for b in range(2):
                    cols = base + b * 512 + ar(512)
                    add((nm, i, b), 4096, lambda inp, i=i, cols=cols: _blk(inp["w_in_odd"][i], cols))
                    if nm != "ov":
                        pc = _perm_cols(cols)
                        add((nm + "p", i, b), 4096, lambda inp, i=i, pc=pc: _blk(inp["w_in_odd"][i], pc))
            for b in range(2):
                add(("oo", i, b), 4096, lambda inp, i=i, b=b: _blk(inp["w_out_odd"][i], b * 512 + ar(512)))
        for jb in range(11):
            j0, j1 = 2 * jb, 2 * jb + 1
            cols = np.concatenate([j0 * 128 + ar(128), DFF + j0 * 128 + ar(128), j1 * 128 + ar(128), DFF + j1 * 128 + ar(128)])
            add(("up", l, jb), 4096, lambda inp, l=l, cols=cols: _blk(inp["ffn_w_up"][l], cols))
        for m in range(8):
            add(("dn", l, m), NJ * 128, lambda inp, l=l, m=m: _blk(inp["ffn_w_down"][l], m * 128 + ar(128)))
    return L


_VSPEC = [("bada", 4 * 48), ("ngain", 4 * 2 * 8), ("fgain", 8), ("glan", 2), ("qn", 4), ("kvn", 4), ("dn", 2),
          ("cw", 4 * 3 * 44), ("cb", 4 * 44)]
VOFF = {}
_o = 0
for _n, _c in _VSPEC:
    VOFF[_n] = _o
    _o += _c
VN = _o
CN = 2688


def _fm(v):
    return np.asarray(v, np.float32).reshape(-1, 128).T


def build_vecs(inp):
    V = np.zeros((128, VN), np.float32)
    for l in range(4):
        V[:, VOFF["bada"] + l * 48: VOFF["bada"] + (l + 1) * 48] = _fm(inp["b_ada"][l])
        for s in range(2):
            o = VOFF["ngain"] + (l * 2 + s) * 8
            V[:, o:o + 8] = _fm(inp["norm_gain"][l, s])
        for t in range(3):
            o = VOFF["cw"] + (l * 3 + t) * 44
            V[:, o:o + 44] = _fm(inp["ffn_conv_w"][l, t])
        o = VOFF["cb"] + l * 44
        V[:, o:o + 44] = _fm(inp["ffn_conv_b"][l])
    V[:, VOFF["fgain"]:VOFF["fgain"] + 8] = _fm(inp["final_gain"])
    for i in range(2):
        V[:, VOFF["glan"] + i] = inp["gla_norm"][i]
        V[:, VOFF["qn"] + 2 * i: VOFF["qn"] + 2 * i + 2] = _fm(inp["mla_q_norm"][i])
        V[:, VOFF["kvn"] + 2 * i: VOFF["kvn"] + 2 * i + 2] = _fm(inp["mla_kv_norm"][i])
        V[:, VOFF["dn"] + i] = inp["diff_norm"][i]
    return V


def build_consts():
    C = np.zeros((128, CN), np.float32)
    C[:, 0:128] = np.eye(128, dtype=np.float32)
    C[:, 128:256] = 1.0
    s = np.arange(128)[:, None]
    t = np.arange(128)[None, :]
    C[:, 256:384] = (s <= t)
    C[:, 384:512] = (s >= t)
    tt = np.arange(1024)
    row = (tt // 64).astype(np.float32)
    col = (tt % 64).astype(np.float32)
    inv = (np.float32(10000.0) ** (-np.arange(0, 32, 2, dtype=np.float32) / np.float32(32))).astype(np.float32)
    ar_ = (row[:, None] * inv).astype(np.float32).T
    ac_ = (col[:, None] * inv).astype(np.float32).T
    cosr, sinr, cosc, sinc = np.cos(ar_), np.sin(ar_), np.cos(ac_), np.sin(ac_)
    C64 = np.concatenate([cosr, cosr, cosc, cosc], 0)
    S64 = np.concatenate([-sinr, sinr, -sinc, sinc], 0)
    C[:, 512:1536] = np.concatenate([C64, C64], 0)
    C[:, 1536:2560] = np.concatenate([S64, S64], 0)
    pm = np.concatenate([_P64, 64 + _P64])
    P = np.zeros((128, 128), np.float32)
    P[pm, np.arange(128)] = 1.0
    C[:, 2560:2688] = P
    return C


class Buf:
    __slots__ = ("w", "r", "name")

    def __init__(self, name=""):
        self.w = None
        self.r = {}
        self.name = name


class Sch:
    ENG = ("pe", "act", "dve", "pool", "sp")

    def __init__(self, nc, es):
        self.nc = nc
        self.es = es
        self.eng = {"pe": nc.tensor, "act": nc.scalar, "dve": nc.vector, "pool": nc.gpsimd, "sp": nc.sync}
        self.sem = {}
        self.cnt = {}
        for e in self.ENG:
            self.sem[e] = es.enter_context(nc.semaphore("s_" + e))
            self.cnt[e] = 0
        self.seen = {e: {} for e in self.ENG}
        self.snap = {}
        self.nwait = 0
        self.ninst = 0
        self.npe = 0
        self.marks = []
        self.log = {e: [] for e in self.ENG}

    def chan(self, key):
        if key not in self.sem:
            self.sem[key] = self.es.enter_context(self.nc.semaphore("d_" + key))
            self.cnt[key] = 0

    def _deps(self, reads, writes):
        d = {}
        for b in reads:
            if b.w is not None:
                k, v = b.w
                if d.get(k, 0) < v:
                    d[k] = v
        for b in writes:
            if b.w is not None:
                k, v = b.w
                if d.get(k, 0) < v:
                    d[k] = v
            for k, v in b.r.items():
                if d.get(k, 0) < v:
                    d[k] = v
        return d

    def _wait(self, e, d):
        seen = self.seen[e]
        h = self.eng[e]
        for k, v in d.items():
            if k == e and e == "pe":
                continue
            if seen.get(k, 0) < v:
                h.wait_ge(self.sem[k], v)
                seen[k] = v
                self.nwait += 1
                self.log[e].append(("w", k, v))

    def _mark(self, ev, reads, writes):
        k, v = ev
        for b in reads:
            if b.r.get(k, 0) < v:
                b.r[k] = v
        for b in writes:
            b.w = ev
            b.r = {}

    def op(self, e, fn, reads=(), writes=()):
        self._wait(e, self._deps(reads, writes))
        inst = fn(self.eng[e])
        self.cnt[e] += 1
        inst.then_inc(self.sem[e], 1)
        self.ninst += 1
        self.log[e].append(("i", e, 1))
        self._mark((e, self.cnt[e]), reads, writes)

    def mm(self, fns, reads=(), writes=()):
        self._wait("pe", self._deps(reads, writes))
        inst = None
        for f in fns:
            inst = f(self.eng["pe"])
            self.ninst += 1
            self.npe += 1
        self.cnt["pe"] += 1
        inst.then_inc(self.sem["pe"], 1)
        self.log["pe"].append(("i", "pe", 1))
        self._mark(("pe", self.cnt["pe"]), reads, writes)

    def dma(self, q, key, out, in_, reads=(), writes=(), join=False):
        self.chan(key)
        d = self._deps(reads, writes)
        if self.cnt[key] > 0 and d.get(key, 0) < self.cnt[key]:
            d[key] = self.cnt[key]
        if q == "pool" and join:
            for k, v in self.snap.items():
                if d.get(k, 0) < v:
                    d[k] = v
        self._wait(q, d)
        self.cnt[key] += 16
        self.eng[q].dma_start(out=out, in_=in_).then_inc(self.sem[key], 16)
        self.ninst += 1
        self.log[q].append(("i", key, 16))
        self._mark((key, self.cnt[key]), reads, writes)

    def barrier(self, final=False):
        d = {k: v for k, v in self.cnt.items() if v > 0}
        for e in self.ENG:
            if e == "pool" and not final:
                continue
            self._wait(e, d)
        self.snap = dict(d)


class Arena:
    def __init__(self, ap, nwords):
        self.ap = ap
        self.n = nwords
        self.top = 0
        self.peak = 0

    def f32(self, words):
        off = self.top
        self.top += words
        self.peak = max(self.peak, self.top)
        assert self.top <= self.n, "arena overflow %d > %d" % (self.top, self.n)
        return self.ap[:, off:off + words]

    def bf16(self, elems):
        words = (elems + 1) // 2
        return self.f32(words).bitcast(BF16)


def build_program(nlayers=DEPTH, groups=("s", "p")):
    WL = weight_layout()
    WOFF = {}
    off = 0
    for key, F, _ in WL:
        WOFF[key] = (off, F)
        off += F
    WTOTAL = off

    nc = bass.Bass("TRN2", target_bir_lowering=False)

    def din(name, shape):
        return nc.dram_tensor(name, shape, F32, kind="ExternalInput").ap()

    def dout(name, shape):
        return nc.dram_tensor(name, shape, F32, kind="ExternalOutput").ap()

    Wd = din("wts", [128, WTOTAL])
    constd = din("consts", [128, CN])
    vecd = din("vecs", [128, VN])
    cTd = din("cT", [128, 16])
    xd = {"s": din("xsT", [1024, 1024]), "p": din("xpT", [1024, 1024])}
    gstd = din("gst", [64, 2048])
    ckvTd = din("ckvT", [2, 256, 512])
    krcTd = din("krcT", [2, 64, 512])
    dkTd = din("dkT", [2, 1024, 512])
    dvd = din("dv", [2, 512, 1024])
    wdecd = din("wdec", [32, 1024])
    lampd = din("lamp", [1, 512])
    yd = {"s": dout("ysT", [1024, 1024]), "p": dout("ypT", [1024, 1024])}
    gsod = dout("gso", [64, 8192])
    ckvod = dout("ckvo", [2, 256, 1024])
    krod = dout("kro", [2, 64, 1024])
    dkod = dout("dko", [2, 1024, 1024])
    dvod = dout("dvo", [2, 1024, 1024])

    es = contextlib.ExitStack()
    with es:
        arena_t = es.enter_context(nc.sbuf_tensor("arena", [128, ARENA_WORDS], F32))
        psall = es.enter_context(nc.psum_tensor("ps", [128, 4096], F32))
        psbf_all = psall.bitcast(BF16)
        S = Sch(nc, es)
        AR = Arena(arena_t, ARENA_WORDS)

        PSB = [Buf("ps%d" % i) for i in range(8)]
        st = {"bank": 0, "slot": 0, "och": 0, "alt": 0}

        def ps(b, n=512):
            return psall[:, b * 512:b * 512 + n]

        def psbf(b):
            return psbf_all[:, b * 1024:(b + 1) * 1024]

        def nextbank():
            b = st["bank"]
            st["bank"] = (b + 1) % 8
            return b

        def nextpair():
            b = st["bank"]
            if b % 2:
                b = (b + 1) % 8
            st["bank"] = (b + 2) % 8
            return b

        def ACT(out, in_, func, reads, writes, bias=0.0, scale=1.0):
            S.op("act", lambda e: e.activation(out=out, in_=in_, func=func, bias=bias, scale=scale), reads, writes)

        def TTo(out, a, b, op, reads, writes):
            S.op("dve", lambda e: e.tensor_tensor(out=out, in0=a, in1=b, op=op), reads, writes)

        def TS(out, a, s1, op0, reads, writes, s2=None, op1=None):
            if op1 is None:
                S.op("dve", lambda e: e.tensor_scalar(out=out, in0=a, scalar1=s1, scalar2=None, op0=op0), reads, writes)
            else:
                S.op("dve", lambda e: e.tensor_scalar(out=out, in0=a, scalar1=s1, scalar2=s2, op0=op0, op1=op1), reads, writes)

        def STT(out, a, scalar, b, op0, op1, reads, writes):
            S.op("dve", lambda e: e.scalar_tensor_tensor(out=out, in0=a, scalar=scalar, in1=b, op0=op0, op1=op1), reads, writes)

        def VCOPY(out, in_, reads, writes):
            S.op("dve", lambda e: e.tensor_copy(out=out, in_=in_), reads, writes)

        def COPY(out, in_, reads, writes):
            st["alt"] ^= 1
            if st["alt"]:
                ACT(out, in_, AF.Copy, reads, writes)
            else:
                VCOPY(out, in_, reads, writes)

        def RECIP(out, in_, reads, writes):
            S.op("dve", lambda e: e.reciprocal(out=out, in_=in_), reads, writes)

        def MM(out, pairs, reads, writes, first=True, last=True):
            n = len(pairs)
            fns = []
            for idx, (l_, r_) in enumerate(pairs):
                fns.append(lambda e, l_=l_, r_=r_, idx=idx: e.matmul(out, lhsT=l_, rhs=r_, start=(first and idx == 0),
                                                                    stop=(last and idx == n - 1)))
            S.mm(fns, reads, writes)

        def OUT_DMA(dst, src, srcbuf):
            k = "o%d" % st["och"]
            st["och"] = (st["och"] + 1) % 4
            S.dma("sp", k, dst, src, reads=[srcbuf], writes=[])

        CONST = AR.f32(CN)
        CONSTB = Buf("const")
        ident_f = CONST[:, 0:128]
        ones_f = CONST[:, 128:256]
        TRI = [CONST[:, 256:384], CONST[:, 384:512]]
        ropeC = CONST[:, 512:1536]
        ropeS = CONST[:, 1536:2560]
        cbf = AR.bf16(384)
        ident_b = cbf[:, 0:128]
        ones_b = cbf[:, 128:256]
        perm_b = cbf[:, 256:384]
        VEC = AR.f32(VN)
        VECB = Buf("vec")
        MOD = AR.f32(4 * 96)
        MOD4 = MOD.rearrange("p (l c g) -> p l c g", l=4, g=2)
        MODB = Buf("mod")
        cT = AR.f32(16)
        sTb = AR.bf16(16)
        sT3 = sTb.rearrange("p (k g) -> p k g", g=2)
        ABt = AR.f32(64)
        AB4 = ABt.rearrange("p (l s k) -> p l s k", l=4, s=2)
        ABB = Buf("ab")
        misc = AR.f32(16)
        MISCB = Buf("misc")
        lamt = AR.f32(512 + 16)
        wdecb = AR.bf16(1024)
        wdec3 = wdecb.rearrange("p (a c) -> p a c", a=4)
        WDECB = Buf("wdec")
        xT = AR.f32(8192)
        xT3 = xT.rearrange("p (k t) -> p k t", k=8)
        XB = [[Buf("x%d%d" % (k, t)) for t in range(2)] for k in range(8)]
        XALL = [XB[k][t] for k in range(8) for t in range(2)]
        hT = AR.bf16(8192)
        hT3 = hT.rearrange("p (k t) -> p k t", k=8)
        HB = Buf("h")
        mixT = AR.bf16(8192)
        mixT3 = mixT.rearrange("p (k t) -> p k t", k=8)
        MIXB = [Buf("mix%d" % k) for k in range(8)]
        slots = [AR.bf16(SLOTF) for _ in range(NSLOT)]
        SLB = [Buf("slot%d" % k) for k in range(NSLOT)]

        def wload(key):
            off, F = WOFF[key]
            k = st["slot"]
            st["slot"] = (k + 1) % NSLOT
            S.dma("pool", "w%d" % k, slots[k][:, 0:F], Wd[:, off:off + F], reads=[], writes=[SLB[k]])
            return slots[k][:, 0:F], SLB[k]

        def w3(ap, nk):
            return ap.rearrange("p (k c) -> p k c", k=nk)

        S.dma("sp", "i0", CONST, constd, writes=[CONSTB])
        S.dma("sp", "i1", VEC, vecd, writes=[VECB])
        S.dma("sp", "i2", cT, cTd, writes=[MODB])
        S.dma("sp", "i3", lamt[0:1, 0:512], lampd, writes=[MISCB])
        S.dma("pool", "i4", wdecb[0:32, :], wdecd, writes=[WDECB])
        VCOPY(cbf[:, 0:256], CONST[:, 0:256], [CONSTB], [CONSTB])
        VCOPY(perm_b, CONST[:, 2560:2688], [CONSTB], [CONSTB])
        ACT(sTb, cT, AF.Silu, [MODB], [MODB])
        lam_init = [0.8 - 0.6 * math.exp(-0.3 * (2 * i + 1)) for i in range(2)]
        lsc = lamt[0:1, 512:528]
        for i in range(2):
            base = i * 256
            TTo(lamt[0:1, base:base + 64], lamt[0:1, base:base + 64], lamt[0:1, base + 64:base + 128], ALU.mult, [MISCB], [MISCB])
            TTo(lamt[0:1, base + 128:base + 192], lamt[0:1, base + 128:base + 192], lamt[0:1, base + 192:base + 256], ALU.mult, [MISCB], [MISCB])
            S.op("dve", lambda e: e.reduce_sum(out=lsc[:, 4 * i:4 * i + 1], in_=lamt[0:1, base:base + 64], axis=mybir.AxisListType.X), [MISCB], [MISCB])
            S.op("dve", lambda e: e.reduce_sum(out=lsc[:, 4 * i + 1:4 * i + 2], in_=lamt[0:1, base + 128:base + 192], axis=mybir.AxisListType.X), [MISCB], [MISCB])
            ACT(lsc[:, 4 * i:4 * i + 2], lsc[:, 4 * i:4 * i + 2], AF.Exp, [MISCB], [MISCB])
            TTo(lsc[:, 4 * i + 2:4 * i + 3], lsc[:, 4 * i + 1:4 * i + 2], lsc[:, 4 * i:4 * i + 1], ALU.subtract, [MISCB], [MISCB])
            TS(lsc[:, 4 * i + 2:4 * i + 3], lsc[:, 4 * i + 2:4 * i + 3], -lam_init[i], ALU.add, [MISCB], [MISCB])
            b = nextbank()
            MM(ps(b)[:, 0:1], [(ones_f[0:1, :], lsc[:, 4 * i + 2:4 * i + 3])], [MISCB, CONSTB], [PSB[b]])
            VCOPY(misc[:, i:i + 1], ps(b)[:, 0:1], [PSB[b]], [MISCB])
            TS(misc[:, 2 + i:3 + i], VEC[:, VOFF["dn"] + i:VOFF["dn"] + i + 1], 1.0 - lam_init[i], ALU.mult, [VECB, MISCB], [MISCB])

        MODL = [Buf("mod%d" % l) for l in range(4)]

        def ada_block(l, blk):
            sl, sb = wload(("ada", l, blk))
            s3 = w3(sl, 8)
            b = nextbank()
            for oc in range(4):
                MM(ps(b)[:, oc * 2:oc * 2 + 2], [(s3[:, kc, oc * 128:(oc + 1) * 128], sT3[:, kc, :]) for kc in range(8)],
                   [sb, MODB], [PSB[b]])
            pv = ps(b)[:, 0:8].rearrange("p (c g) -> p c g", g=2)
            c0 = blk * 4
            for g in range(2):
                TTo(MOD4[:, l, c0:c0 + 4, g], pv[:, :, g], VEC[:, VOFF["bada"] + l * 48 + c0: VOFF["bada"] + l * 48 + c0 + 4], ALU.add,
                    [PSB[b], VECB], [MODL[l]])
        for l_ in range(1 if ADA_INTERLEAVE else 4):
            for blk in range(12):
                ada_block(l_, blk)

        def run_group(gname):
            sample = gname == "s"
            g = 1 if sample else 0
            nseq, L = (1, 1024) if sample else (4, 256)
            cps = L // 128
            for kc in range(8):
                S.dma("sp", "x%d" % (kc % 2), xT3[:, kc, :], xd[gname][kc * 128:(kc + 1) * 128, :], writes=[XB[kc][0], XB[kc][1]])
            for l in range(4):
                for s in range(2):
                    pass

            def mk_ab(l):
                for s in range(2):
                    STT(AB4[:, l, s, :], MOD4[:, l, (1 + 3 * s) * 8:(2 + 3 * s) * 8, g], 1.0,
                        VEC[:, VOFF["ngain"] + (l * 2 + s) * 8: VOFF["ngain"] + (l * 2 + s + 1) * 8], ALU.add, ALU.mult,
                        [MODL[l], VECB], [ABB])

            def modcol(l, part, kc):
                return MOD4[:, l, part * 8 + kc, g:g + 1]

            def rms_stats(src_fn, srcbufs_fn, nchunks, th, inv_n, tmp):
                sq, sqB, rs, rsB = tmp
                b = nextbank()
                for c in range(nchunks):
                    ACT(sq[c % 2], src_fn(c, th), AF.Square, srcbufs_fn(c, th), [sqB[c % 2]])
                    MM(ps(b), [(ones_b, sq[c % 2])], [sqB[c % 2], CONSTB], [PSB[b]], first=(c == 0), last=(c == nchunks - 1))
                ACT(rs, ps(b), AF.Ln, [PSB[b]], [rsB], bias=EPS, scale=inv_n)
                ACT(rs, rs, AF.Exp, [rsB], [rsB], scale=-0.5)
                return rs

            def mk_tmp():
                sq = [AR.bf16(512), AR.bf16(512)]
                return (sq, [Buf(), Buf()], AR.f32(512), Buf())

            def norm_mod(l, s):
                m = AR.top
                tmps = [mk_tmp(), mk_tmp()]
                t2 = [AR.f32(512), AR.f32(512)]
                t2B = [Buf(), Buf()]
                for th in range(2):
                    tsl = slice(th * 512, (th + 1) * 512)
                    rs = rms_stats(lambda c, th: xT3[:, c, th * 512:(th + 1) * 512], lambda c, th: [XB[c][th]], 8, th, 1.0 / D, tmps[th])
                    rsB = tmps[th][3]
                    for kc in range(8):
                        TTo(t2[kc % 2], xT3[:, kc, tsl], rs, ALU.mult, [XB[kc][th], rsB], [t2B[kc % 2]])
                        ACT(hT3[:, kc, tsl], t2[kc % 2], AF.Identity, [t2B[kc % 2], ABB, MODL[l]], [HB],
                            bias=modcol(l, 3 * s, kc), scale=AB4[:, l, s, kc:kc + 1])
                AR.top = m

            def out_proj(key, i, l):
                for b2 in range(2):
                    sl, sb = wload((key, i, b2))
                    s3 = w3(sl, 8)
                    for mo in range(4):
                        mc = b2 * 4 + mo
                        for th in range(2):
                            tsl = slice(th * 512, (th + 1) * 512)
                            b = nextbank()
                            MM(ps(b), [(s3[:, kc, mo * 128:(mo + 1) * 128], mixT3[:, kc, tsl]) for kc in range(8)],
                               [sb] + MIXB, [PSB[b]])
                            STT(xT3[:, mc, tsl], ps(b), modcol(l, 2, mc), xT3[:, mc, tsl], ALU.mult, ALU.add,
                                [PSB[b], MODL[l], XB[mc][th]], [XB[mc][th]])

            def rope_evac(dst, b0, b1, rows, tsl, dstbufs, tmp):
                (t1, t1B, t2_, t2B_) = tmp
                TTo(t1[0:rows, :], ps(b0)[0:rows, :], ropeC[0:rows, tsl], ALU.mult, [PSB[b0], CONSTB], [t1B])
                TTo(t2_[0:rows, :], ps(b1)[0:rows, :], ropeS[0:rows, tsl], ALU.mult, [PSB[b1], CONSTB], [t2B_])
                TTo(dst, t1[0:rows, :], t2_[0:rows, :], ALU.add, [t1B, t2B_], dstbufs)

            def attn_blocks():
                if sample:
                    return [(0, qb * 512, 512, list(range(12))) for qb in range(2)]
                return [(s_, s_ * 256, 256, [2 * s_, 2 * s_ + 1]) for s_ in range(4)]

            def mixer_even(i, l):
                m0 = AR.top
                sl5, sb5 = wload(("e5", i))
                s53 = w3(sl5, 8)
                lrT = [AR.bf16(1024), AR.bf16(1024)]
                lrB = [Buf(), Buf()]
                for d in range(2):
                    S.op("dve", lambda e: e.memset(lrT[d][0:32, :], 1.0), [], [lrB[d]])
                    for th in range(2):
                        b = nextbank()
                        MM(ps(b)[0:16, :], [(s53[:, kc, d * 16:(d + 1) * 16], hT3[:, kc, th * 512:(th + 1) * 512]) for kc in range(8)],
                           [sb5, HB], [PSB[b]])
                        ACT(lrT[d][0:16, th * 512:(th + 1) * 512], ps(b)[0:16, :], AF.Copy, [PSB[b]], [lrB[d]])
                spT = AR.f32(4096)
                spT3 = spT.rearrange("p (c f) -> p c f", c=8)
                SPB = Buf()
                etmp = [AR.f32(512), AR.f32(512)]
                etB = [Buf(), Buf()]
                for tc in range(8):
                    b = nextbank()
                    for d in range(2):
                        MM(ps(b)[:, d * 256:(d + 1) * 256], [(lrT[d][0:17, tc * 128:(tc + 1) * 128], wdec3[0:17, i * 2 + d, :])],
                           [lrB[d], WDECB], [PSB[b]])
                    ACT(etmp[tc % 2], ps(b), AF.Exp, [PSB[b]], [etB[tc % 2]], scale=-1.0)
                    ACT(spT3[:, tc, :], etmp[tc % 2], AF.Ln, [etB[tc % 2]], [SPB], bias=1.0)
                sl3, sb3 = wload(("e3", i))
                s33 = w3(sl3, 8)
                vTok = AR.bf16(4096)
                vT3 = vTok.rearrange("p (c f) -> p c f", c=8)
                VTB = Buf()
                for tc in range(8):
                    b = nextbank()
                    MM(ps(b), [(hT3[:, kc, tc * 128:(tc + 1) * 128], s33[:, kc, :]) for kc in range(8)], [sb3, HB], [PSB[b]])
                    COPY(vT3[:, tc, :], ps(b), [PSB[b]], [VTB])
                slg, sbg = wload(("e2", i))
                sg3 = w3(slg, 8)
                slqk, sbqk = wload(("e1", i))
                sqk3 = w3(slqk, 8)
                Sst = AR.f32(1024)
                Sst3 = Sst.rearrange("p (a v) -> p a v", a=8)
                Sbf = AR.bf16(1024)
                Sbf3 = Sbf.rearrange("p (a v) -> p a v", a=8)
                SSB = [Buf() for _ in range(8)]
                SBB = [Buf() for _ in range(8)]
                if sample:
                    S.dma("sp", "i5", Sst[0:64, :], gstd[:, i * 1024:(i + 1) * 1024], writes=SSB)
                    VCOPY(Sbf[0:64, :], Sst[0:64, :], SSB, SBB)
                else:
                    stage = AR.f32(1024)
                    stage3 = stage.rearrange("p (a v) -> p a v", a=8)
                    STGB = Buf()
                eb = [AR.f32(1024), AR.f32(1024)]
                enb = [AR.f32(1024), AR.f32(1024)]
                EBB = [Buf(), Buf()]
                ENB = [Buf(), Buf()]
                Qt = [AR.bf16(1024), AR.bf16(1024)]
                Kt = [AR.bf16(1024), AR.bf16(1024)]
                QKB = [Buf(), Buf()]
                Ktok = [AR.bf16(512), AR.bf16(512)]
                Ktok3 = [k_.rearrange("p (c f) -> p c f", c=8) for k_ in Ktok]
                KTB = [Buf(), Buf()]
                Am = [AR.bf16(128) for _ in range(4)]
                AMB = [Buf() for _ in range(4)]
                oT = AR.f32(1024)
                OTB = Buf()
                ntmp = mk_tmp()
                gsb = AR.f32(512)
                GSB = Buf()
                t3 = AR.f32(512)
                T3B = Buf()
                amk = 0
                for h in range(4):
                    for d in range(2):
                        bp = nextpair()
                        for tc in range(8):
                            MM(psall[0:64, bp * 512 + tc * 128: bp * 512 + (tc + 1) * 128],
                               [(spT3[:, tc, d * 256 + h * 64: d * 256 + (h + 1) * 64], TRI[d])], [SPB, CONSTB], [PSB[bp], PSB[bp + 1]])
                        ACT(eb[d][0:64, :], psall[0:64, bp * 512:bp * 512 + 1024], AF.Exp, [PSB[bp], PSB[bp + 1]], [EBB[d]], scale=-1.0 / 16)
                        ACT(enb[d][0:64, :], psall[0:64, bp * 512:bp * 512 + 1024], AF.Exp, [PSB[bp], PSB[bp + 1]], [ENB[d]], scale=1.0 / 16)
                    for which in range(2):
                        coloff = which * 256 + h * 64
                        for th in range(2):
                            tsl = slice(th * 512, (th + 1) * 512)
                            b = nextbank()
                            MM(ps(b)[0:64, :], [(sqk3[:, kc, coloff:coloff + 64], hT3[:, kc, tsl]) for kc in range(8)], [sbqk, HB], [PSB[b]])
                            for d in range(2):
                                if which == 0:
                                    STT(Qt[d][0:64, tsl], ps(b)[0:64, :], 0.125, eb[d][0:64, tsl], ALU.mult, ALU.mult,
                                        [PSB[b], EBB[d]], [QKB[d]])
                                else:
                                    TTo(Kt[d][0:64, tsl], ps(b)[0:64, :], enb[d][0:64, tsl], ALU.mult, [PSB[b], ENB[d]], [QKB[d]])
                    for d in range(2):
                        b = nextbank()
                        S.mm([lambda e, tc=tc: e.transpose(psbf(b)[:, tc * 64:(tc + 1) * 64], Kt[d][0:64, tc * 128:(tc + 1) * 128], ident_b[0:64, 0:64])
                              for tc in range(8)], [QKB[d], CONSTB], [PSB[b]])
                        COPY(Ktok[d], psbf(b)[:, 0:512], [PSB[b]], [KTB[d]])
                    op_ = [nextpair(), nextpair()]
                    accb = (op_[0], op_[0] + 1, op_[1], op_[1] + 1)

                    def fb():
                        b_ = nextbank()
                        while b_ in accb:
                            b_ = nextbank()
                        return b_

                    def emitA(step):
                        for d in range(2):
                            tc = step if d == 0 else 7 - step
                            tcs = slice(tc * 128, (tc + 1) * 128)
                            b = fb()
                            MM(ps(b)[:, 0:128], [(Kt[d][0:64, tcs], Qt[d][0:64, tcs])], [QKB[d]], [PSB[b]])
                            k_ = (step % 2) * 2 + d
                            TTo(Am[k_], ps(b)[:, 0:128], TRI[d], ALU.mult, [PSB[b], CONSTB], [AMB[k_]])
                    emitA(0)
                    for step in range(8):
                        if step + 1 < 8:
                            emitA(step + 1)
                        flags = []
                        for d in range(2):
                            tc = step if d == 0 else 7 - step
                            pos = tc % cps
                            first_in_seq = (pos == 0) if d == 0 else (pos == cps - 1)
                            last_in_seq = (pos == cps - 1) if d == 0 else (pos == 0)
                            need_state = (not last_in_seq) or (not sample)
                            b2 = None
                            if need_state:
                                b2 = fb()
                                MM(ps(b2)[0:64, 0:128], [(Ktok3[d][:, tc, :], vT3[:, tc, h * 128:(h + 1) * 128])], [KTB[d], VTB], [PSB[b2]])
                            flags.append((tc, first_in_seq, last_in_seq, need_state, b2))
                        for d in range(2):
                            tc, first_in_seq, last_in_seq, need_state, b2 = flags[d]
                            tcs = slice(tc * 128, (tc + 1) * 128)
                            sidx = d * 4 + h
                            k_ = (step % 2) * 2 + d
                            pairs = [(vT3[:, tc, h * 128:(h + 1) * 128], Am[k_])]
                            rd = [VTB, AMB[k_]]
                            if sample or not first_in_seq:
                                pairs.append((Sbf3[0:64, sidx, :], Qt[d][0:64, tcs]))
                                rd += [SBB[sidx], QKB[d]]
                            ob = op_[d]
                            MM(psall[:, ob * 512 + tc * 128: ob * 512 + (tc + 1) * 128], pairs, rd, [PSB[ob], PSB[ob + 1]])
                        for d in range(2):
                            tc, first_in_seq, last_in_seq, need_state, b2 = flags[d]
                            sidx = d * 4 + h
                            if need_state:
                                col = tc * 128 + 127 if d == 0 else tc * 128
                                ebc = eb[d][0:64, col:col + 1]
                                final_out = last_in_seq and not sample
                                if final_out:
                                    dst = stage3[0:64, (tc // cps) * 2 + d, :]
                                    dB = [STGB]
                                else:
                                    dst = Sst3[0:64, sidx, :]
                                    dB = [SSB[sidx]]
                                if first_in_seq and not sample:
                                    TS(dst, ps(b2)[0:64, 0:128], ebc, ALU.mult, [PSB[b2], EBB[d]], dB)
                                else:
                                    TS(Sst3[0:64, sidx, :], Sst3[0:64, sidx, :], ebc, ALU.mult, [SSB[sidx], EBB[d]], [SSB[sidx]])
                                    STT(dst, ps(b2)[0:64, 0:128], ebc, Sst3[0:64, sidx, :], ALU.mult, ALU.add,
                                        [PSB[b2], EBB[d], SSB[sidx]], dB)
                                if not last_in_seq:
                                    ACT(Sbf3[0:64, sidx, :], Sst3[0:64, sidx, :], AF.Copy, [SSB[sidx]], [SBB[sidx]])
                    if not sample:
                        for sq_ in range(4):
                            for d in range(2):
                                o_ = (((sq_ * 2 + i) * 2 + d) * 4 + h) * 128
                                OUT_DMA(gsod[:, o_:o_ + 128], stage3[0:64, sq_ * 2 + d, :], STGB)
                    ACT(oT, psall[:, op_[0] * 512: op_[0] * 512 + 1024], AF.Copy, [PSB[op_[0]], PSB[op_[0] + 1]], [OTB])
                    TTo(oT, psall[:, op_[1] * 512: op_[1] * 512 + 1024], oT, ALU.add, [PSB[op_[1]], PSB[op_[1] + 1], OTB], [OTB])
                    for th in range(2):
                        tsl = slice(th * 512, (th + 1) * 512)
                        rs = rms_stats(lambda c, th: oT[:, th * 512:(th + 1) * 512], lambda c, th: [OTB], 1, th, 1.0 / 128, ntmp)
                        b = nextbank()
                        MM(ps(b), [(sg3[:, kc, h * 128:(h + 1) * 128], hT3[:, kc, tsl]) for kc in range(8)], [sbg, HB], [PSB[b]])
                        ACT(gsb, ps(b), AF.Silu, [PSB[b]], [GSB])
                        TTo(t3, oT[:, tsl], rs, ALU.mult, [OTB, ntmp[3]], [T3B])
                        STT(mixT3[:, h, tsl], t3, VEC[:, VOFF["glan"] + i:VOFF["glan"] + i + 1], gsb, ALU.mult, ALU.mult,
                            [T3B, VECB, GSB], [MIXB[h]])
                S.barrier()
                AR.top = m0
                S.marks.append((gname + "%d.mla" % l, S.npe))
                SK = 1536 if sample else 1024
                KOFF = 512 if sample else 0
                sl5, sb5 = wload(("e5", i))
                s53 = w3(sl5, 8)
                sl4, sb4 = wload(("e4", i))
                s43 = w3(sl4, 8)
                cbuf = AR.f32(2048)
                cb3 = cbuf.rearrange("p (c t) -> p c t", c=2)
                CBB = Buf()
                cqn = AR.bf16(2048)
                cqn3 = cqn.rearrange("p (c t) -> p c t", c=2)
                CQB = Buf()
                ckn = AR.bf16(2 * SK)
                ckn3 = ckn.rearrange("p (c t) -> p c t", c=2)
                CKB = Buf()
                krT = AR.bf16(SK)
                KRB = Buf()
                ntmp1 = mk_tmp()
                ntmps = [ntmp1, ntmp1]
                stg = [AR.f32(512) for _ in range(2 if sample else 4)]
                STB = [Buf() for _ in range(4)]
                sk = [0]

                def nstg():
                    k_ = sk[0] % 4
                    sk[0] += 1
                    return stg[k_], STB[k_]
                rtmp = (stg[0], STB[0], stg[1], STB[1])
                if sample:
                    for c in range(2):
                        S.dma("pool", "i6", ckn3[:, c, 0:512], ckvTd[i, c * 128:(c + 1) * 128, :], writes=[CKB], join=True)
                    S.dma("pool", "i6", krT[0:64, 0:512], krcTd[i], writes=[KRB], join=True)
                for kind in range(2):
                    coloff = kind * 256
                    gname_ = "qn" if kind == 0 else "kvn"
                    for th in range(2):
                        tsl = slice(th * 512, (th + 1) * 512)
                        for c in range(2):
                            b = nextbank()
                            MM(ps(b), [(s43[:, kc, coloff + c * 128: coloff + (c + 1) * 128], hT3[:, kc, tsl]) for kc in range(8)],
                               [sb4, HB], [PSB[b]])
                            COPY(cb3[:, c, tsl], ps(b), [PSB[b]], [CBB])
                    for th in range(2):
                        tsl = slice(th * 512, (th + 1) * 512)
                        rs = rms_stats(lambda c, th: cb3[:, c, th * 512:(th + 1) * 512], lambda c, th: [CBB], 2, th, 1.0 / 256, ntmps[th])
                        rsB = ntmps[th][3]
                        for c in range(2):
                            gcol = VEC[:, VOFF[gname_] + 2 * i + c: VOFF[gname_] + 2 * i + c + 1]
                            if kind == 0:
                                STT(cqn3[:, c, tsl], cb3[:, c, tsl], gcol, rs, ALU.mult, ALU.mult, [CBB, VECB, rsB], [CQB])
                            elif sample:
                                STT(ckn3[:, c, 512 + th * 512: 512 + (th + 1) * 512], cb3[:, c, tsl], gcol, rs, ALU.mult, ALU.mult,
                                    [CBB, VECB, rsB], [CKB])
                            else:
                                sg_, sgB = nstg()
                                STT(sg_, cb3[:, c, tsl], gcol, rs, ALU.mult, ALU.mult, [CBB, VECB, rsB], [sgB])
                                ACT(ckn3[:, c, tsl], sg_, AF.Copy, [sgB], [CKB])
                                OUT_DMA(ckvod[i, c * 128:(c + 1) * 128, tsl], sg_, sgB)
                for th in range(2):
                    tsl = slice(th * 512, (th + 1) * 512)
                    b = nextbank()
                    MM(ps(b)[0:64, :], [(s53[:, kc, 32:96], hT3[:, kc, tsl]) for kc in range(8)], [sb5, HB], [PSB[b]])
                    if sample:
                        b1 = nextbank()
                        MM(ps(b1)[0:64, :], [(s53[:, kc, 96:160], hT3[:, kc, tsl]) for kc in range(8)], [sb5, HB], [PSB[b1]])
                        rope_evac(krT[0:64, 512 + th * 512: 512 + (th + 1) * 512], b, b1, 64, tsl, [KRB], rtmp)
                    else:
                        sg_, sgB = nstg()
                        ACT(sg_[0:64, :], ps(b)[0:64, :], AF.Copy, [PSB[b]], [sgB])
                        VCOPY(krT[0:64, tsl], sg_[0:64, :], [sgB], [KRB])
                        OUT_DMA(krod[i, :, tsl], sg_[0:64, :], sgB)
                sluq, sbuq = wload(("uq", i))
                suq3 = w3(sluq, 2)
                qn = AR.bf16(4096)
                qn3 = qn.rearrange("p (h t) -> p h t", h=4)
                QNB = Buf()
                qr = AR.bf16(4096)
                qr3 = qr.rearrange("p (h t) -> p h t", h=4)
                QRB = Buf()
                for h in range(4):
                    for th in range(2):
                        tsl = slice(th * 512, (th + 1) * 512)
                        b = nextbank()
                        MM(ps(b), [(suq3[:, kc, h * 128:(h + 1) * 128], cqn3[:, kc, tsl]) for kc in range(2)], [sbuq, CQB], [PSB[b]])
                        COPY(qn3[:, h, tsl], ps(b), [PSB[b]], [QNB])
                        b = nextbank()
                        MM(ps(b)[0:64, :], [(suq3[:, kc, 512 + h * 64: 512 + (h + 1) * 64], cqn3[:, kc, tsl]) for kc in range(2)],
                           [sbuq, CQB], [PSB[b]])
                        if sample:
                            b1 = nextbank()
                            MM(ps(b1)[0:64, :], [(suq3[:, kc, 768 + h * 64: 768 + (h + 1) * 64], cqn3[:, kc, tsl]) for kc in range(2)],
                               [sbuq, CQB], [PSB[b1]])
                            rope_evac(qr3[0:64, h, tsl], b, b1, 64, tsl, [QRB], rtmp)
                        else:
                            COPY(qr3[0:64, h, tsl], ps(b)[0:64, :], [PSB[b]], [QRB])
                slkv, sbkv = wload(("ukv", i))
                skv3 = w3(slkv, 2)
                kn = AR.bf16(4 * SK)
                kn3 = kn.rearrange("p (h t) -> p h t", h=4)
                KNB = Buf()
                nkc = SK // 128
                vml = AR.bf16(nkc * 512)
                vml3 = vml.rearrange("p (c f) -> p c f", c=nkc)
                VMB = Buf()
                for h in range(4):
                    for kt in range(SK // 512):
                        ksl = slice(kt * 512, (kt + 1) * 512)
                        b = nextbank()
                        MM(ps(b), [(skv3[:, kc, h * 128:(h + 1) * 128], ckn3[:, kc, ksl]) for kc in range(2)], [sbkv, CKB], [PSB[b]])
                        COPY(kn3[:, h, ksl], ps(b), [PSB[b]], [KNB])
                for c2 in range(nkc):
                    b = nextbank()
                    MM(ps(b), [(ckn3[:, kc, c2 * 128:(c2 + 1) * 128], skv3[:, kc, 512:1024]) for kc in range(2)], [sbkv, CKB], [PSB[b]])
                    COPY(vml3[:, c2, :], ps(b), [PSB[b]], [VMB])
                PT = [AR.bf16(512) for _ in range(4)]
                PTB = [Buf() for _ in range(4)]
                rden = [AR.f32(512), AR.f32(512)]
                RDB = [Buf(), Buf()]
                it = 0
                for (_sq, q0, nq, kcs) in attn_blocks():
                    qsl = slice(q0, q0 + nq)
                    for h in range(4):
                        bo, bd = (4, 5) if it % 2 == 0 else (6, 7)
                        n = len(kcs)

                        def emitS(j):
                            sb_ = j % 4
                            ks = slice(kcs[j] * 128, (kcs[j] + 1) * 128)
                            MM(ps(sb_)[:, 0:nq], [(kn3[:, h, ks], qn3[:, h, qsl]), (krT[0:64, ks], qr3[0:64, h, qsl])],
                               [KNB, QNB, KRB, QRB], [PSB[sb_]])
                        for j in range(min(2, n)):
                            emitS(j)
                        for j in range(n):
                            sb_ = j % 4
                            ACT(PT[sb_][:, 0:nq], ps(sb_)[:, 0:nq], AF.Exp, [PSB[sb_]], [PTB[sb_]], scale=MLA_SCALE)
                            MM(ps(bo)[:, 0:nq], [(vml3[:, kcs[j], h * 128:(h + 1) * 128], PT[sb_][:, 0:nq])], [VMB, PTB[sb_]], [PSB[bo]],
                               first=(j == 0), last=(j == n - 1))
                            MM(ps(bd)[:, 0:nq], [(ones_b, PT[sb_][:, 0:nq])], [CONSTB, PTB[sb_]], [PSB[bd]], first=(j == 0), last=(j == n - 1))
                            if j + 2 < n:
                                emitS(j + 2)
                        r_ = it % 2
                        ACT(rden[r_][:, 0:nq], ps(bd)[:, 0:nq], AF.Ln, [PSB[bd]], [RDB[r_]])
                        ACT(rden[r_][:, 0:nq], rden[r_][:, 0:nq], AF.Exp, [RDB[r_]], [RDB[r_]], scale=-1.0)
                        TTo(mixT3[:, 4 + h, qsl], ps(bo)[:, 0:nq], rden[r_][:, 0:nq], ALU.mult, [PSB[bo], RDB[r_]], [MIXB[4 + h]])
                        it += 1
                out_proj("oe", i, l)
                S.barrier()
                AR.top = m0

            def mixer_odd(i, l):
                m0 = AR.top
                SK = 1536 if sample else 1024
                nkc = SK // 128
                qT_ = AR.bf16(8192)
                q3 = qT_.rearrange("p (h t) -> p h t", h=8)
                QB = Buf()
                kT_ = AR.bf16(8 * SK)
                k3 = kT_.rearrange("p (h t) -> p h t", h=8)
                KB = Buf()
                vTok = AR.bf16(nkc * 1024)
                v3 = vTok.rearrange("p (c f) -> p c f", c=nkc)
                VB = Buf()
                PT = [AR.bf16(512) for _ in range(4 if sample else 0)]
                PTB = [Buf() for _ in range(4)]
                tA = [AR.f32(512) for _ in range(2)]
                tAB = [Buf() for _ in range(2)]
                rtmp = (tA[0], tAB[0], tA[1], tAB[1])
                ntmp = mk_tmp()
                if not sample:
                    stg = [AR.f32(512) for _ in range(4)]
                    STB = [Buf() for _ in range(4)]
                sk = [0]
                rawk = [0]
                if sample:
                    for c in range(8):
                        S.dma("pool", "i6", k3[:, c, 0:512], dkTd[i, c * 128:(c + 1) * 128, :], writes=[KB], join=True)
                    for c in range(4):
                        S.dma("pool", "i7", v3[:, c, :], dvd[i, c * 128:(c + 1) * 128, :], writes=[VB], join=True)
                KOFF = 512 if sample else 0
                for nm, dst3, dB in (("oq", q3, QB), ("ok", k3, KB)):
                    toff = KOFF if nm == "ok" else 0
                    for b2 in range(2):
                        sl, sb = wload((nm, i, b2))
                        s3 = w3(sl, 8)
                        if sample:
                            slp, sbp = wload((nm + "p", i, b2))
                            sp3 = w3(slp, 8)
                        for mo in range(4):
                            hc = b2 * 4 + mo
                            for th in range(2):
                                tsl = slice(th * 512, (th + 1) * 512)
                                dsl = slice(toff + th * 512, toff + (th + 1) * 512)
                                b = nextbank()
                                MM(ps(b), [(s3[:, kc, mo * 128:(mo + 1) * 128], hT3[:, kc, tsl]) for kc in range(8)], [sb, HB], [PSB[b]])
                                if sample:
                                    b1 = nextbank()
                                    MM(ps(b1), [(sp3[:, kc, mo * 128:(mo + 1) * 128], hT3[:, kc, tsl]) for kc in range(8)], [sbp, HB], [PSB[b1]])
                                    rope_evac(dst3[:, hc, dsl], b, b1, 128, tsl, [dB], rtmp)
                                elif nm == "oq":
                                    COPY(dst3[:, hc, dsl], ps(b), [PSB[b]], [dB])
                                else:
                                    k_ = sk[0] % 4
                                    sk[0] += 1
                                    ACT(stg[k_], ps(b), AF.Copy, [PSB[b]], [STB[k_]])
                                    VCOPY(dst3[:, hc, dsl], stg[k_], [STB[k_]], [dB])
                                    OUT_DMA(dkod[i, hc * 128:(hc + 1) * 128, tsl], stg[k_], STB[k_])
                for b2 in range(2):
                    sl, sb = wload(("ov", i, b2))
                    s3 = w3(sl, 8)
                    for tc in range(8):
                        b = nextbank()
                        MM(ps(b), [(hT3[:, kc, tc * 128:(tc + 1) * 128], s3[:, kc, :]) for kc in range(8)], [sb, HB], [PSB[b]])
                        if sample:
                            COPY(v3[:, 4 + tc, b2 * 512:(b2 + 1) * 512], ps(b), [PSB[b]], [VB])
                        else:
                            k_ = sk[0] % 4
                            sk[0] += 1
                            ACT(stg[k_], ps(b), AF.Copy, [PSB[b]], [STB[k_]])
                            VCOPY(v3[:, tc, b2 * 512:(b2 + 1) * 512], stg[k_], [STB[k_]], [VB])
                            OUT_DMA(dvod[i, tc * 128:(tc + 1) * 128, b2 * 512:(b2 + 1) * 512], stg[k_], STB[k_])
                nset = 1 if sample else 2
                osbs = [AR.f32(1024) for _ in range(nset)]
                OSBs = [Buf() for _ in range(nset)]
                dsbs = [AR.f32(1024) for _ in range(nset)]
                DSBs = [Buf() for _ in range(nset)]
                def mk_part2(h, qsl, nq, o2v, d2v, OSB, DSB):
                    def part2(bap, bbuf):
                        TTo(o2v[:, 0, 0:nq], o2v[:, 0, 0:nq], d2v[:, 1, 0:nq], ALU.mult, [OSB, DSB], [OSB])
                        TTo(o2v[:, 1, 0:nq], o2v[:, 1, 0:nq], d2v[:, 0, 0:nq], ALU.mult, [OSB, DSB], [OSB])
                        STT(o2v[:, 0, 0:nq], o2v[:, 1, 0:nq], misc[:, i:i + 1], o2v[:, 0, 0:nq], ALU.mult, ALU.add, [OSB, MISCB], [OSB])
                        TTo(d2v[:, 0, 0:nq], d2v[:, 0, 0:nq], d2v[:, 1, 0:nq], ALU.mult, [DSB], [DSB])
                        sq, sqB, rs, rsB = ntmp
                        ACT(sq[0][:, 0:nq], o2v[:, 0, 0:nq], AF.Square, [OSB], [sqB[0]])
                        ACT(sq[1][:, 0:nq], d2v[:, 0, 0:nq], AF.Square, [DSB], [sqB[1]], scale=math.sqrt(EPS))
                        MM(bap, [(ones_b, sq[0][:, 0:nq]), (ones_b, sq[1][:, 0:nq])], [sqB[0], sqB[1], CONSTB], [bbuf])
                        ACT(rs[:, 0:nq], bap, AF.Ln, [bbuf], [rsB], scale=1.0 / 128)
                        ACT(rs[:, 0:nq], rs[:, 0:nq], AF.Exp, [rsB], [rsB], scale=-0.5)
                        STT(mixT3[:, h, qsl], o2v[:, 0, 0:nq], misc[:, 2 + i:3 + i], rs[:, 0:nq], ALU.mult, ALU.mult, [OSB, MISCB, rsB], [MIXB[h]])
                    return part2
                nb = 0
                pend = [None]
                if not sample:
                    acc = [(4, 6), (5, 7)]
                    PT2 = [AR.bf16(2048), AR.bf16(2048)]
                    PT2B = [Buf(), Buf()]
                    items = [(q0, nq, kcs, h) for (_sq, q0, nq, kcs) in attn_blocks() for h in range(8)]
                    ps4 = psall[:, 0:2048].rearrange("p (t c) -> p t c", t=4)

                    def emitS_all(idx):
                        q0, nq, kcs, h = items[idx]
                        fns = []
                        for j in range(2):
                            ks = slice(kcs[j] * 128, (kcs[j] + 1) * 128)
                            for n_ in range(2):
                                t_ = 2 * j + n_
                                fns.append(lambda e, n_=n_, t_=t_, ks=ks: e.matmul(ps(t_)[:, 0:nq], lhsT=k3[n_ * 64:(n_ + 1) * 64, h, ks],
                                                                                  rhs=q3[n_ * 64:(n_ + 1) * 64, h, q0:q0 + nq], start=True, stop=True))
                        S.mm(fns, [KB, QB], [PSB[0], PSB[1], PSB[2], PSB[3]])
                    emitS_all(0)
                    for idx, (q0, nq, kcs, h) in enumerate(items):
                        qsl = slice(q0, q0 + nq)
                        pk = idx % 2
                        ptv = PT2[pk].rearrange("p (t c) -> p t c", t=4)
                        ACT(ptv[:, :, 0:nq], ps4[:, :, 0:nq], AF.Exp, [PSB[0], PSB[1], PSB[2], PSB[3]], [PT2B[pk]], scale=DIFF_SCALE)
                        if idx + 1 < len(items):
                            emitS_all(idx + 1)
                        for j in range(2):
                            for n_ in range(2):
                                t_ = 2 * j + n_
                                bo, bd = acc[n_]
                                MM(ps(bo)[:, 0:nq], [(v3[:, kcs[j], h * 128:(h + 1) * 128], ptv[:, t_, 0:nq])], [VB, PT2B[pk]], [PSB[bo]],
                                   first=(j == 0), last=(j == 1))
                                MM(ps(bd)[:, 0:nq], [(ones_b, ptv[:, t_, 0:nq])], [CONSTB, PT2B[pk]], [PSB[bd]], first=(j == 0), last=(j == 1))
                        if pend[0] is not None:
                            pend[0](ps(4)[:, 256:256 + nq], PSB[4])
                            pend[0] = None
                        osb, OSB, dsb, DSB = osbs[nb % nset], OSBs[nb % nset], dsbs[nb % nset], DSBs[nb % nset]
                        nb += 1
                        o2v = osb.rearrange("p (b n) -> p b n", b=2)
                        d2v = dsb.rearrange("p (b n) -> p b n", b=2)
                        pso = psall[:, 4 * 512:6 * 512].rearrange("p (b n) -> p b n", b=2)
                        psd = psall[:, 6 * 512:8 * 512].rearrange("p (b n) -> p b n", b=2)
                        ACT(d2v[:, :, 0:nq], psd[:, :, 0:nq], AF.Copy, [PSB[6], PSB[7]], [DSB])
                        VCOPY(o2v[:, :, 0:nq], pso[:, :, 0:nq], [PSB[4], PSB[5]], [OSB])
                        pend[0] = mk_part2(h, qsl, nq, o2v, d2v, OSB, DSB)
                for (_sq, q0, nq, kcs) in (attn_blocks() if sample else []):
                    qsl = slice(q0, q0 + nq)
                    for h in range(8):
                        n = len(kcs)
                        acc = [(4, 6), (5, 7)]

                        def emitS(j):
                            ks = slice(kcs[j] * 128, (kcs[j] + 1) * 128)
                            b0_ = 2 * (j % 2)
                            S.mm([lambda e, n_=n_: e.matmul(ps(b0_ + n_)[:, 0:nq], lhsT=k3[n_ * 64:(n_ + 1) * 64, h, ks],
                                                            rhs=q3[n_ * 64:(n_ + 1) * 64, h, qsl], start=True, stop=True) for n_ in range(2)],
                                 [KB, QB], [PSB[b0_], PSB[b0_ + 1]])
                        emitS(0)
                        jpt = min(2, n - 1)
                        for j in range(n):
                            for n_ in range(2):
                                sb_ = 2 * (j % 2) + n_
                                ACT(PT[sb_][:, 0:nq], ps(sb_)[:, 0:nq], AF.Exp, [PSB[sb_]], [PTB[sb_]], scale=DIFF_SCALE)
                            if j + 1 < n:
                                emitS(j + 1)
                            for n_ in range(2):
                                sb_ = 2 * (j % 2) + n_
                                bo, bd = acc[n_]
                                MM(ps(bo)[:, 0:nq], [(v3[:, kcs[j], h * 128:(h + 1) * 128], PT[sb_][:, 0:nq])], [VB, PTB[sb_]], [PSB[bo]],
                                   first=(j == 0), last=(j == n - 1))
                                MM(ps(bd)[:, 0:nq], [(ones_b, PT[sb_][:, 0:nq])], [CONSTB, PTB[sb_]], [PSB[bd]], first=(j == 0), last=(j == n - 1))
                            if pend[0] is not None and j == jpt:
                                pend[0](ps(2 * (j % 2))[:, 0:nq], PSB[2 * (j % 2)])
                                pend[0] = None
                        if os.environ.get("KDBG_NOFIN"):
                            VCOPY(mixT3[:, h, qsl], ps(4)[:, 0:nq], [PSB[4], PSB[5], PSB[6], PSB[7]], [MIXB[h]])
                            continue
                        osb, OSB, dsb, DSB = osbs[nb % nset], OSBs[nb % nset], dsbs[nb % nset], DSBs[nb % nset]
                        nb += 1
                        o2v = osb.rearrange("p (b n) -> p b n", b=2)
                        d2v = dsb.rearrange("p (b n) -> p b n", b=2)
                        pso = psall[:, 4 * 512:6 * 512].rearrange("p (b n) -> p b n", b=2)
                        psd = psall[:, 6 * 512:8 * 512].rearrange("p (b n) -> p b n", b=2)
                        ACT(d2v[:, :, 0:nq], psd[:, :, 0:nq], AF.Copy, [PSB[6], PSB[7]], [DSB])
                        VCOPY(o2v[:, :, 0:nq], pso[:, :, 0:nq], [PSB[4], PSB[5]], [OSB])

                        pend[0] = mk_part2(h, qsl, nq, o2v, d2v, OSB, DSB)
                if pend[0] is not None:
                    pend[0](ps(0)[:, 0:256 if not sample else 512], PSB[0])
                    pend[0] = None
                out_proj("oo", i, l)
                S.barrier()
                AR.top = m0

            def ffn(l):
                m0 = AR.top
                actT = AR.bf16(NJ * 1024)
                act3 = actT.rearrange("p (j t) -> p j t", j=NJ)
                ACB = [Buf() for _ in range(NJ)]
                ca = [AR.f32(1024), AR.f32(1024)]
                cg = [AR.f32(1024), AR.f32(1024)]
                CAB = [Buf(), Buf()]
                CGB = [Buf(), Buf()]

                def v3d(ap):
                    return ap.rearrange("p (s t) -> p s t", s=nseq)
                do_ada = ADA_INTERLEAVE and sample and (l + 1 < DEPTH) and (groups[0] == "s")
                for jb in range(11):
                    sl, sb = wload(("up", l, jb))
                    s3 = w3(sl, 8)
                    if do_ada:
                        ada_block(l + 1, jb)
                    for jj in range(2):
                        j = 2 * jb + jj
                        r = j % 2
                        for ag in range(2):
                            bp = nextpair()
                            for th in range(2):
                                MM(ps(bp + th), [(s3[:, kc, (jj * 2 + ag) * 128:(jj * 2 + ag + 1) * 128], hT3[:, kc, th * 512:(th + 1) * 512]) for kc in range(8)],
                                   [sb, HB], [PSB[bp + th]])
                            pp = psall[:, bp * 512:bp * 512 + 1024]
                            cidx = j + ag * NJ
                            w0 = VEC[:, VOFF["cw"] + (l * 3 + 0) * 44 + cidx: VOFF["cw"] + (l * 3 + 0) * 44 + cidx + 1]
                            w1 = VEC[:, VOFF["cw"] + (l * 3 + 1) * 44 + cidx: VOFF["cw"] + (l * 3 + 1) * 44 + cidx + 1]
                            w2 = VEC[:, VOFF["cw"] + (l * 3 + 2) * 44 + cidx: VOFF["cw"] + (l * 3 + 2) * 44 + cidx + 1]
                            bb = VEC[:, VOFF["cb"] + l * 44 + cidx: VOFF["cb"] + l * 44 + cidx + 1]
                            cdst, cB = (ca[r], CAB[r]) if ag == 0 else (cg[r], CGB[r])
                            pbufs = [PSB[bp], PSB[bp + 1]]
                            ACT(cdst, pp, AF.Identity, pbufs + [VECB], [cB], bias=bb, scale=w1)
                            STT(v3d(cdst)[:, :, 1:], v3d(pp)[:, :, :-1], w0, v3d(cdst)[:, :, 1:], ALU.mult, ALU.add, pbufs + [VECB, cB], [cB])
                            STT(v3d(cdst)[:, :, :-1], v3d(pp)[:, :, 1:], w2, v3d(cdst)[:, :, :-1], ALU.mult, ALU.add, pbufs + [VECB, cB], [cB])
                        ACT(ca[r], ca[r], AF.Silu, [CAB[r]], [CAB[r]])
                        TTo(act3[:, j, :], ca[r], cg[r], ALU.mult, [CAB[r], CGB[r]], [ACB[j]])
                for m in range(8):
                    sl, sb = wload(("dn", l, m))
                    s3 = w3(sl, NJ)
                    if do_ada and m == 0:
                        ada_block(l + 1, 11)
                    for th in range(2):
                        tsl = slice(th * 512, (th + 1) * 512)
                        b = nextbank()
                        MM(ps(b), [(s3[:, j, :], act3[:, j, tsl]) for j in range(NJ)], [sb] + ACB, [PSB[b]])
                        STT(xT3[:, m, tsl], ps(b), modcol(l, 5, m), xT3[:, m, tsl], ALU.mult, ALU.add, [PSB[b], MODL[l], XB[m][th]], [XB[m][th]])
                S.barrier()
                AR.top = m0

            for l in range(nlayers):
                S.marks.append((gname + "%d.norm1" % l, S.npe))
                mk_ab(l)
                norm_mod(l, 0)
                S.marks.append((gname + "%d.mixer" % l, S.npe))
                if l % 2 == 0:
                    mixer_even(l // 2, l)
                else:
                    mixer_odd(l // 2, l)
                S.marks.append((gname + "%d.norm2" % l, S.npe))
                norm_mod(l, 1)
                S.marks.append((gname + "%d.ffn" % l, S.npe))
                ffn(l)
            S.marks.append((gname + ".final", S.npe))
            m0 = AR.top
            tmps = [mk_tmp(), mk_tmp()]
            stg = [AR.f32(512) for _ in range(4)]
            STB = [Buf() for _ in range(4)]
            k_ = 0
            for th in range(2):
                tsl = slice(th * 512, (th + 1) * 512)
                rs = rms_stats(lambda c, th: xT3[:, c, th * 512:(th + 1) * 512], lambda c, th: [XB[c][th]], 8, th, 1.0 / D, tmps[th])
                for kc in range(8):
                    STT(stg[k_ % 4], xT3[:, kc, tsl], VEC[:, VOFF["fgain"] + kc:VOFF["fgain"] + kc + 1], rs, ALU.mult, ALU.mult,
                        [XB[kc][th], VECB, tmps[th][3]], [STB[k_ % 4]])
                    OUT_DMA(yd[gname][kc * 128:(kc + 1) * 128, tsl], stg[k_ % 4], STB[k_ % 4])
                    k_ += 1
            S.barrier()
            AR.top = m0

        for gname in groups:
            run_group(gname)
        S.barrier(final=True)
        S.marks.append(("end", S.npe))
        global _MARKS, _LOG
        _MARKS = S.marks
        _LOG = S.log
        print("program: inst=%d waits=%d arena_peak=%d/%d wtotal=%d" % (S.ninst, S.nwait, AR.peak, ARENA_WORDS, WTOTAL))
    return nc, WL, WTOTAL


_CACHE = {}
_MARKS = []
_LOG = {}


def _get_program(nlayers=DEPTH, groups=("s", "p")):
    k = (nlayers, groups)
    if k not in _CACHE:
        _CACHE[k] = build_program(nlayers, groups)
    return _CACHE[k]


def make_in_maps(inp, WL, WTOTAL):
    inp = {k: np.asarray(v) for k, v in inp.items()}
    W = np.empty((128, WTOTAL), np.float32)
    off = 0
    for key, F, fn in WL:
        W[:, off:off + F] = fn(inp)
        off += F
    consts = build_consts()
    vecs = build_vecs(inp)
    wdec = np.zeros((32, 4, 256), np.float32)
    for i in range(2):
        for d in range(2):
            wdec[0:16, i * 2 + d] = inp["gla_w_decay"][i, d]
            wdec[16, i * 2 + d] = inp["gla_b_decay"][i, d]
    wdec = wdec.reshape(32, 1024)
    lamp = np.ascontiguousarray(inp["diff_lambda"].reshape(1, 512)).astype(np.float32)
    maps = []
    for c in range(NCORES):
        cvec = np.stack([inp["c_ctx"], inp["c"][c]], 0)
        cT = np.ascontiguousarray(cvec.reshape(2, 8, 128).transpose(2, 1, 0).reshape(128, 16))
        xs = np.ascontiguousarray(inp["x_sample"][c].T)
        xp = np.ascontiguousarray(inp["x_prompt"][4 * c:4 * c + 4].reshape(1024, 1024).T)
        gst = np.ascontiguousarray(inp["state_gla"][c].transpose(3, 0, 1, 2, 4).reshape(64, 2048))
        ckvT = np.ascontiguousarray(inp["cache_mla_ckv"][c].transpose(0, 2, 1))
        krcT = np.ascontiguousarray(inp["cache_mla_krope"][c].transpose(0, 2, 1))
        dkT = np.ascontiguousarray(inp["cache_diff_k"][c].reshape(2, 512, 1024).transpose(0, 2, 1))
        dv = np.ascontiguousarray(inp["cache_diff_v"][c].reshape(2, 512, 1024))
        maps.append({"wts": W, "consts": consts, "vecs": vecs, "cT": cT, "xsT": xs, "xpT": xp, "gst": gst, "ckvT": ckvT,
                     "krcT": krcT, "dkT": dkT, "dv": dv, "wdec": wdec, "lamp": lamp})
    return maps


def assemble(results):
    ys = np.stack([r["ysT"].T for r in results], 0)
    yp = np.concatenate([r["ypT"].T.reshape(4, 256, 1024) for r in results], 0)
    gs = np.concatenate([r["gso"].reshape(64, 4, 2, 2, 4, 128).transpose(1, 2, 3, 4, 0, 5) for r in results], 0)
    ckv = np.concatenate([r["ckvo"].reshape(2, 256, 4, 256).transpose(2, 0, 3, 1) for r in results], 0)
    kr = np.concatenate([r["kro"].reshape(2, 64, 4, 256).transpose(2, 0, 3, 1) for r in results], 0)
    dk = np.concatenate([r["dko"].reshape(2, 1024, 4, 256).transpose(2, 0, 3, 1).reshape(4, 2, 256, 8, 128) for r in results], 0)
    dv = np.concatenate([r["dvo"].reshape(2, 4, 256, 8, 128).transpose(1, 0, 2, 3, 4) for r in results], 0)
    f = lambda a: np.ascontiguousarray(a, dtype=np.float32)
    return (f(yp), f(ys), f(gs), f(ckv), f(kr), f(dk), f(dv))


def kernel(**inputs):
    nc, WL, WTOTAL = _get_program()
    maps = make_in_maps(inputs, WL, WTOTAL)
    res = run_bass_kernel_spmd(nc, maps, core_ids=list(range(NCORES)))
    return assemble(res.results)
```

```python
import math
import os
import contextlib
import numpy as np
import concourse.bass as bass
import concourse.mybir as mybir
from concourse.bass_utils import run_bass_kernel_spmd

F32 = mybir.dt.float32
BF16 = mybir.dt.bfloat16
AF = mybir.ActivationFunctionType
ALU = mybir.AluOpType

NCORES = 8
D = 1024
TT = 1024
DFF = 2816
NJ = 22
EPS = 1e-6
DEPTH = 4
MLA_SCALE = 192 ** -0.5
DIFF_SCALE = 64 ** -0.5
NSLOT = 4
SLOTF = 4096
ARENA_WORDS = 53000
ADA_INTERLEAVE = True


def _blk(W, cols):
    K = W.shape[0]
    nk = K // 128
    sub = W[:, cols]
    return np.ascontiguousarray(sub.reshape(nk, 128, len(cols)).transpose(1, 0, 2).reshape(128, nk * len(cols)))


_P64 = np.concatenate([np.arange(16, 32), np.arange(0, 16), np.arange(48, 64), np.arange(32, 48)])


def _perm_cols(cols):
    cols = np.asarray(cols)
    g = cols.reshape(-1, 64)
    return g[:, _P64].reshape(-1)


def weight_layout():
    L = []
    ar = np.arange

    def add(key, F, fn):
        L.append((key, F, fn))

    for l in range(DEPTH):
        for b in range(12):
            add(("ada", l, b), 4096, lambda inp, l=l, b=b: _blk(inp["w_ada"][l], b * 512 + ar(512)))
    for l in range(DEPTH):
        i = l // 2
        if l % 2 == 0:
            kr = 2080 + ar(64)
            c5 = np.concatenate([1536 + ar(32), kr, _perm_cols(kr)])
            add(("e5", i), 8 * 160, lambda inp, i=i, c5=c5: _blk(inp["w_in_even"][i], c5))
            add(("e3", i), 4096, lambda inp, i=i: _blk(inp["w_in_even"][i], 512 + ar(512)))
            add(("e2", i), 4096, lambda inp, i=i: _blk(inp["w_in_even"][i], 1024 + ar(512)))
            add(("e1", i), 4096, lambda inp, i=i: _blk(inp["w_in_even"][i], ar(512)))
            add(("e4", i), 4096, lambda inp, i=i: _blk(inp["w_in_even"][i], 1568 + ar(512)))
            nope = np.concatenate([h * 192 + ar(128) for h in range(4)])
            rope = np.concatenate([h * 192 + 128 + ar(64) for h in range(4)])
            cuq = np.concatenate([nope, rope, _perm_cols(rope)])
            add(("uq", i), 2 * 1024, lambda inp, i=i, cuq=cuq: _blk(inp["mla_w_uq"][i], cuq))
            kn = np.concatenate([h * 256 + ar(128) for h in range(4)])
            vv = np.concatenate([h * 256 + 128 + ar(128) for h in range(4)])
            ckv = np.concatenate([kn, vv])
            add(("ukv", i), 2 * 1024, lambda inp, i=i, ckv=ckv: _blk(inp["mla_w_ukv"][i], ckv))
            for b in range(2):
                add(("oe", i, b), 4096, lambda inp, i=i, b=b: _blk(inp["w_out_even"][i], b * 512 + ar(512)))
        else:
            for nm, base in (("oq", 0), ("ok", 1024), ("ov", 2048)):
                for b in range(2):
                    cols = base + b * 512 + ar(512)
                    add((nm, i, b), 4096, lambda inp, i=i, cols=cols: _blk(inp["w_in_odd"][i], cols))
                    if nm != "ov":
                        pc = _perm_cols(cols)
                        add((nm + "p", i, b), 4096, lambda inp, i=i, pc=pc: _blk(inp["w_in_odd"][i], pc))
            for b in range(2):
                add(("oo", i, b), 4096, lambda inp, i=i, b=b: _blk(inp["w_out_odd"][i], b * 512 + ar(512)))
        for jb in range(11):
            j0, j1 = 2 * jb, 2 * jb + 1
            cols = np.concatenate([j0 * 128 + ar(128), DFF + j0 * 128 + ar(128), j1 * 128 + ar(128), DFF + j1 * 128 + ar(128)])
            add(("up", l, jb), 4096, lambda inp, l=l, cols=cols: _blk(inp["ffn_w_up"][l], cols))
        for m in range(8):
            add(("dn", l, m), NJ * 128, lambda inp, l=l, m=m: _blk(inp["ffn_w_down"][l], m * 128 + ar(128)))
    return L


_VSPEC = [("bada", 4 * 48), ("ngain", 4 * 2 * 8), ("fgain", 8), ("glan", 2), ("qn", 4), ("kvn", 4), ("dn", 2),
          ("cw", 4 * 3 * 44), ("cb", 4 * 44)]
VOFF = {}
_o = 0
for _n, _c in _VSPEC:
    VOFF[_n] = _o
    _o += _c
VN = _o
CN = 2688


def _fm(v):
    return np.asarray(v, np.float32).reshape(-1, 128).T


def build_vecs(inp):
    V = np.zeros((128, VN), np.float32)
    for l in range(4):
        V[:, VOFF["bada"] + l * 48: VOFF["bada"] + (l + 1) * 48] = _fm(inp["b_ada"][l])
        for s in range(2):
            o = VOFF["ngain"] + (l * 2 + s) * 8
            V[:, o:o + 8] = _fm(inp["norm_gain"][l, s])
        for t in range(3):
            o = VOFF["cw"] + (l * 3 + t) * 44
            V[:, o:o + 44] = _fm(inp["ffn_conv_w"][l, t])
        o = VOFF["cb"] + l * 44
        V[:, o:o + 44] = _fm(inp["ffn_conv_b"][l])
    V[:, VOFF["fgain"]:VOFF["fgain"] + 8] = _fm(inp["final_gain"])
    for i in range(2):
        V[:, VOFF["glan"] + i] = inp["gla_norm"][i]
        V[:, VOFF["qn"] + 2 * i: VOFF["qn"] + 2 * i + 2] = _fm(inp["mla_q_norm"][i])
        V[:, VOFF["kvn"] + 2 * i: VOFF["kvn"] + 2 * i + 2] = _fm(inp["mla_kv_norm"][i])
        V[:, VOFF["dn"] + i] = inp["diff_norm"][i]
    return V


def build_consts():
    C = np.zeros((128, CN), np.float32)
    C[:, 0:128] = np.eye(128, dtype=np.float32)
    C[:, 128:256] = 1.0
    s = np.arange(128)[:, None]
    t = np.arange(128)[None, :]
    C[:, 256:384] = (s <= t)
    C[:, 384:512] = (s >= t)
    tt = np.arange(1024)
    row = (tt // 64).astype(np.float32)
    col = (tt % 64).astype(np.float32)
    inv = (np.float32(10000.0) ** (-np.arange(0, 32, 2, dtype=np.float32) / np.float32(32))).astype(np.float32)
    ar_ = (row[:, None] * inv).astype(np.float32).T
    ac_ = (col[:, None] * inv).astype(np.float32).T
    cosr, sinr, cosc, sinc = np.cos(ar_), np.sin(ar_), np.cos(ac_), np.sin(ac_)
    C64 = np.concatenate([cosr, cosr, cosc, cosc], 0)
    S64 = np.concatenate([-sinr, sinr, -sinc, sinc], 0)
    C[:, 512:1536] = np.concatenate([C64, C64], 0)
    C[:, 1536:2560] = np.concatenate([S64, S64], 0)
    pm = np.concatenate([_P64, 64 + _P64])
    P = np.zeros((128, 128), np.float32)
    P[pm, np.arange(128)] = 1.0
    C[:, 2560:2688] = P
    return C


class Buf:
    __slots__ = ("w", "r", "name")

    def __init__(self, name=""):
        self.w = None
        self.r = {}
        self.name = name


class Sch:
    ENG = ("pe", "act", "dve", "pool", "sp")

    def __init__(self, nc, es):
        self.nc = nc
        self.es = es
        self.eng = {"pe": nc.tensor, "act": nc.scalar, "dve": nc.vector, "pool": nc.gpsimd, "sp": nc.sync}
        self.sem = {}
        self.cnt = {}
        for e in self.ENG:
            self.sem[e] = es.enter_context(nc.semaphore("s_" + e))
            self.cnt[e] = 0
        self.seen = {e: {} for e in self.ENG}
        self.snap = {}
        self.nwait = 0
        self.ninst = 0
        self.npe = 0
        self.marks = []
        self.log = {e: [] for e in self.ENG}

    def chan(self, key):
        if key not in self.sem:
            self.sem[key] = self.es.enter_context(self.nc.semaphore("d_" + key))
            self.cnt[key] = 0

    def _deps(self, reads, writes):
        d = {}
        for b in reads:
            if b.w is not None:
                k, v = b.w
                if d.get(k, 0) < v:
                    d[k] = v
        for b in writes:
            if b.w is not None:
                k, v = b.w
                if d.get(k, 0) < v:
                    d[k] = v
            for k, v in b.r.items():
                if d.get(k, 0) < v:
                    d[k] = v
        return d

    def _wait(self, e, d):
        seen = self.seen[e]
        h = self.eng[e]
        for k, v in d.items():
            if k == e and e == "pe":
                continue
            if seen.get(k, 0) < v:
                h.wait_ge(self.sem[k], v)
                seen[k] = v
                self.nwait += 1
                self.log[e].append(("w", k, v))

    def _mark(self, ev, reads, writes):
        k, v = ev
        for b in reads:
            if b.r.get(k, 0) < v:
                b.r[k] = v
        for b in writes:
            b.w = ev
            b.r = {}

    def op(self, e, fn, reads=(), writes=()):
        self._wait(e, self._deps(reads, writes))
        inst = fn(self.eng[e])
        self.cnt[e] += 1
        inst.then_inc(self.sem[e], 1)
        self.ninst += 1
        self.log[e].append(("i", e, 1))
        self._mark((e, self.cnt[e]), reads, writes)

    def mm(self, fns, reads=(), writes=()):
        self._wait("pe", self._deps(reads, writes))
        inst = None
        for f in fns:
            inst = f(self.eng["pe"])
            self.ninst += 1
            self.npe += 1
        self.cnt["pe"] += 1
        inst.then_inc(self.sem["pe"], 1)
        self.log["pe"].append(("i", "pe", 1))
        self._mark(("pe", self.cnt["pe"]), reads, writes)

    def dma(self, q, key, out, in_, reads=(), writes=(), join=False):
        self.chan(key)
        d = self._deps(reads, writes)
        if self.cnt[key] > 0 and d.get(key, 0) < self.cnt[key]:
            d[key] = self.cnt[key]
        if q == "pool" and join:
            for k, v in self.snap.items():
                if d.get(k, 0) < v:
                    d[k] = v
        self._wait(q, d)
        self.cnt[key] += 16
        self.eng[q].dma_start(out=out, in_=in_).then_inc(self.sem[key], 16)
        self.ninst += 1
        self.log[q].append(("i", key, 16))
        self._mark((key, self.cnt[key]), reads, writes)

    def barrier(self, final=False):
        d = {k: v for k, v in self.cnt.items() if v > 0}
        for e in self.ENG:
            if e == "pool" and not final:
                continue
            self._wait(e, d)
        self.snap = dict(d)


class Arena:
    def __init__(self, ap, nwords):
        self.ap = ap
        self.n = nwords
        self.top = 0
        self.peak = 0

    def f32(self, words):
        off = self.top
        self.top += words
        self.peak = max(self.peak, self.top)
        assert self.top <= self.n, "arena overflow %d > %d" % (self.top, self.n)
        return self.ap[:, off:off + words]

    def bf16(self, elems):
        words = (elems + 1) // 2
        return self.f32(words).bitcast(BF16)


def build_program(nlayers=DEPTH, groups=("s", "p")):
    WL = weight_layout()
    WOFF = {}
    off = 0
    for key, F, _ in WL:
        WOFF[key] = (off, F)
        off += F
    WTOTAL = off

    nc = bass.Bass("TRN2", target_bir_lowering=False)

    def din(name, shape):
        return nc.dram_tensor(name, shape, F32, kind="ExternalInput").ap()

    def dout(name, shape):
        return nc.dram_tensor(name, shape, F32, kind="ExternalOutput").ap()

    Wd = din("wts", [128, WTOTAL])
    constd = din("consts", [128, CN])
    vecd = din("vecs", [128, VN])
    cTd = din("cT", [128, 16])
    xd = {"s": din("xsT", [1024, 1024]), "p": din("xpT", [1024, 1024])}
    gstd = din("gst", [64, 2048])
    ckvTd = din("ckvT", [2, 256, 512])
    krcTd = din("krcT", [2, 64, 512])
    dkTd = din("dkT", [2, 1024, 512])
    dvd = din("dv", [2, 512, 1024])
    wdecd = din("wdec", [32, 1024])
    lampd = din("lamp", [1, 512])
    yd = {"s": dout("ysT", [1024, 1024]), "p": dout("ypT", [1024, 1024])}
    gsod = dout("gso", [64, 8192])
    ckvod = dout("ckvo", [2, 256, 1024])
    krod = dout("kro", [2, 64, 1024])
    dkod = dout("dko", [2, 1024, 1024])
    dvod = dout("dvo", [2, 1024, 1024])

    es = contextlib.ExitStack()
    with es:
        arena_t = es.enter_context(nc.sbuf_tensor("arena", [128, ARENA_WORDS], F32))
        psall = es.enter_context(nc.psum_tensor("ps", [128, 4096], F32))
        psbf_all = psall.bitcast(BF16)
        S = Sch(nc, es)
        AR = Arena(arena_t, ARENA_WORDS)

        PSB = [Buf("ps%d" % i) for i in range(8)]
        st = {"bank": 0, "slot": 0, "och": 0, "alt": 0}

        def ps(b, n=512):
            return psall[:, b * 512:b * 512 + n]

        def psbf(b):
            return psbf_all[:, b * 1024:(b + 1) * 1024]

        def nextbank():
            b = st["bank"]
            st["bank"] = (b + 1) % 8
            return b

        def nextpair():
            b = st["bank"]
            if b % 2:
                b = (b + 1) % 8
            st["bank"] = (b + 2) % 8
            return b

        def ACT(out, in_, func, reads, writes, bias=0.0, scale=1.0):
            S.op("act", lambda e: e.activation(out=out, in_=in_, func=func, bias=bias, scale=scale), reads, writes)

        def TTo(out, a, b, op, reads, writes):
            S.op("dve", lambda e: e.tensor_tensor(out=out, in0=a, in1=b, op=op), reads, writes)

        def TS(out, a, s1, op0, reads, writes, s2=None, op1=None):
            if op1 is None:
                S.op("dve", lambda e: e.tensor_scalar(out=out, in0=a, scalar1=s1, scalar2=None, op0=op0), reads, writes)
            else:
                S.op("dve", lambda e: e.tensor_scalar(out=out, in0=a, scalar1=s1, scalar2=s2, op0=op0, op1=op1), reads, writes)

        def STT(out, a, scalar, b, op0, op1, reads, writes):
            S.op("dve", lambda e: e.scalar_tensor_tensor(out=out, in0=a, scalar=scalar, in1=b, op0=op0, op1=op1), reads, writes)

        def VCOPY(out, in_, reads, writes):
            S.op("dve", lambda e: e.tensor_copy(out=out, in_=in_), reads, writes)

        def COPY(out, in_, reads, writes):
            st["alt"] ^= 1
            if st["alt"]:
                ACT(out, in_, AF.Copy, reads, writes)
            else:
                VCOPY(out, in_, reads, writes)

        def RECIP(out, in_, reads, writes):
            S.op("dve", lambda e: e.reciprocal(out=out, in_=in_), reads, writes)

        def MM(out, pairs, reads, writes, first=True, last=True):
            n = len(pairs)
            fns = []
            for idx, (l_, r_) in enumerate(pairs):
                fns.append(lambda e, l_=l_, r_=r_, idx=idx: e.matmul(out, lhsT=l_, rhs=r_, start=(first and idx == 0),
                                                                    stop=(last and idx == n - 1)))
            S.mm(fns, reads, writes)

        def OUT_DMA(dst, src, srcbuf):
            k = "o%d" % st["och"]
            st["och"] = (st["och"] + 1) % 4
            S.dma("sp", k, dst, src, reads=[srcbuf], writes=[])

        CONST = AR.f32(CN)
        CONSTB = Buf("const")
        ident_f = CONST[:, 0:128]
        ones_f = CONST[:, 128:256]
        TRI = [CONST[:, 256:384], CONST[:, 384:512]]
        ropeC = CONST[:, 512:1536]
        ropeS = CONST[:, 1536:2560]
        cbf = AR.bf16(384)
        ident_b = cbf[:, 0:128]
        ones_b = cbf[:, 128:256]
        perm_b = cbf[:, 256:384]
        VEC = AR.f32(VN)
        VECB = Buf("vec")
        MOD = AR.f32(4 * 96)
        MOD4 = MOD.rearrange("p (l c g) -> p l c g", l=4, g=2)
        MODB = Buf("mod")
        cT = AR.f32(16)
        sTb = AR.bf16(16)
        sT3 = sTb.rearrange("p (k g) -> p k g", g=2)
        ABt = AR.f32(64)
        AB4 = ABt.rearrange("p (l s k) -> p l s k", l=4, s=2)
        ABB = Buf("ab")
        misc = AR.f32(16)
        MISCB = Buf("misc")
        lamt = AR.f32(512 + 16)
        wdecb = AR.bf16(1024)
        wdec3 = wdecb.rearrange("p (a c) -> p a c", a=4)
        WDECB = Buf("wdec")
        xT = AR.f32(8192)
        xT3 = xT.rearrange("p (k t) -> p k t", k=8)
        XB = [[Buf("x%d%d" % (k, t)) for t in range(2)] for k in range(8)]
        XALL = [XB[k][t] for k in range(8) for t in range(2)]
        hT = AR.bf16(8192)
        hT3 = hT.rearrange("p (k t) -> p k t", k=8)
        HB = Buf("h")
        mixT = AR.bf16(8192)
        mixT3 = mixT.rearrange("p (k t) -> p k t", k=8)
        MIXB = [Buf("mix%d" % k) for k in range(8)]
        slots = [AR.bf16(SLOTF) for _ in range(NSLOT)]
        SLB = [Buf("slot%d" % k) for k in range(NSLOT)]

        def wload(key):
            off, F = WOFF[key]
            k = st["slot"]
            st["slot"] = (k + 1) % NSLOT
            S.dma("pool", "w%d" % k, slots[k][:, 0:F], Wd[:, off:off + F], reads=[], writes=[SLB[k]])
            return slots[k][:, 0:F], SLB[k]

        def w3(ap, nk):
            return ap.rearrange("p (k c) -> p k c", k=nk)

        S.dma("sp", "i0", CONST, constd, writes=[CONSTB])
        S.dma("sp", "i1", VEC, vecd, writes=[VECB])
        S.dma("sp", "i2", cT, cTd, writes=[MODB])
        S.dma("sp", "i3", lamt[0:1, 0:512], lampd, writes=[MISCB])
        S.dma("pool", "i4", wdecb[0:32, :], wdecd, writes=[WDECB])
        VCOPY(cbf[:, 0:256], CONST[:, 0:256], [CONSTB], [CONSTB])
        VCOPY(perm_b, CONST[:, 2560:2688], [CONSTB], [CONSTB])
        ACT(sTb, cT, AF.Silu, [MODB], [MODB])
        lam_init = [0.8 - 0.6 * math.exp(-0.3 * (2 * i + 1)) for i in range(2)]
        lsc = lamt[0:1, 512:528]
        for i in range(2):
            base = i * 256
            TTo(lamt[0:1, base:base + 64], lamt[0:1, base:base + 64], lamt[0:1, base + 64:base + 128], ALU.mult, [MISCB], [MISCB])
            TTo(lamt[0:1, base + 128:base + 192], lamt[0:1, base + 128:base + 192], lamt[0:1, base + 192:base + 256], ALU.mult, [MISCB], [MISCB])
            S.op("dve", lambda e: e.reduce_sum(out=lsc[:, 4 * i:4 * i + 1], in_=lamt[0:1, base:base + 64], axis=mybir.AxisListType.X), [MISCB], [MISCB])
            S.op("dve", lambda e: e.reduce_sum(out=lsc[:, 4 * i + 1:4 * i + 2], in_=lamt[0:1, base + 128:base + 192], axis=mybir.AxisListType.X), [MISCB], [MISCB])
            ACT(lsc[:, 4 * i:4 * i + 2], lsc[:, 4 * i:4 * i + 2], AF.Exp, [MISCB], [MISCB])
            TTo(lsc[:, 4 * i + 2:4 * i + 3], lsc[:, 4 * i + 1:4 * i + 2], lsc[:, 4 * i:4 * i + 1], ALU.subtract, [MISCB], [MISCB])
            TS(lsc[:, 4 * i + 2:4 * i + 3], lsc[:, 4 * i + 2:4 * i + 3], -lam_init[i], ALU.add, [MISCB], [MISCB])
            b = nextbank()
            MM(ps(b)[:, 0:1], [(ones_f[0:1, :], lsc[:, 4 * i + 2:4 * i + 3])], [MISCB, CONSTB], [PSB[b]])
            VCOPY(misc[:, i:i + 1], ps(b)[:, 0:1], [PSB[b]], [MISCB])
            TS(misc[:, 2 + i:3 + i], VEC[:, VOFF["dn"] + i:VOFF["dn"] + i + 1], 1.0 - lam_init[i], ALU.mult, [VECB, MISCB], [MISCB])

        MODL = [Buf("mod%d" % l) for l in range(4)]

        def ada_block(l, blk):
            sl, sb = wload(("ada", l, blk))
            s3 = w3(sl, 8)
            b = nextbank()
            for oc in range(4):
                MM(ps(b)[:, oc * 2:oc * 2 + 2], [(s3[:, kc, oc * 128:(oc + 1) * 128], sT3[:, kc, :]) for kc in range(8)],
                   [sb, MODB], [PSB[b]])
            pv = ps(b)[:, 0:8].rearrange("p (c g) -> p c g", g=2)
            c0 = blk * 4
            for g in range(2):
                TTo(MOD4[:, l, c0:c0 + 4, g], pv[:, :, g], VEC[:, VOFF["bada"] + l * 48 + c0: VOFF["bada"] + l * 48 + c0 + 4], ALU.add,
                    [PSB[b], VECB], [MODL[l]])
        for l_ in range(1 if ADA_INTERLEAVE else 4):
            for blk in range(12):
                ada_block(l_, blk)

        def run_group(gname):
            sample = gname == "s"
            g = 1 if sample else 0
            nseq, L = (1, 1024) if sample else (4, 256)
            cps = L // 128
            for kc in range(8):
                S.dma("sp", "x%d" % (kc % 2), xT3[:, kc, :], xd[gname][kc * 128:(kc + 1) * 128, :], writes=[XB[kc][0], XB[kc][1]])
            for l in range(4):
                for s in range(2):
                    pass

            def mk_ab(l):
                for s in range(2):
                    STT(AB4[:, l, s, :], MOD4[:, l, (1 + 3 * s) * 8:(2 + 3 * s) * 8, g], 1.0,
                        VEC[:, VOFF["ngain"] + (l * 2 + s) * 8: VOFF["ngain"] + (l * 2 + s + 1) * 8], ALU.add, ALU.mult,
                        [MODL[l], VECB], [ABB])

            def modcol(l, part, kc):
                return MOD4[:, l, part * 8 + kc, g:g + 1]

            def rms_stats(src_fn, srcbufs_fn, nchunks, th, inv_n, tmp):
                sq, sqB, rs, rsB = tmp
                b = nextbank()
                for c in range(nchunks):
                    ACT(sq[c % 2], src_fn(c, th), AF.Square, srcbufs_fn(c, th), [sqB[c % 2]])
                    MM(ps(b), [(ones_b, sq[c % 2])], [sqB[c % 2], CONSTB], [PSB[b]], first=(c == 0), last=(c == nchunks - 1))
                ACT(rs, ps(b), AF.Ln, [PSB[b]], [rsB], bias=EPS, scale=inv_n)
                ACT(rs, rs, AF.Exp, [rsB], [rsB], scale=-0.5)
                return rs

            def mk_tmp():
                sq = [AR.bf16(512), AR.bf16(512)]
                return (sq, [Buf(), Buf()], AR.f32(512), Buf())

            def norm_mod(l, s):
                m = AR.top
                tmps = [mk_tmp(), mk_tmp()]
                t2 = [AR.f32(512), AR.f32(512)]
                t2B = [Buf(), Buf()]
                for th in range(2):
                    tsl = slice(th * 512, (th + 1) * 512)
                    rs = rms_stats(lambda c, th: xT3[:, c, th * 512:(th + 1) * 512], lambda c, th: [XB[c][th]], 8, th, 1.0 / D, tmps[th])
                    rsB = tmps[th][3]
                    for kc in range(8):
                        TTo(t2[kc % 2], xT3[:, kc, tsl], rs, ALU.mult, [XB[kc][th], rsB], [t2B[kc % 2]])
                        ACT(hT3[:, kc, tsl], t2[kc % 2], AF.Identity, [t2B[kc % 2], ABB, MODL[l]], [HB],
                            bias=modcol(l, 3 * s, kc), scale=AB4[:, l, s, kc:kc + 1])
                AR.top = m

            def out_proj(key, i, l):
                for b2 in range(2):
                    sl, sb = wload((key, i, b2))
                    s3 = w3(sl, 8)
                    for mo in range(4):
                        mc = b2 * 4 + mo
                        for th in range(2):
                            tsl = slice(th * 512, (th + 1) * 512)
                            b = nextbank()
                            MM(ps(b), [(s3[:, kc, mo * 128:(mo + 1) * 128], mixT3[:, kc, tsl]) for kc in range(8)],
                               [sb] + MIXB, [PSB[b]])
                            STT(xT3[:, mc, tsl], ps(b), modcol(l, 2, mc), xT3[:, mc, tsl], ALU.mult, ALU.add,
                                [PSB[b], MODL[l], XB[mc][th]], [XB[mc][th]])

            def rope_evac(dst, b0, b1, rows, tsl, dstbufs, tmp):
                (t1, t1B, t2_, t2B_) = tmp
                TTo(t1[0:rows, :], ps(b0)[0:rows, :], ropeC[0:rows, tsl], ALU.mult, [PSB[b0], CONSTB], [t1B])
                TTo(t2_[0:rows, :], ps(b1)[0:rows, :], ropeS[0:rows, tsl], ALU.mult, [PSB[b1], CONSTB], [t2B_])
                TTo(dst, t1[0:rows, :], t2_[0:rows, :], ALU.add, [t1B, t2B_], dstbufs)

            def attn_blocks():
                if sample:
                    return [(0, qb * 512, 512, list(range(12))) for qb in range(2)]
                return [(s_, s_ * 256, 256, [2 * s_, 2 * s_ + 1]) for s_ in range(4)]

            def mixer_even(i, l):
                m0 = AR.top
                sl5, sb5 = wload(("e5", i))
                s53 = w3(sl5, 8)
                lrT = [AR.bf16(1024), AR.bf16(1024)]
                lrB = [Buf(), Buf()]
                for d in range(2):
                    S.op("dve", lambda e: e.memset(lrT[d][0:32, :], 1.0), [], [lrB[d]])
                    for th in range(2):
                        b = nextbank()
                        MM(ps(b)[0:16, :], [(s53[:, kc, d * 16:(d + 1) * 16], hT3[:, kc, th * 512:(th + 1) * 512]) for kc in range(8)],
                           [sb5, HB], [PSB[b]])
                        ACT(lrT[d][0:16, th * 512:(th + 1) * 512], ps(b)[0:16, :], AF.Copy, [PSB[b]], [lrB[d]])
                spT = AR.f32(4096)
                spT3 = spT.rearrange("p (c f) -> p c f", c=8)
                SPB = Buf()
                etmp = [AR.f32(512), AR.f32(512)]
                etB = [Buf(), Buf()]
                for tc in range(8):
                    b = nextbank()
                    for d in range(2):
                        MM(ps(b)[:, d * 256:(d + 1) * 256], [(lrT[d][0:17, tc * 128:(tc + 1) * 128], wdec3[0:17, i * 2 + d, :])],
                           [lrB[d], WDECB], [PSB[b]])
                    ACT(etmp[tc % 2], ps(b), AF.Exp, [PSB[b]], [etB[tc % 2]], scale=-1.0)
                    ACT(spT3[:, tc, :], etmp[tc % 2], AF.Ln, [etB[tc % 2]], [SPB], bias=1.0)
                sl3, sb3 = wload(("e3", i))
                s33 = w3(sl3, 8)
                vTok = AR.bf16(4096)
                vT3 = vTok.rearrange("p (c f) -> p c f", c=8)
                VTB = Buf()
                for tc in range(8):
                    b = nextbank()
                    MM(ps(b), [(hT3[:, kc, tc * 128:(tc + 1) * 128], s33[:, kc, :]) for kc in range(8)], [sb3, HB], [PSB[b]])
                    COPY(vT3[:, tc, :], ps(b), [PSB[b]], [VTB])
                slg, sbg = wload(("e2", i))
                sg3 = w3(slg, 8)
                slqk, sbqk = wload(("e1", i))
                sqk3 = w3(slqk, 8)
                Sst = AR.f32(1024)
                Sst3 = Sst.rearrange("p (a v) -> p a v", a=8)
                Sbf = AR.bf16(1024)
                Sbf3 = Sbf.rearrange("p (a v) -> p a v", a=8)
                SSB = [Buf() for _ in range(8)]
                SBB = [Buf() for _ in range(8)]
                if sample:
                    S.dma("sp", "i5", Sst[0:64, :], gstd[:, i * 1024:(i + 1) * 1024], writes=SSB)
                    VCOPY(Sbf[0:64, :], Sst[0:64, :], SSB, SBB)
                else:
                    stage = AR.f32(1024)
                    stage3 = stage.rearrange("p (a v) -> p a v", a=8)
                    STGB = Buf()
                eb = [AR.f32(1024), AR.f32(1024)]
                enb = [AR.f32(1024), AR.f32(1024)]
                EBB = [Buf(), Buf()]
                ENB = [Buf(), Buf()]
                Qt = [AR.bf16(1024), AR.bf16(1024)]
                Kt = [AR.bf16(1024), AR.bf16(1024)]
                QKB = [Buf(), Buf()]
                Ktok = [AR.bf16(512), AR.bf16(512)]
                Ktok3 = [k_.rearrange("p (c f) -> p c f", c=8) for k_ in Ktok]
                KTB = [Buf(), Buf()]
                Am = [AR.bf16(128) for _ in range(4)]
                AMB = [Buf() for _ in range(4)]
                oT = AR.f32(1024)
                OTB = Buf()
                ntmp = mk_tmp()
                gsb = AR.f32(512)
                GSB = Buf()
                t3 = AR.f32(512)
                T3B = Buf()
                amk = 0
                for h in range(4):
                    for d in range(2):
                        bp = nextpair()
                        for tc in range(8):
                            MM(psall[0:64, bp * 512 + tc * 128: bp * 512 + (tc + 1) * 128],
                               [(spT3[:, tc, d * 256 + h * 64: d * 256 + (h + 1) * 64], TRI[d])], [SPB, CONSTB], [PSB[bp], PSB[bp + 1]])
                        ACT(eb[d][0:64, :], psall[0:64, bp * 512:bp * 512 + 1024], AF.Exp, [PSB[bp], PSB[bp + 1]], [EBB[d]], scale=-1.0 / 16)
                        ACT(enb[d][0:64, :], psall[0:64, bp * 512:bp * 512 + 1024], AF.Exp, [PSB[bp], PSB[bp + 1]], [ENB[d]], scale=1.0 / 16)
                    for which in range(2):
                        coloff = which * 256 + h * 64
                        for th in range(2):
                            tsl = slice(th * 512, (th + 1) * 512)
                            b = nextbank()
                            MM(ps(b)[0:64, :], [(sqk3[:, kc, coloff:coloff + 64], hT3[:, kc, tsl]) for kc in range(8)], [sbqk, HB], [PSB[b]])
                            for d in range(2):
                                if which == 0:
                                    STT(Qt[d][0:64, tsl], ps(b)[0:64, :], 0.125, eb[d][0:64, tsl], ALU.mult, ALU.mult,
                                        [PSB[b], EBB[d]], [QKB[d]])
                                else:
                                    TTo(Kt[d][0:64, tsl], ps(b)[0:64, :], enb[d][0:64, tsl], ALU.mult, [PSB[b], ENB[d]], [QKB[d]])
                    for d in range(2):
                        b = nextbank()
                        S.mm([lambda e, tc=tc: e.transpose(psbf(b)[:, tc * 64:(tc + 1) * 64], Kt[d][0:64, tc * 128:(tc + 1) * 128], ident_b[0:64, 0:64])
                              for tc in range(8)], [QKB[d], CONSTB], [PSB[b]])
                        COPY(Ktok[d], psbf(b)[:, 0:512], [PSB[b]], [KTB[d]])
                    op_ = [nextpair(), nextpair()]
                    accb = (op_[0], op_[0] + 1, op_[1], op_[1] + 1)

                    def fb():
                        b_ = nextbank()
                        while b_ in accb:
                            b_ = nextbank()
                        return b_

                    def emitA(step):
                        for d in range(2):
                            tc = step if d == 0 else 7 - step
                            tcs = slice(tc * 128, (tc + 1) * 128)
                            b = fb()
                            MM(ps(b)[:, 0:128], [(Kt[d][0:64, tcs], Qt[d][0:64, tcs])], [QKB[d]], [PSB[b]])
                            k_ = (step % 2) * 2 + d
                            TTo(Am[k_], ps(b)[:, 0:128], TRI[d], ALU.mult, [PSB[b], CONSTB], [AMB[k_]])
                    emitA(0)
                    for step in range(8):
                        if step + 1 < 8:
                            emitA(step + 1)
                        flags = []
                        for d in range(2):
                            tc = step if d == 0 else 7 - step
                            pos = tc % cps
                            first_in_seq = (pos == 0) if d == 0 else (pos == cps - 1)
                            last_in_seq = (pos == cps - 1) if d == 0 else (pos == 0)
                            need_state = (not last_in_seq) or (not sample)
                            b2 = None
                            if need_state:
                                b2 = fb()
                                MM(ps(b2)[0:64, 0:128], [(Ktok3[d][:, tc, :], vT3[:, tc, h * 128:(h + 1) * 128])], [KTB[d], VTB], [PSB[b2]])
                            flags.append((tc, first_in_seq, last_in_seq, need_state, b2))
                        for d in range(2):
                            tc, first_in_seq, last_in_seq, need_state, b2 = flags[d]
                            tcs = slice(tc * 128, (tc + 1) * 128)
                            sidx = d * 4 + h
                            k_ = (step % 2) * 2 + d
                            pairs = [(vT3[:, tc, h * 128:(h + 1) * 128], Am[k_])]
                            rd = [VTB, AMB[k_]]
                            if sample or not first_in_seq:
                                pairs.append((Sbf3[0:64, sidx, :], Qt[d][0:64, tcs]))
                                rd += [SBB[sidx], QKB[d]]
                            ob = op_[d]
                            MM(psall[:, ob * 512 + tc * 128: ob * 512 + (tc + 1) * 128], pairs, rd, [PSB[ob], PSB[ob + 1]])
                        for d in range(2):
                            tc, first_in_seq, last_in_seq, need_state, b2 = flags[d]
                            sidx = d * 4 + h
                            if need_state:
                                col = tc * 128 + 127 if d == 0 else tc * 128
                                ebc = eb[d][0:64, col:col + 1]
                                final_out = last_in_seq and not sample
                                if final_out:
                                    dst = stage3[0:64, (tc // cps) * 2 + d, :]
                                    dB = [STGB]
                                else:
                                    dst = Sst3[0:64, sidx, :]
                                    dB = [SSB[sidx]]
                                if first_in_seq and not sample:
                                    TS(dst, ps(b2)[0:64, 0:128], ebc, ALU.mult, [PSB[b2], EBB[d]], dB)
                                else:
                                    TS(Sst3[0:64, sidx, :], Sst3[0:64, sidx, :], ebc, ALU.mult, [SSB[sidx], EBB[d]], [SSB[sidx]])
                                    STT(dst, ps(b2)[0:64, 0:128], ebc, Sst3[0:64, sidx, :], ALU.mult, ALU.add,
                                        [PSB[b2], EBB[d], SSB[sidx]], dB)
                                if not last_in_seq:
                                    ACT(Sbf3[0:64, sidx, :], Sst3[0:64, sidx, :], AF.Copy, [SSB[sidx]], [SBB[sidx]])
                    if not sample:
                        for sq_ in range(4):
                            for d in range(2):
                                o_ = (((sq_ * 2 + i) * 2 + d) * 4 + h) * 128
                                OUT_DMA(gsod[:, o_:o_ + 128], stage3[0:64, sq_ * 2 + d, :], STGB)
                    ACT(oT, psall[:, op_[0] * 512: op_[0] * 512 + 1024], AF.Copy, [PSB[op_[0]], PSB[op_[0] + 1]], [OTB])
                    TTo(oT, psall[:, op_[1] * 512: op_[1] * 512 + 1024], oT, ALU.add, [PSB[op_[1]], PSB[op_[1] + 1], OTB], [OTB])
                    for th in range(2):
                        tsl = slice(th * 512, (th + 1) * 512)
                        rs = rms_stats(lambda c, th: oT[:, th * 512:(th + 1) * 512], lambda c, th: [OTB], 1, th, 1.0 / 128, ntmp)
                        b = nextbank()
                        MM(ps(b), [(sg3[:, kc, h * 128:(h + 1) * 128], hT3[:, kc, tsl]) for kc in range(8)], [sbg, HB], [PSB[b]])
                        ACT(gsb, ps(b), AF.Silu, [PSB[b]], [GSB])
                        TTo(t3, oT[:, tsl], rs, ALU.mult, [OTB, ntmp[3]], [T3B])
                        STT(mixT3[:, h, tsl], t3, VEC[:, VOFF["glan"] + i:VOFF["glan"] + i + 1], gsb, ALU.mult, ALU.mult,
                            [T3B, VECB, GSB], [MIXB[h]])
                S.barrier()
                AR.top = m0
                S.marks.append((gname + "%d.mla" % l, S.npe))
                SK = 1536 if sample else 1024
                KOFF = 512 if sample else 0
                sl5, sb5 = wload(("e5", i))
                s53 = w3(sl5, 8)
                sl4, sb4 = wload(("e4", i))
                s43 = w3(sl4, 8)
                cbuf = AR.f32(2048)
                cb3 = cbuf.rearrange("p (c t) -> p c t", c=2)
                CBB = Buf()
                cqn = AR.bf16(2048)
                cqn3 = cqn.rearrange("p (c t) -> p c t", c=2)
                CQB = Buf()
                ckn = AR.bf16(2 * SK)
                ckn3 = ckn.rearrange("p (c t) -> p c t", c=2)
                CKB = Buf()
                krT = AR.bf16(SK)
                KRB = Buf()
                ntmp1 = mk_tmp()
                ntmps = [ntmp1, ntmp1]
                stg = [AR.f32(512) for _ in range(2 if sample else 4)]
                STB = [Buf() for _ in range(4)]
                sk = [0]

                def nstg():
                    k_ = sk[0] % 4
                    sk[0] += 1
                    return stg[k_], STB[k_]
                rtmp = (stg[0], STB[0], stg[1], STB[1])
                if sample:
                    for c in range(2):
                        S.dma("pool", "i6", ckn3[:, c, 0:512], ckvTd[i, c * 128:(c + 1) * 128, :], writes=[CKB], join=True)
                    S.dma("pool", "i6", krT[0:64, 0:512], krcTd[i], writes=[KRB], join=True)
                for kind in range(2):
                    coloff = kind * 256
                    gname_ = "qn" if kind == 0 else "kvn"
                    for th in range(2):
                        tsl = slice(th * 512, (th + 1) * 512)
                        for c in range(2):
                            b = nextbank()
                            MM(ps(b), [(s43[:, kc, coloff + c * 128: coloff + (c + 1) * 128], hT3[:, kc, tsl]) for kc in range(8)],
                               [sb4, HB], [PSB[b]])
                            COPY(cb3[:, c, tsl], ps(b), [PSB[b]], [CBB])
                    for th in range(2):
                        tsl = slice(th * 512, (th + 1) * 512)
                        rs = rms_stats(lambda c, th: cb3[:, c, th * 512:(th + 1) * 512], lambda c, th: [CBB], 2, th, 1.0 / 256, ntmps[th])
                        rsB = ntmps[th][3]
                        for c in range(2):
                            gcol = VEC[:, VOFF[gname_] + 2 * i + c: VOFF[gname_] + 2 * i + c + 1]
                            if kind == 0:
                                STT(cqn3[:, c, tsl], cb3[:, c, tsl], gcol, rs, ALU.mult, ALU.mult, [CBB, VECB, rsB], [CQB])
                            elif sample:
                                STT(ckn3[:, c, 512 + th * 512: 512 + (th + 1) * 512], cb3[:, c, tsl], gcol, rs, ALU.mult, ALU.mult,
                                    [CBB, VECB, rsB], [CKB])
                            else:
                                sg_, sgB = nstg()
                                STT(sg_, cb3[:, c, tsl], gcol, rs, ALU.mult, ALU.mult, [CBB, VECB, rsB], [sgB])
                                ACT(ckn3[:, c, tsl], sg_, AF.Copy, [sgB], [CKB])
                                OUT_DMA(ckvod[i, c * 128:(c + 1) * 128, tsl], sg_, sgB)
                for th in range(2):
                    tsl = slice(th * 512, (th + 1) * 512)
                    b = nextbank()
                    MM(ps(b)[0:64, :], [(s53[:, kc, 32:96], hT3[:, kc, tsl]) for kc in range(8)], [sb5, HB], [PSB[b]])
                    if sample:
                        b1 = nextbank()
                        MM(ps(b1)[0:64, :], [(s53[:, kc, 96:160], hT3[:, kc, tsl]) for kc in range(8)], [sb5, HB], [PSB[b1]])
                        rope_evac(krT[0:64, 512 + th * 512: 512 + (th + 1) * 512], b, b1, 64, tsl, [KRB], rtmp)
                    else:
                        sg_, sgB = nstg()
                        ACT(sg_[0:64, :], ps(b)[0:64, :], AF.Copy, [PSB[b]], [sgB])
                        VCOPY(krT[0:64, tsl], sg_[0:64, :], [sgB], [KRB])
                        OUT_DMA(krod[i, :, tsl], sg_[0:64, :], sgB)
                sluq, sbuq = wload(("uq", i))
                suq3 = w3(sluq, 2)
                qn = AR.bf16(4096)
                qn3 = qn.rearrange("p (h t) -> p h t", h=4)
                QNB = Buf()
                qr = AR.bf16(4096)
                qr3 = qr.rearrange("p (h t) -> p h t", h=4)
                QRB = Buf()
                for h in range(4):
                    for th in range(2):
                        tsl = slice(th * 512, (th + 1) * 512)
                        b = nextbank()
                        MM(ps(b), [(suq3[:, kc, h * 128:(h + 1) * 128], cqn3[:, kc, tsl]) for kc in range(2)], [sbuq, CQB], [PSB[b]])
                        COPY(qn3[:, h, tsl], ps(b), [PSB[b]], [QNB])
                        b = nextbank()
                        MM(ps(b)[0:64, :], [(suq3[:, kc, 512 + h * 64: 512 + (h + 1) * 64], cqn3[:, kc, tsl]) for kc in range(2)],
                           [sbuq, CQB], [PSB[b]])
                        if sample:
                            b1 = nextbank()
                            MM(ps(b1)[0:64, :], [(suq3[:, kc, 768 + h * 64: 768 + (h + 1) * 64], cqn3[:, kc, tsl]) for kc in range(2)],
                               [sbuq, CQB], [PSB[b1]])
                            rope_evac(qr3[0:64, h, tsl], b, b1, 64, tsl, [QRB], rtmp)
                        else:
                            COPY(qr3[0:64, h, tsl], ps(b)[0:64, :], [PSB[b]], [QRB])
                slkv, sbkv = wload(("ukv", i))
                skv3 = w3(slkv, 2)
                kn = AR.bf16(4 * SK)
                kn3 = kn.rearrange("p (h t) -> p h t", h=4)
                KNB = Buf()
                nkc = SK // 128
                vml = AR.bf16(nkc * 512)
                vml3 = vml.rearrange("p (c f) -> p c f", c=nkc)
                VMB = Buf()
                for h in range(4):
                    for kt in range(SK // 512):
                        ksl = slice(kt * 512, (kt + 1) * 512)
                        b = nextbank()
                        MM(ps(b), [(skv3[:, kc, h * 128:(h + 1) * 128], ckn3[:, kc, ksl]) for kc in range(2)], [sbkv, CKB], [PSB[b]])
                        COPY(kn3[:, h, ksl], ps(b), [PSB[b]], [KNB])
                for c2 in range(nkc):
                    b = nextbank()
                    MM(ps(b), [(ckn3[:, kc, c2 * 128:(c2 + 1) * 128], skv3[:, kc, 512:1024]) for kc in range(2)], [sbkv, CKB], [PSB[b]])
                    COPY(vml3[:, c2, :], ps(b), [PSB[b]], [VMB])
                PT = [AR.bf16(512) for _ in range(4)]
                PTB = [Buf() for _ in range(4)]
                rden = [AR.f32(512), AR.f32(512)]
                RDB = [Buf(), Buf()]
                it = 0
                for (_sq, q0, nq, kcs) in attn_blocks():
                    qsl = slice(q0, q0 + nq)
                    for h in range(4):
                        bo, bd = (4, 5) if it % 2 == 0 else (6, 7)
                        n = len(kcs)

                        def emitS(j):
                            sb_ = j % 4
                            ks = slice(kcs[j] * 128, (kcs[j] + 1) * 128)
                            MM(ps(sb_)[:, 0:nq], [(kn3[:, h, ks], qn3[:, h, qsl]), (krT[0:64, ks], qr3[0:64, h, qsl])],
                               [KNB, QNB, KRB, QRB], [PSB[sb_]])
                        for j in range(min(2, n)):
                            emitS(j)
                        for j in range(n):
                            sb_ = j % 4
                            ACT(PT[sb_][:, 0:nq], ps(sb_)[:, 0:nq], AF.Exp, [PSB[sb_]], [PTB[sb_]], scale=MLA_SCALE)
                            MM(ps(bo)[:, 0:nq], [(vml3[:, kcs[j], h * 128:(h + 1) * 128], PT[sb_][:, 0:nq])], [VMB, PTB[sb_]], [PSB[bo]],
                               first=(j == 0), last=(j == n - 1))
                            MM(ps(bd)[:, 0:nq], [(ones_b, PT[sb_][:, 0:nq])], [CONSTB, PTB[sb_]], [PSB[bd]], first=(j == 0), last=(j == n - 1))
                            if j + 2 < n:
                                emitS(j + 2)
                        r_ = it % 2
                        ACT(rden[r_][:, 0:nq], ps(bd)[:, 0:nq], AF.Ln, [PSB[bd]], [RDB[r_]])
                        ACT(rden[r_][:, 0:nq], rden[r_][:, 0:nq], AF.Exp, [RDB[r_]], [RDB[r_]], scale=-1.0)
                        TTo(mixT3[:, 4 + h, qsl], ps(bo)[:, 0:nq], rden[r_][:, 0:nq], ALU.mult, [PSB[bo], RDB[r_]], [MIXB[4 + h]])
                        it += 1
                out_proj("oe", i, l)
                S.barrier()
                AR.top = m0

            def mixer_odd(i, l):
                m0 = AR.top
                SK = 1536 if sample else 1024
                nkc = SK // 128
                qT_ = AR.bf16(8192)
                q3 = qT_.rearrange("p (h t) -> p h t", h=8)
                QB = Buf()
                kT_ = AR.bf16(8 * SK)
                k3 = kT_.rearrange("p (h t) -> p h t", h=8)
                KB = Buf()
                vTok = AR.bf16(nkc * 1024)
                v3 = vTok.rearrange("p (c f) -> p c f", c=nkc)
                VB = Buf()
                PT = [AR.bf16(512) for _ in range(4 if sample else 0)]
                PTB = [Buf() for _ in range(4)]
                tA = [AR.f32(512) for _ in range(2)]
                tAB = [Buf() for _ in range(2)]
                rtmp = (tA[0], tAB[0], tA[1], tAB[1])
                ntmp = mk_tmp()
                if not sample:
                    stg = [AR.f32(512) for _ in range(4)]
                    STB = [Buf() for _ in range(4)]
                sk = [0]
                rawk = [0]
                if sample:
                    for c in range(8):
                        S.dma("pool", "i6", k3[:, c, 0:512], dkTd[i, c * 128:(c + 1) * 128, :], writes=[KB], join=True)
                    for c in range(4):
                        S.dma("pool", "i7", v3[:, c, :], dvd[i, c * 128:(c + 1) * 128, :], writes=[VB], join=True)
                KOFF = 512 if sample else 0
                for nm, dst3, dB in (("oq", q3, QB), ("ok", k3, KB)):
                    toff = KOFF if nm == "ok" else 0
                    for b2 in range(2):
                        sl, sb = wload((nm, i, b2))
                        s3 = w3(sl, 8)
                        if sample:
                            slp, sbp = wload((nm + "p", i, b2))
                            sp3 = w3(slp, 8)
                        for mo in range(4):
                            hc = b2 * 4 + mo
                            for th in range(2):
                                tsl = slice(th * 512, (th + 1) * 512)
                                dsl = slice(toff + th * 512, toff + (th + 1) * 512)
                                b = nextbank()
                                MM(ps(b), [(s3[:, kc, mo * 128:(mo + 1) * 128], hT3[:, kc, tsl]) for kc in range(8)], [sb, HB], [PSB[b]])
                                if sample:
                                    b1 = nextbank()
                                    MM(ps(b1), [(sp3[:, kc, mo * 128:(mo + 1) * 128], hT3[:, kc, tsl]) for kc in range(8)], [sbp, HB], [PSB[b1]])
                                    rope_evac(dst3[:, hc, dsl], b, b1, 128, tsl, [dB], rtmp)
                                elif nm == "oq":
                                    COPY(dst3[:, hc, dsl], ps(b), [PSB[b]], [dB])
                                else:
                                    k_ = sk[0] % 4
                                    sk[0] += 1
                                    ACT(stg[k_], ps(b), AF.Copy, [PSB[b]], [STB[k_]])
                                    VCOPY(dst3[:, hc, dsl], stg[k_], [STB[k_]], [dB])
                                    OUT_DMA(dkod[i, hc * 128:(hc + 1) * 128, tsl], stg[k_], STB[k_])
                for b2 in range(2):
                    sl, sb = wload(("ov", i, b2))
                    s3 = w3(sl, 8)
                    for tc in range(8):
                        b = nextbank()
                        MM(ps(b), [(hT3[:, kc, tc * 128:(tc + 1) * 128], s3[:, kc, :]) for kc in range(8)], [sb, HB], [PSB[b]])
                        if sample:
                            COPY(v3[:, 4 + tc, b2 * 512:(b2 + 1) * 512], ps(b), [PSB[b]], [VB])
                        else:
                            k_ = sk[0] % 4
                            sk[0] += 1
                            ACT(stg[k_], ps(b), AF.Copy, [PSB[b]], [STB[k_]])
                            VCOPY(v3[:, tc, b2 * 512:(b2 + 1) * 512], stg[k_], [STB[k_]], [VB])
                            OUT_DMA(dvod[i, tc * 128:(tc + 1) * 128, b2 * 512:(b2 + 1) * 512], stg[k_], STB[k_])
                nset = 1 if sample else 2
                osbs = [AR.f32(1024) for _ in range(nset)]
                OSBs = [Buf() for _ in range(nset)]
                dsbs = [AR.f32(1024) for _ in range(nset)]
                DSBs = [Buf() for _ in range(nset)]
                def mk_part2(h, qsl, nq, o2v, d2v, OSB, DSB):
                    def part2(bap, bbuf):
                        TTo(o2v[:, 0, 0:nq], o2v[:, 0, 0:nq], d2v[:, 1, 0:nq], ALU.mult, [OSB, DSB], [OSB])
                        TTo(o2v[:, 1, 0:nq], o2v[:, 1, 0:nq], d2v[:, 0, 0:nq], ALU.mult, [OSB, DSB], [OSB])
                        STT(o2v[:, 0, 0:nq], o2v[:, 1, 0:nq], misc[:, i:i + 1], o2v[:, 0, 0:nq], ALU.mult, ALU.add, [OSB, MISCB], [OSB])
                        TTo(d2v[:, 0, 0:nq], d2v[:, 0, 0:nq], d2v[:, 1, 0:nq], ALU.mult, [DSB], [DSB])
                        sq, sqB, rs, rsB = ntmp
                        ACT(sq[0][:, 0:nq], o2v[:, 0, 0:nq], AF.Square, [OSB], [sqB[0]])
                        ACT(sq[1][:, 0:nq], d2v[:, 0, 0:nq], AF.Square, [DSB], [sqB[1]], scale=math.sqrt(EPS))
                        MM(bap, [(ones_b, sq[0][:, 0:nq]), (ones_b, sq[1][:, 0:nq])], [sqB[0], sqB[1], CONSTB], [bbuf])
                        ACT(rs[:, 0:nq], bap, AF.Ln, [bbuf], [rsB], scale=1.0 / 128)
                        ACT(rs[:, 0:nq], rs[:, 0:nq], AF.Exp, [rsB], [rsB], scale=-0.5)
                        STT(mixT3[:, h, qsl], o2v[:, 0, 0:nq], misc[:, 2 + i:3 + i], rs[:, 0:nq], ALU.mult, ALU.mult, [OSB, MISCB, rsB], [MIXB[h]])
                    return part2
                nb = 0
                pend = [None]
                if not sample:
                    acc = [(4, 6), (5, 7)]
                    PT2 = [AR.bf16(1024), AR.bf16(1024)]
                    PT2B = [Buf(), Buf()]
                    items = [(q0, nq, kcs, h) for (_sq, q0, nq, kcs) in attn_blocks() for h in range(8)]
                    assert all(it_[1] == 256 for it_ in items)

                    def emitS_all(idx):
                        q0, nq, kcs, h = items[idx]
                        base = 2 * (idx % 2)
                        for j in range(2):
                            ks = slice(kcs[j] * 128, (kcs[j] + 1) * 128)
                            for n_ in range(2):
                                bk = base + n_
                                MM(ps(bk)[:, j * 256:j * 256 + 256],
                                   [(k3[n_ * 64:(n_ + 1) * 64, h, ks], q3[n_ * 64:(n_ + 1) * 64, h, q0:q0 + nq])],
                                   [KB, QB], [PSB[bk]])
                    emitS_all(0)
                    for idx, (q0, nq, kcs, h) in enumerate(items):
                        qsl = slice(q0, q0 + nq)
                        pk = idx % 2
                        base = 2 * pk
                        ptv = PT2[pk].rearrange("p (t c) -> p t c", t=4)
                        ACT(PT2[pk], psall[:, base * 512:base * 512 + 1024], AF.Exp, [PSB[base], PSB[base + 1]], [PT2B[pk]], scale=DIFF_SCALE)
                        if idx + 1 < len(items):
                            emitS_all(idx + 1)
                        for j in range(2):
                            for n_ in range(2):
                                t_ = 2 * n_ + j
                                bo, bd = acc[n_]
                                MM(ps(bo)[:, 0:nq], [(v3[:, kcs[j], h * 128:(h + 1) * 128], ptv[:, t_, 0:nq])], [VB, PT2B[pk]], [PSB[bo]],
                                   first=(j == 0), last=(j == 1))
                                MM(ps(bd)[:, 0:nq], [(ones_b, ptv[:, t_, 0:nq])], [CONSTB, PT2B[pk]], [PSB[bd]], first=(j == 0), last=(j == 1))
                        if pend[0] is not None:
                            pend[0](ps(4)[:, 256:256 + nq], PSB[4])
                            pend[0] = None
                        osb, OSB, dsb, DSB = osbs[nb % nset], OSBs[nb % nset], dsbs[nb % nset], DSBs[nb % nset]
                        nb += 1
                        o2v = osb.rearrange("p (b n) -> p b n", b=2)
                        d2v = dsb.rearrange("p (b n) -> p b n", b=2)
                        pso = psall[:, 4 * 512:6 * 512].rearrange("p (b n) -> p b n", b=2)
                        psd = psall[:, 6 * 512:8 * 512].rearrange("p (b n) -> p b n", b=2)
                        ACT(d2v[:, :, 0:nq], psd[:, :, 0:nq], AF.Copy, [PSB[6], PSB[7]], [DSB])
                        VCOPY(o2v[:, :, 0:nq], pso[:, :, 0:nq], [PSB[4], PSB[5]], [OSB])
                        pend[0] = mk_part2(h, qsl, nq, o2v, d2v, OSB, DSB)
                for (_sq, q0, nq, kcs) in (attn_blocks() if sample else []):
                    qsl = slice(q0, q0 + nq)
                    for h in range(8):
                        n = len(kcs)
                        acc = [(4, 6), (5, 7)]

                        def emitS(j):
                            ks = slice(kcs[j] * 128, (kcs[j] + 1) * 128)
                            for n_ in range(2):
                                sb_ = 2 * (j % 2) + n_
                                MM(ps(sb_)[:, 0:nq], [(k3[n_ * 64:(n_ + 1) * 64, h, ks], q3[n_ * 64:(n_ + 1) * 64, h, qsl])], [KB, QB], [PSB[sb_]])
                        emitS(0)
                        jpt = min(2, n - 1)
                        for j in range(n):
                            for n_ in range(2):
                                sb_ = 2 * (j % 2) + n_
                                ACT(PT[sb_][:, 0:nq], ps(sb_)[:, 0:nq], AF.Exp, [PSB[sb_]], [PTB[sb_]], scale=DIFF_SCALE)
                            if j + 1 < n:
                                emitS(j + 1)
                            for n_ in range(2):
                                sb_ = 2 * (j % 2) + n_
                                bo, bd = acc[n_]
                                MM(ps(bo)[:, 0:nq], [(v3[:, kcs[j], h * 128:(h + 1) * 128], PT[sb_][:, 0:nq])], [VB, PTB[sb_]], [PSB[bo]],
                                   first=(j == 0), last=(j == n - 1))
                                MM(ps(bd)[:, 0:nq], [(ones_b, PT[sb_][:, 0:nq])], [CONSTB, PTB[sb_]], [PSB[bd]], first=(j == 0), last=(j == n - 1))
                            if pend[0] is not None and j == jpt:
                                pend[0](ps(2 * (j % 2))[:, 0:nq], PSB[2 * (j % 2)])
                                pend[0] = None
                        if os.environ.get("KDBG_NOFIN"):
                            VCOPY(mixT3[:, h, qsl], ps(4)[:, 0:nq], [PSB[4], PSB[5], PSB[6], PSB[7]], [MIXB[h]])
                            continue
                        osb, OSB, dsb, DSB = osbs[nb % nset], OSBs[nb % nset], dsbs[nb % nset], DSBs[nb % nset]
                        nb += 1
                        o2v = osb.rearrange("p (b n) -> p b n", b=2)
                        d2v = dsb.rearrange("p (b n) -> p b n", b=2)
                        pso = psall[:, 4 * 512:6 * 512].rearrange("p (b n) -> p b n", b=2)
                        psd = psall[:, 6 * 512:8 * 512].rearrange("p (b n) -> p b n", b=2)
                        ACT(d2v[:, :, 0:nq], psd[:, :, 0:nq], AF.Copy, [PSB[6], PSB[7]], [DSB])
                        VCOPY(o2v[:, :, 0:nq], pso[:, :, 0:nq], [PSB[4], PSB[5]], [OSB])

                        pend[0] = mk_part2(h, qsl, nq, o2v, d2v, OSB, DSB)
                if pend[0] is not None:
                    pend[0](ps(0)[:, 0:256 if not sample else 512], PSB[0])
                    pend[0] = None
                out_proj("oo", i, l)
                S.barrier()
                AR.top = m0

            def ffn(l):
                m0 = AR.top
                actT = AR.bf16(NJ * 1024)
                act3 = actT.rearrange("p (j t) -> p j t", j=NJ)
                ACB = [Buf() for _ in range(NJ)]
                ca = [AR.f32(1024), AR.f32(1024)]
                cg = [AR.f32(1024), AR.f32(1024)]
                CAB = [Buf(), Buf()]
                CGB = [Buf(), Buf()]

                def v3d(ap):
                    return ap.rearrange("p (s t) -> p s t", s=nseq)
                do_ada = ADA_INTERLEAVE and sample and (l + 1 < DEPTH) and (groups[0] == "s")
                for jb in range(11):
                    sl, sb = wload(("up", l, jb))
                    s3 = w3(sl, 8)
                    if do_ada:
                        ada_block(l + 1, jb)
                    for jj in range(2):
                        j = 2 * jb + jj
                        r = j % 2
                        for ag in range(2):
                            bp = nextpair()
                            for th in range(2):
                                MM(ps(bp + th), [(s3[:, kc, (jj * 2 + ag) * 128:(jj * 2 + ag + 1) * 128], hT3[:, kc, th * 512:(th + 1) * 512]) for kc in range(8)],
                                   [sb, HB], [PSB[bp + th]])
                            pp = psall[:, bp * 512:bp * 512 + 1024]
                            cidx = j + ag * NJ
                            w0 = VEC[:, VOFF["cw"] + (l * 3 + 0) * 44 + cidx: VOFF["cw"] + (l * 3 + 0) * 44 + cidx + 1]
                            w1 = VEC[:, VOFF["cw"] + (l * 3 + 1) * 44 + cidx: VOFF["cw"] + (l * 3 + 1) * 44 + cidx + 1]
                            w2 = VEC[:, VOFF["cw"] + (l * 3 + 2) * 44 + cidx: VOFF["cw"] + (l * 3 + 2) * 44 + cidx + 1]
                            bb = VEC[:, VOFF["cb"] + l * 44 + cidx: VOFF["cb"] + l * 44 + cidx + 1]
                            cdst, cB = (ca[r], CAB[r]) if ag == 0 else (cg[r], CGB[r])
                            pbufs = [PSB[bp], PSB[bp + 1]]
                            ACT(cdst, pp, AF.Identity, pbufs + [VECB], [cB], bias=bb, scale=w1)
                            STT(v3d(cdst)[:, :, 1:], v3d(pp)[:, :, :-1], w0, v3d(cdst)[:, :, 1:], ALU.mult, ALU.add, pbufs + [VECB, cB], [cB])
                            STT(v3d(cdst)[:, :, :-1], v3d(pp)[:, :, 1:], w2, v3d(cdst)[:, :, :-1], ALU.mult, ALU.add, pbufs + [VECB, cB], [cB])
                        ACT(ca[r], ca[r], AF.Silu, [CAB[r]], [CAB[r]])
                        TTo(act3[:, j, :], ca[r], cg[r], ALU.mult, [CAB[r], CGB[r]], [ACB[j]])
                for m in range(8):
                    sl, sb = wload(("dn", l, m))
                    s3 = w3(sl, NJ)
                    if do_ada and m == 0:
                        ada_block(l + 1, 11)
                    for th in range(2):
                        tsl = slice(th * 512, (th + 1) * 512)
                        b = nextbank()
                        MM(ps(b), [(s3[:, j, :], act3[:, j, tsl]) for j in range(NJ)], [sb] + ACB, [PSB[b]])
                        STT(xT3[:, m, tsl], ps(b), modcol(l, 5, m), xT3[:, m, tsl], ALU.mult, ALU.add, [PSB[b], MODL[l], XB[m][th]], [XB[m][th]])
                S.barrier()
                AR.top = m0

            for l in range(nlayers):
                S.marks.append((gname + "%d.norm1" % l, S.npe))
                mk_ab(l)
                norm_mod(l, 0)
                S.marks.append((gname + "%d.mixer" % l, S.npe))
                if l % 2 == 0:
                    mixer_even(l // 2, l)
                else:
                    mixer_odd(l // 2, l)
                S.marks.append((gname + "%d.norm2" % l, S.npe))
                norm_mod(l, 1)
                S.marks.append((gname + "%d.ffn" % l, S.npe))
                ffn(l)
            S.marks.append((gname + ".final", S.npe))
            m0 = AR.top
            tmps = [mk_tmp(), mk_tmp()]
            stg = [AR.f32(512) for _ in range(4)]
            STB = [Buf() for _ in range(4)]
            k_ = 0
            for th in range(2):
                tsl = slice(th * 512, (th + 1) * 512)
                rs = rms_stats(lambda c, th: xT3[:, c, th * 512:(th + 1) * 512], lambda c, th: [XB[c][th]], 8, th, 1.0 / D, tmps[th])
                for kc in range(8):
                    STT(stg[k_ % 4], xT3[:, kc, tsl], VEC[:, VOFF["fgain"] + kc:VOFF["fgain"] + kc + 1], rs, ALU.mult, ALU.mult,
                        [XB[kc][th], VECB, tmps[th][3]], [STB[k_ % 4]])
                    OUT_DMA(yd[gname][kc * 128:(kc + 1) * 128, tsl], stg[k_ % 4], STB[k_ % 4])
                    k_ += 1
            S.barrier()
            AR.top = m0

        for gname in groups:
            run_group(gname)
        S.barrier(final=True)
        S.marks.append(("end", S.npe))
        global _MARKS, _LOG
        _MARKS = S.marks
        _LOG = S.log
        print("program: inst=%d waits=%d arena_peak=%d/%d wtotal=%d" % (S.ninst, S.nwait, AR.peak, ARENA_WORDS, WTOTAL))
    return nc, WL, WTOTAL


_CACHE = {}
_MARKS = []
_LOG = {}


def _get_program(nlayers=DEPTH, groups=("s", "p")):
    k = (nlayers, groups)
    if k not in _CACHE:
        _CACHE[k] = build_program(nlayers, groups)
    return _CACHE[k]


def make_in_maps(inp, WL, WTOTAL):
    inp = {k: np.asarray(v) for k, v in inp.items()}
    W = np.empty((128, WTOTAL), np.float32)
    off = 0
    for key, F, fn in WL:
        W[:, off:off + F] = fn(inp)
        off += F
    consts = build_consts()
    vecs = build_vecs(inp)
    wdec = np.zeros((32, 4, 256), np.float32)
    for i in range(2):
        for d in range(2):
            wdec[0:16, i * 2 + d] = inp["gla_w_decay"][i, d]
            wdec[16, i * 2 + d] = inp["gla_b_decay"][i, d]
    wdec = wdec.reshape(32, 1024)
    lamp = np.ascontiguousarray(inp["diff_lambda"].reshape(1, 512)).astype(np.float32)
    maps = []
    for c in range(NCORES):
        cvec = np.stack([inp["c_ctx"], inp["c"][c]], 0)
        cT = np.ascontiguousarray(cvec.reshape(2, 8, 128).transpose(2, 1, 0).reshape(128, 16))
        xs = np.ascontiguousarray(inp["x_sample"][c].T)
        xp = np.ascontiguousarray(inp["x_prompt"][4 * c:4 * c + 4].reshape(1024, 1024).T)
        gst = np.ascontiguousarray(inp["state_gla"][c].transpose(3, 0, 1, 2, 4).reshape(64, 2048))
        ckvT = np.ascontiguousarray(inp["cache_mla_ckv"][c].transpose(0, 2, 1))
        krcT = np.ascontiguousarray(inp["cache_mla_krope"][c].transpose(0, 2, 1))
        dkT = np.ascontiguousarray(inp["cache_diff_k"][c].reshape(2, 512, 1024).transpose(0, 2, 1))
        dv = np.ascontiguousarray(inp["cache_diff_v"][c].reshape(2, 512, 1024))
        maps.append({"wts": W, "consts": consts, "vecs": vecs, "cT": cT, "xsT": xs, "xpT": xp, "gst": gst, "ckvT": ckvT,
                     "krcT": krcT, "dkT": dkT, "dv": dv, "wdec": wdec, "lamp": lamp})
    return maps


def assemble(results):
    ys = np.stack([r["ysT"].T for r in results], 0)
    yp = np.concatenate([r["ypT"].T.reshape(4, 256, 1024) for r in results], 0)
    gs = np.concatenate([r["gso"].reshape(64, 4, 2, 2, 4, 128).transpose(1, 2, 3, 4, 0, 5) for r in results], 0)
    ckv = np.concatenate([r["ckvo"].reshape(2, 256, 4, 256).transpose(2, 0, 3, 1) for r in results], 0)
    kr = np.concatenate([r["kro"].reshape(2, 64, 4, 256).transpose(2, 0, 3, 1) for r in results], 0)
    dk = np.concatenate([r["dko"].reshape(2, 1024, 4, 256).transpose(2, 0, 3, 1).reshape(4, 2, 256, 8, 128) for r in results], 0)
    dv = np.concatenate([r["dvo"].reshape(2, 4, 256, 8, 128).transpose(1, 0, 2, 3, 4) for r in results], 0)
    f = lambda a: np.ascontiguousarray(a, dtype=np.float32)
    return (f(yp), f(ys), f(gs), f(ckv), f(kr), f(dk), f(dv))


def kernel(**inputs):
    nc, WL, WTOTAL = _get_program()
    maps = make_in_maps(inputs, WL, WTOTAL)
    res = run_bass_kernel_spmd(nc, maps, core_ids=list(range(NCORES)))
    return assemble(res.results)
```

```python
import math
import os
import contextlib
import numpy as np
import concourse.bass as bass
import concourse.mybir as mybir
from concourse.bass_utils import run_bass_kernel_spmd

F32 = mybir.dt.float32
BF16 = mybir.dt.bfloat16
AF = mybir.ActivationFunctionType
ALU = mybir.AluOpType

NCORES = 8
D = 1024
TT = 1024
DFF = 2816
NJ = 22
EPS = 1e-6
DEPTH = 4
MLA_SCALE = 192 ** -0.5
DIFF_SCALE = 64 ** -0.5
NSLOT = 4
SLOTF = 4096
ARENA_WORDS = 53000
ADA_INTERLEAVE = True


def _blk(W, cols):
    K = W.shape[0]
    nk = K // 128
    sub = W[:, cols]
    return np.ascontiguousarray(sub.reshape(nk, 128, len(cols)).transpose(1, 0, 2).reshape(128, nk * len(cols)))


_P64 = np.concatenate([np.arange(16, 32), np.arange(0, 16), np.arange(48, 64), np.arange(32, 48)])


def _perm_cols(cols):
    cols = np.asarray(cols)
    g = cols.reshape(-1, 64)
    return g[:, _P64].reshape(-1)


def weight_layout():
    L = []
    ar = np.arange

    def add(key, F, fn):
        L.append((key, F, fn))

    for l in range(DEPTH):
        for b in range(12):
            add(("ada", l, b), 4096, lambda inp, l=l, b=b: _blk(inp["w_ada"][l], b * 512 + ar(512)))
    for l in range(DEPTH):
        i = l // 2
        if l % 2 == 0:
            kr = 2080 + ar(64)
            c5 = np.concatenate([1536 + ar(32), kr, _perm_cols(kr)])
            add(("e5", i), 8 * 160, lambda inp, i=i, c5=c5: _blk(inp["w_in_even"][i], c5))
            add(("e3", i), 4096, lambda inp, i=i: _blk(inp["w_in_even"][i], 512 + ar(512)))
            add(("e2", i), 4096, lambda inp, i=i: _blk(inp["w_in_even"][i], 1024 + ar(512)))
            add(("e1", i), 4096, lambda inp, i=i: _blk(inp["w_in_even"][i], ar(512)))
            add(("e4", i), 4096, lambda inp, i=i: _blk(inp["w_in_even"][i], 1568 + ar(512)))
            nope = np.concatenate([h * 192 + ar(128) for h in range(4)])
            rope = np.concatenate([h * 192 + 128 + ar(64) for h in range(4)])
            cuq = np.concatenate([nope, rope, _perm_cols(rope)])
            add(("uq", i), 2 * 1024, lambda inp, i=i, cuq=cuq: _blk(inp["mla_w_uq"][i], cuq))
            kn = np.concatenate([h * 256 + ar(128) for h in range(4)])
            vv = np.concatenate([h * 256 + 128 + ar(128) for h in range(4)])
            ckv = np.concatenate([kn, vv])
            add(("ukv", i), 2 * 1024, lambda inp, i=i, ckv=ckv: _blk(inp["mla_w_ukv"][i], ckv))
            for b in range(2):
                add(("oe", i, b), 4096, lambda inp, i=i, b=b: _blk(inp["w_out_even"][i], b * 512 + ar(512)))
        else:
            for nm, base in (("oq", 0), ("ok", 1024), ("ov", 2048)):
                for b in range(2):
                    cols = base + b * 512 + ar(512)
                    add((nm, i, b), 4096, lambda inp, i=i, cols=cols: _blk(inp["w_in_odd"][i], cols))
                    if nm != "ov":
                        pc = _perm_cols(cols)
                        add((nm + "p", i, b), 4096, lambda inp, i=i, pc=pc: _blk(inp["w_in_odd"][i], pc))
            for b in range(2):
                add(("oo", i, b), 4096, lambda inp, i=i, b=b: _blk(inp["w_out_odd"][i], b * 512 + ar(512)))
        for jb in range(11):
            j0, j1 = 2 * jb, 2 * jb + 1
            cols = np.concatenate([j0 * 128 + ar(128), DFF + j0 * 128 + ar(128), j1 * 128 + ar(128), DFF + j1 * 128 + ar(128)])
            add(("up", l, jb), 4096, lambda inp, l=l, cols=cols: _blk(inp["ffn_w_up"][l], cols))
        for m in range(8):
            add(("dn", l, m), NJ * 128, lambda inp, l=l, m=m: _blk(inp["ffn_w_down"][l], m * 128 + ar(128)))
    return L


_VSPEC = [("bada", 4 * 48), ("ngain", 4 * 2 * 8), ("fgain", 8), ("glan", 2), ("qn", 4), ("kvn", 4), ("dn", 2),
          ("cw", 4 * 3 * 44), ("cb", 4 * 44)]
VOFF = {}
_o = 0
for _n, _c in _VSPEC:
    VOFF[_n] = _o
    _o += _c
VN = _o
CN = 2688


def _fm(v):
    return np.asarray(v, np.float32).reshape(-1, 128).T


def build_vecs(inp):
    V = np.zeros((128, VN), np.float32)
    for l in range(4):
        V[:, VOFF["bada"] + l * 48: VOFF["bada"] + (l + 1) * 48] = _fm(inp["b_ada"][l])
        for s in range(2):
            o = VOFF["ngain"] + (l * 2 + s) * 8
            V[:, o:o + 8] = _fm(inp["norm_gain"][l, s])
        for t in range(3):
            o = VOFF["cw"] + (l * 3 + t) * 44
            V[:, o:o + 44] = _fm(inp["ffn_conv_w"][l, t])
        o = VOFF["cb"] + l * 44
        V[:, o:o + 44] = _fm(inp["ffn_conv_b"][l])
    V[:, VOFF["fgain"]:VOFF["fgain"] + 8] = _fm(inp["final_gain"])
    for i in range(2):
        V[:, VOFF["glan"] + i] = inp["gla_norm"][i]
        V[:, VOFF["qn"] + 2 * i: VOFF["qn"] + 2 * i + 2] = _fm(inp["mla_q_norm"][i])
        V[:, VOFF["kvn"] + 2 * i: VOFF["kvn"] + 2 * i + 2] = _fm(inp["mla_kv_norm"][i])
        V[:, VOFF["dn"] + i] = inp["diff_norm"][i]
    return V


def build_consts():
    C = np.zeros((128, CN), np.float32)
    C[:, 0:128] = np.eye(128, dtype=np.float32)
    C[:, 128:256] = 1.0
    s = np.arange(128)[:, None]
    t = np.arange(128)[None, :]
    C[:, 256:384] = (s <= t)
    C[:, 384:512] = (s >= t)
    tt = np.arange(1024)
    row = (tt // 64).astype(np.float32)
    col = (tt % 64).astype(np.float32)
    inv = (np.float32(10000.0) ** (-np.arange(0, 32, 2, dtype=np.float32) / np.float32(32))).astype(np.float32)
    ar_ = (row[:, None] * inv).astype(np.float32).T
    ac_ = (col[:, None] * inv).astype(np.float32).T
    cosr, sinr, cosc, sinc = np.cos(ar_), np.sin(ar_), np.cos(ac_), np.sin(ac_)
    C64 = np.concatenate([cosr, cosr, cosc, cosc], 0)
    S64 = np.concatenate([-sinr, sinr, -sinc, sinc], 0)
    C[:, 512:1536] = np.concatenate([C64, C64], 0)
    C[:, 1536:2560] = np.concatenate([S64, S64], 0)
    pm = np.concatenate([_P64, 64 + _P64])
    P = np.zeros((128, 128), np.float32)
    P[pm, np.arange(128)] = 1.0
    C[:, 2560:2688] = P
    return C


class Buf:
    __slots__ = ("w", "r", "name")

    def __init__(self, name=""):
        self.w = None
        self.r = {}
        self.name = name


class Sch:
    ENG = ("pe", "act", "dve", "pool", "sp")

    def __init__(self, nc, es):
        self.nc = nc
        self.es = es
        self.eng = {"pe": nc.tensor, "act": nc.scalar, "dve": nc.vector, "pool": nc.gpsimd, "sp": nc.sync}
        self.sem = {}
        self.cnt = {}
        for e in self.ENG:
            self.sem[e] = es.enter_context(nc.semaphore("s_" + e))
            self.cnt[e] = 0
        self.seen = {e: {} for e in self.ENG}
        self.snap = {}
        self.nwait = 0
        self.ninst = 0
        self.npe = 0
        self.marks = []
        self.log = {e: [] for e in self.ENG}

    def chan(self, key):
        if key not in self.sem:
            self.sem[key] = self.es.enter_context(self.nc.semaphore("d_" + key))
            self.cnt[key] = 0

    def _deps(self, reads, writes):
        d = {}
        for b in reads:
            if b.w is not None:
                k, v = b.w
                if d.get(k, 0) < v:
                    d[k] = v
        for b in writes:
            if b.w is not None:
                k, v = b.w
                if d.get(k, 0) < v:
                    d[k] = v
            for k, v in b.r.items():
                if d.get(k, 0) < v:
                    d[k] = v
        return d

    def _wait(self, e, d):
        seen = self.seen[e]
        h = self.eng[e]
        for k, v in d.items():
            if k == e and e == "pe":
                continue
            if seen.get(k, 0) < v:
                h.wait_ge(self.sem[k], v)
                seen[k] = v
                self.nwait += 1
                self.log[e].append(("w", k, v))

    def _mark(self, ev, reads, writes):
        k, v = ev
        for b in reads:
            if b.r.get(k, 0) < v:
                b.r[k] = v
        for b in writes:
            b.w = ev
            b.r = {}

    def op(self, e, fn, reads=(), writes=()):
        self._wait(e, self._deps(reads, writes))
        inst = fn(self.eng[e])
        self.cnt[e] += 1
        inst.then_inc(self.sem[e], 1)
        self.ninst += 1
        self.log[e].append(("i", e, 1))
        self._mark((e, self.cnt[e]), reads, writes)

    def mm(self, fns, reads=(), writes=()):
        self._wait("pe", self._deps(reads, writes))
        inst = None
        for f in fns:
            inst = f(self.eng["pe"])
            self.ninst += 1
            self.npe += 1
        self.cnt["pe"] += 1
        inst.then_inc(self.sem["pe"], 1)
        self.log["pe"].append(("i", "pe", 1))
        self._mark(("pe", self.cnt["pe"]), reads, writes)

    def dma(self, q, key, out, in_, reads=(), writes=(), join=False):
        self.chan(key)
        d = self._deps(reads, writes)
        if self.cnt[key] > 0 and d.get(key, 0) < self.cnt[key]:
            d[key] = self.cnt[key]
        if q == "pool" and join:
            for k, v in self.snap.items():
                if d.get(k, 0) < v:
                    d[k] = v
        self._wait(q, d)
        self.cnt[key] += 16
        self.eng[q].dma_start(out=out, in_=in_).then_inc(self.sem[key], 16)
        self.ninst += 1
        self.log[q].append(("i", key, 16))
        self._mark((key, self.cnt[key]), reads, writes)

    def dma_multi(self, q, key, pieces, reads=(), writes=(), join=False):
        self.chan(key)
        d = self._deps(reads, writes)
        if self.cnt[key] > 0 and d.get(key, 0) < self.cnt[key]:
            d[key] = self.cnt[key]
        if q == "pool" and join:
            for k, v in self.snap.items():
                if d.get(k, 0) < v:
                    d[k] = v
        self._wait(q, d)
        for (out, in_) in pieces:
            self.cnt[key] += 16
            self.eng[q].dma_start(out=out, in_=in_).then_inc(self.sem[key], 16)
            self.ninst += 1
            self.log[q].append(("i", key, 16))
        self._mark((key, self.cnt[key]), reads, writes)

    def barrier(self, final=False):
        d = {k: v for k, v in self.cnt.items() if v > 0}
        for e in self.ENG:
            if e == "pool" and not final:
                continue
            self._wait(e, d)
        self.snap = dict(d)


class Arena:
    def __init__(self, ap, nwords):
        self.ap = ap
        self.n = nwords
        self.top = 0
        self.peak = 0

    def f32(self, words):
        off = self.top
        self.top += words
        self.peak = max(self.peak, self.top)
        assert self.top <= self.n, "arena overflow %d > %d" % (self.top, self.n)
        return self.ap[:, off:off + words]

    def bf16(self, elems):
        words = (elems + 1) // 2
        return self.f32(words).bitcast(BF16)


def build_program(nlayers=DEPTH, groups=("s", "p")):
    WL = weight_layout()
    WOFF = {}
    off = 0
    for key, F, _ in WL:
        WOFF[key] = (off, F)
        off += F
    WTOTAL = off

    nc = bass.Bass("TRN2", target_bir_lowering=False)

    def din(name, shape):
        return nc.dram_tensor(name, shape, F32, kind="ExternalInput").ap()

    def dout(name, shape):
        return nc.dram_tensor(name, shape, F32, kind="ExternalOutput").ap()

    Wd = din("wts", [128, WTOTAL])
    constd = din("consts", [128, CN])
    vecd = din("vecs", [128, VN])
    cTd = din("cT", [128, 16])
    xd = {"s": din("xsT", [1024, 1024]), "p": din("xpT", [1024, 1024])}
    gstd = din("gst", [64, 2048])
    ckvTd = din("ckvT", [2, 256, 512])
    krcTd = din("krcT", [2, 64, 512])
    dkTd = din("dkT", [2, 1024, 512])
    dvd = din("dv", [2, 512, 1024])
    wdecd = din("wdec", [32, 1024])
    lampd = din("lamp", [1, 512])
    yd = {"s": dout("ysT", [1024, 1024]), "p": dout("ypT", [1024, 1024])}
    gsod = dout("gso", [64, 8192])
    ckvod = dout("ckvo", [2, 256, 1024])
    krod = dout("kro", [2, 64, 1024])
    dkod = dout("dko", [2, 1024, 1024])
    dvod = dout("dvo", [2, 1024, 1024])

    es = contextlib.ExitStack()
    with es:
        arena_t = es.enter_context(nc.sbuf_tensor("arena", [128, ARENA_WORDS], F32))
        psall = es.enter_context(nc.psum_tensor("ps", [128, 4096], F32))
        psbf_all = psall.bitcast(BF16)
        S = Sch(nc, es)
        AR = Arena(arena_t, ARENA_WORDS)

        PSB = [Buf("ps%d" % i) for i in range(8)]
        st = {"bank": 0, "slot": 0, "och": 0, "alt": 0}

        def ps(b, n=512):
            return psall[:, b * 512:b * 512 + n]

        def psbf(b):
            return psbf_all[:, b * 1024:(b + 1) * 1024]

        def nextbank():
            b = st["bank"]
            st["bank"] = (b + 1) % 8
            return b

        def nextpair():
            b = st["bank"]
            if b % 2:
                b = (b + 1) % 8
            st["bank"] = (b + 2) % 8
            return b

        def ACT(out, in_, func, reads, writes, bias=0.0, scale=1.0):
            S.op("act", lambda e: e.activation(out=out, in_=in_, func=func, bias=bias, scale=scale), reads, writes)

        def TTo(out, a, b, op, reads, writes):
            S.op("dve", lambda e: e.tensor_tensor(out=out, in0=a, in1=b, op=op), reads, writes)

        def TS(out, a, s1, op0, reads, writes, s2=None, op1=None):
            if op1 is None:
                S.op("dve", lambda e: e.tensor_scalar(out=out, in0=a, scalar1=s1, scalar2=None, op0=op0), reads, writes)
            else:
                S.op("dve", lambda e: e.tensor_scalar(out=out, in0=a, scalar1=s1, scalar2=s2, op0=op0, op1=op1), reads, writes)

        def STT(out, a, scalar, b, op0, op1, reads, writes):
            S.op("dve", lambda e: e.scalar_tensor_tensor(out=out, in0=a, scalar=scalar, in1=b, op0=op0, op1=op1), reads, writes)

        def VCOPY(out, in_, reads, writes):
            S.op("dve", lambda e: e.tensor_copy(out=out, in_=in_), reads, writes)

        def COPY(out, in_, reads, writes):
            st["alt"] ^= 1
            if st["alt"]:
                ACT(out, in_, AF.Copy, reads, writes)
            else:
                VCOPY(out, in_, reads, writes)

        def RECIP(out, in_, reads, writes):
            S.op("dve", lambda e: e.reciprocal(out=out, in_=in_), reads, writes)

        def MM(out, pairs, reads, writes, first=True, last=True):
            n = len(pairs)
            fns = []
            for idx, (l_, r_) in enumerate(pairs):
                fns.append(lambda e, l_=l_, r_=r_, idx=idx: e.matmul(out, lhsT=l_, rhs=r_, start=(first and idx == 0),
                                                                    stop=(last and idx == n - 1)))
            S.mm(fns, reads, writes)

        def OUT_DMA(dst, src, srcbuf):
            k = "o%d" % st["och"]
            st["och"] = (st["och"] + 1) % 8
            S.dma("sp", k, dst, src, reads=[srcbuf], writes=[])

        CONST = AR.f32(CN)
        CONSTB = Buf("const")
        ident_f = CONST[:, 0:128]
        ones_f = CONST[:, 128:256]
        TRI = [CONST[:, 256:384], CONST[:, 384:512]]
        ropeC = CONST[:, 512:1536]
        ropeS = CONST[:, 1536:2560]
        cbf = AR.bf16(384)
        ident_b = cbf[:, 0:128]
        ones_b = cbf[:, 128:256]
        perm_b = cbf[:, 256:384]
        VEC = AR.f32(VN)
        VECB = Buf("vec")
        MOD = AR.f32(4 * 96)
        MOD4 = MOD.rearrange("p (l c g) -> p l c g", l=4, g=2)
        MODB = Buf("mod")
        cT = AR.f32(16)
        sTb = AR.bf16(16)
        sT3 = sTb.rearrange("p (k g) -> p k g", g=2)
        ABt = AR.f32(64)
        AB4 = ABt.rearrange("p (l s k) -> p l s k", l=4, s=2)
        ABB = Buf("ab")
        misc = AR.f32(16)
        MISCB = Buf("misc")
        lamt = AR.f32(512 + 16)
        wdecb = AR.bf16(1024)
        wdec3 = wdecb.rearrange("p (a c) -> p a c", a=4)
        WDECB = Buf("wdec")
        xT = AR.f32(8192)
        xT3 = xT.rearrange("p (k t) -> p k t", k=8)
        XB = [[Buf("x%d%d" % (k, t)) for t in range(2)] for k in range(8)]
        XALL = [XB[k][t] for k in range(8) for t in range(2)]
        hT = AR.bf16(8192)
        hT3 = hT.rearrange("p (k t) -> p k t", k=8)
        HB = Buf("h")
        mixT = AR.bf16(8192)
        mixT3 = mixT.rearrange("p (k t) -> p k t", k=8)
        MIXB = [Buf("mix%d" % k) for k in range(8)]
        slots = [AR.bf16(SLOTF) for _ in range(NSLOT)]
        SLB = [Buf("slot%d" % k) for k in range(NSLOT)]

        def wload(key):
            off, F = WOFF[key]
            k = st["slot"]
            st["slot"] = (k + 1) % NSLOT
            S.dma("pool", "w%d" % k, slots[k][:, 0:F], Wd[:, off:off + F], reads=[], writes=[SLB[k]])
            return slots[k][:, 0:F], SLB[k]

        def w3(ap, nk):
            return ap.rearrange("p (k c) -> p k c", k=nk)

        S.dma("sp", "i0", CONST, constd, writes=[CONSTB])
        S.dma("sp", "i1", VEC, vecd, writes=[VECB])
        S.dma("sp", "i2", cT, cTd, writes=[MODB])
        S.dma("sp", "i3", lamt[0:1, 0:512], lampd, writes=[MISCB])
        S.dma("pool", "i4", wdecb[0:32, :], wdecd, writes=[WDECB])
        VCOPY(cbf[:, 0:256], CONST[:, 0:256], [CONSTB], [CONSTB])
        VCOPY(perm_b, CONST[:, 2560:2688], [CONSTB], [CONSTB])
        ACT(sTb, cT, AF.Silu, [MODB], [MODB])
        lam_init = [0.8 - 0.6 * math.exp(-0.3 * (2 * i + 1)) for i in range(2)]
        lsc = lamt[0:1, 512:528]
        for i in range(2):
            base = i * 256
            TTo(lamt[0:1, base:base + 64], lamt[0:1, base:base + 64], lamt[0:1, base + 64:base + 128], ALU.mult, [MISCB], [MISCB])
            TTo(lamt[0:1, base + 128:base + 192], lamt[0:1, base + 128:base + 192], lamt[0:1, base + 192:base + 256], ALU.mult, [MISCB], [MISCB])
            S.op("dve", lambda e: e.reduce_sum(out=lsc[:, 4 * i:4 * i + 1], in_=lamt[0:1, base:base + 64], axis=mybir.AxisListType.X), [MISCB], [MISCB])
            S.op("dve", lambda e: e.reduce_sum(out=lsc[:, 4 * i + 1:4 * i + 2], in_=lamt[0:1, base + 128:base + 192], axis=mybir.AxisListType.X), [MISCB], [MISCB])
            ACT(lsc[:, 4 * i:4 * i + 2], lsc[:, 4 * i:4 * i + 2], AF.Exp, [MISCB], [MISCB])
            TTo(lsc[:, 4 * i + 2:4 * i + 3], lsc[:, 4 * i + 1:4 * i + 2], lsc[:, 4 * i:4 * i + 1], ALU.subtract, [MISCB], [MISCB])
            TS(lsc[:, 4 * i + 2:4 * i + 3], lsc[:, 4 * i + 2:4 * i + 3], -lam_init[i], ALU.add, [MISCB], [MISCB])
            b = nextbank()
            MM(ps(b)[:, 0:1], [(ones_f[0:1, :], lsc[:, 4 * i + 2:4 * i + 3])], [MISCB, CONSTB], [PSB[b]])
            VCOPY(misc[:, i:i + 1], ps(b)[:, 0:1], [PSB[b]], [MISCB])
            TS(misc[:, 2 + i:3 + i], VEC[:, VOFF["dn"] + i:VOFF["dn"] + i + 1], 1.0 - lam_init[i], ALU.mult, [VECB, MISCB], [MISCB])

        MODL = [Buf("mod%d" % l) for l in range(4)]

        def ada_block(l, blk):
            sl, sb = wload(("ada", l, blk))
            s3 = w3(sl, 8)
            b = nextbank()
            for oc in range(4):
                MM(ps(b)[:, oc * 2:oc * 2 + 2], [(s3[:, kc, oc * 128:(oc + 1) * 128], sT3[:, kc, :]) for kc in range(8)],
                   [sb, MODB], [PSB[b]])
            pv = ps(b)[:, 0:8].rearrange("p (c g) -> p c g", g=2)
            c0 = blk * 4
            for g in range(2):
                TTo(MOD4[:, l, c0:c0 + 4, g], pv[:, :, g], VEC[:, VOFF["bada"] + l * 48 + c0: VOFF["bada"] + l * 48 + c0 + 4], ALU.add,
                    [PSB[b], VECB], [MODL[l]])
        for l_ in range(1 if ADA_INTERLEAVE else 4):
            for blk in range(12):
                ada_block(l_, blk)

        def run_group(gname):
            sample = gname == "s"
            g = 1 if sample else 0
            nseq, L = (1, 1024) if sample else (4, 256)
            cps = L // 128
            S.dma_multi("sp", "x0", [(xT3[:, kc, :], xd[gname][kc * 128:(kc + 1) * 128, :]) for kc in range(8)], writes=XALL)
            for l in range(4):
                for s in range(2):
                    pass

            def mk_ab(l):
                for s in range(2):
                    STT(AB4[:, l, s, :], MOD4[:, l, (1 + 3 * s) * 8:(2 + 3 * s) * 8, g], 1.0,
                        VEC[:, VOFF["ngain"] + (l * 2 + s) * 8: VOFF["ngain"] + (l * 2 + s + 1) * 8], ALU.add, ALU.mult,
                        [MODL[l], VECB], [ABB])

            def modcol(l, part, kc):
                return MOD4[:, l, part * 8 + kc, g:g + 1]

            def rms_stats(src_fn, srcbufs_fn, nchunks, th, inv_n, tmp):
                sq, sqB, rs, rsB = tmp
                b = nextbank()
                for c in range(nchunks):
                    ACT(sq[c % 2], src_fn(c, th), AF.Square, srcbufs_fn(c, th), [sqB[c % 2]])
                    MM(ps(b), [(ones_b, sq[c % 2])], [sqB[c % 2], CONSTB], [PSB[b]], first=(c == 0), last=(c == nchunks - 1))
                ACT(rs, ps(b), AF.Ln, [PSB[b]], [rsB], bias=EPS, scale=inv_n)
                ACT(rs, rs, AF.Exp, [rsB], [rsB], scale=-0.5)
                return rs

            def mk_tmp():
                sq = [AR.bf16(512), AR.bf16(512)]
                return (sq, [Buf(), Buf()], AR.f32(512), Buf())

            def norm_mod(l, s):
                m = AR.top
                tmps = [mk_tmp(), mk_tmp()]
                t2 = [AR.f32(512), AR.f32(512)]
                t2B = [Buf(), Buf()]
                for th in range(2):
                    tsl = slice(th * 512, (th + 1) * 512)
                    rs = rms_stats(lambda c, th: xT3[:, c, th * 512:(th + 1) * 512], lambda c, th: [XB[c][th]], 8, th, 1.0 / D, tmps[th])
                    rsB = tmps[th][3]
                    for kc in range(8):
                        TTo(t2[kc % 2], xT3[:, kc, tsl], rs, ALU.mult, [XB[kc][th], rsB], [t2B[kc % 2]])
                        ACT(hT3[:, kc, tsl], t2[kc % 2], AF.Identity, [t2B[kc % 2], ABB, MODL[l]], [HB],
                            bias=modcol(l, 3 * s, kc), scale=AB4[:, l, s, kc:kc + 1])
                AR.top = m

            def out_proj(key, i, l):
                for b2 in range(2):
                    sl, sb = wload((key, i, b2))
                    s3 = w3(sl, 8)
                    for mo in range(4):
                        mc = b2 * 4 + mo
                        for th in range(2):
                            tsl = slice(th * 512, (th + 1) * 512)
                            b = nextbank()
                            MM(ps(b), [(s3[:, kc, mo * 128:(mo + 1) * 128], mixT3[:, kc, tsl]) for kc in range(8)],
                               [sb] + MIXB, [PSB[b]])
                            STT(xT3[:, mc, tsl], ps(b), modcol(l, 2, mc), xT3[:, mc, tsl], ALU.mult, ALU.add,
                                [PSB[b], MODL[l], XB[mc][th]], [XB[mc][th]])

            def rope_evac(dst, b0, b1, rows, tsl, dstbufs, tmp):
                (t1, t1B, t2_, t2B_) = tmp
                TTo(t1[0:rows, :], ps(b0)[0:rows, :], ropeC[0:rows, tsl], ALU.mult, [PSB[b0], CONSTB], [t1B])
                TTo(t2_[0:rows, :], ps(b1)[0:rows, :], ropeS[0:rows, tsl], ALU.mult, [PSB[b1], CONSTB], [t2B_])
                TTo(dst, t1[0:rows, :], t2_[0:rows, :], ALU.add, [t1B, t2B_], dstbufs)

            def attn_blocks():
                if sample:
                    return [(0, qb * 512, 512, list(range(12))) for qb in range(2)]
                return [(s_, s_ * 256, 256, [2 * s_, 2 * s_ + 1]) for s_ in range(4)]

            def mixer_even(i, l):
                m0 = AR.top
                sl5, sb5 = wload(("e5", i))
                s53 = w3(sl5, 8)
                lrT = [AR.bf16(1024), AR.bf16(1024)]
                lrB = [Buf(), Buf()]
                for d in range(2):
                    S.op("dve", lambda e: e.memset(lrT[d][0:32, :], 1.0), [], [lrB[d]])
                    for th in range(2):
                        b = nextbank()
                        MM(ps(b)[0:16, :], [(s53[:, kc, d * 16:(d + 1) * 16], hT3[:, kc, th * 512:(th + 1) * 512]) for kc in range(8)],
                           [sb5, HB], [PSB[b]])
                        ACT(lrT[d][0:16, th * 512:(th + 1) * 512], ps(b)[0:16, :], AF.Copy, [PSB[b]], [lrB[d]])
                spT = AR.f32(4096)
                spT3 = spT.rearrange("p (c f) -> p c f", c=8)
                SPB = Buf()
                etmp = [AR.f32(512), AR.f32(512)]
                etB = [Buf(), Buf()]
                for tc in range(8):
                    b = nextbank()
                    for d in range(2):
                        MM(ps(b)[:, d * 256:(d + 1) * 256], [(lrT[d][0:17, tc * 128:(tc + 1) * 128], wdec3[0:17, i * 2 + d, :])],
                           [lrB[d], WDECB], [PSB[b]])
                    ACT(etmp[tc % 2], ps(b), AF.Exp, [PSB[b]], [etB[tc % 2]], scale=-1.0)
                    ACT(spT3[:, tc, :], etmp[tc % 2], AF.Ln, [etB[tc % 2]], [SPB], bias=1.0)
                sl3, sb3 = wload(("e3", i))
                s33 = w3(sl3, 8)
                vTok = AR.bf16(4096)
                vT3 = vTok.rearrange("p (c f) -> p c f", c=8)
                VTB = Buf()
                for tc in range(8):
                    b = nextbank()
                    MM(ps(b), [(hT3[:, kc, tc * 128:(tc + 1) * 128], s33[:, kc, :]) for kc in range(8)], [sb3, HB], [PSB[b]])
                    COPY(vT3[:, tc, :], ps(b), [PSB[b]], [VTB])
                slg, sbg = wload(("e2", i))
                sg3 = w3(slg, 8)
                slqk, sbqk = wload(("e1", i))
                sqk3 = w3(slqk, 8)
                Sst = AR.f32(1024)
                Sst3 = Sst.rearrange("p (a v) -> p a v", a=8)
                Sbf = AR.bf16(1024)
                Sbf3 = Sbf.rearrange("p (a v) -> p a v", a=8)
                SSB = [Buf() for _ in range(8)]
                SBB = [Buf() for _ in range(8)]
                if sample:
                    S.dma("sp", "i5", Sst[0:64, :], gstd[:, i * 1024:(i + 1) * 1024], writes=SSB)
                    VCOPY(Sbf[0:64, :], Sst[0:64, :], SSB, SBB)
                else:
                    stage = AR.f32(1024)
                    stage3 = stage.rearrange("p (a v) -> p a v", a=8)
                    STGB = Buf()
                eb = [AR.f32(1024), AR.f32(1024)]
                enb = [AR.f32(1024), AR.f32(1024)]
                EBB = [Buf(), Buf()]
                ENB = [Buf(), Buf()]
                Qt = [AR.bf16(1024), AR.bf16(1024)]
                Kt = [AR.bf16(1024), AR.bf16(1024)]
                QKB = [Buf(), Buf()]
                Ktok = [AR.bf16(512), AR.bf16(512)]
                Ktok3 = [k_.rearrange("p (c f) -> p c f", c=8) for k_ in Ktok]
                KTB = [Buf(), Buf()]
                Am = [AR.bf16(128) for _ in range(4)]
                AMB = [Buf() for _ in range(4)]
                oT = AR.f32(1024)
                OTB = Buf()
                ntmp = mk_tmp()
                gsb = AR.f32(512)
                GSB = Buf()
                t3 = AR.f32(512)
                T3B = Buf()
                amk = 0
                for h in range(4):
                    for d in range(2):
                        bp = nextpair()
                        for tc in range(8):
                            MM(psall[0:64, bp * 512 + tc * 128: bp * 512 + (tc + 1) * 128],
                               [(spT3[:, tc, d * 256 + h * 64: d * 256 + (h + 1) * 64], TRI[d])], [SPB, CONSTB], [PSB[bp], PSB[bp + 1]])
                        ACT(eb[d][0:64, :], psall[0:64, bp * 512:bp * 512 + 1024], AF.Exp, [PSB[bp], PSB[bp + 1]], [EBB[d]], scale=-1.0 / 16)
                        ACT(enb[d][0:64, :], psall[0:64, bp * 512:bp * 512 + 1024], AF.Exp, [PSB[bp], PSB[bp + 1]], [ENB[d]], scale=1.0 / 16)
                    for which in range(2):
                        coloff = which * 256 + h * 64
                        for th in range(2):
                            tsl = slice(th * 512, (th + 1) * 512)
                            b = nextbank()
                            MM(ps(b)[0:64, :], [(sqk3[:, kc, coloff:coloff + 64], hT3[:, kc, tsl]) for kc in range(8)], [sbqk, HB], [PSB[b]])
                            for d in range(2):
                                if which == 0:
                                    STT(Qt[d][0:64, tsl], ps(b)[0:64, :], 0.125, eb[d][0:64, tsl], ALU.mult, ALU.mult,
                                        [PSB[b], EBB[d]], [QKB[d]])
                                else:
                                    TTo(Kt[d][0:64, tsl], ps(b)[0:64, :], enb[d][0:64, tsl], ALU.mult, [PSB[b], ENB[d]], [QKB[d]])
                    for d in range(2):
                        b = nextbank()
                        S.mm([lambda e, tc=tc: e.transpose(psbf(b)[:, tc * 64:(tc + 1) * 64], Kt[d][0:64, tc * 128:(tc + 1) * 128], ident_b[0:64, 0:64])
                              for tc in range(8)], [QKB[d], CONSTB], [PSB[b]])
                        COPY(Ktok[d], psbf(b)[:, 0:512], [PSB[b]], [KTB[d]])
                    op_ = [nextpair(), nextpair()]
                    accb = (op_[0], op_[0] + 1, op_[1], op_[1] + 1)

                    def fb():
                        b_ = nextbank()
                        while b_ in accb:
                            b_ = nextbank()
                        return b_

                    def emitA(step):
                        for d in range(2):
                            tc = step if d == 0 else 7 - step
                            tcs = slice(tc * 128, (tc + 1) * 128)
                            b = fb()
                            MM(ps(b)[:, 0:128], [(Kt[d][0:64, tcs], Qt[d][0:64, tcs])], [QKB[d]], [PSB[b]])
                            k_ = (step % 2) * 2 + d
                            TTo(Am[k_], ps(b)[:, 0:128], TRI[d], ALU.mult, [PSB[b], CONSTB], [AMB[k_]])
                    emitA(0)
                    for step in range(8):
                        if step + 1 < 8:
                            emitA(step + 1)
                        flags = []
                        for d in range(2):
                            tc = step if d == 0 else 7 - step
                            pos = tc % cps
                            first_in_seq = (pos == 0) if d == 0 else (pos == cps - 1)
                            last_in_seq = (pos == cps - 1) if d == 0 else (pos == 0)
                            need_state = (not last_in_seq) or (not sample)
                            b2 = None
                            if need_state:
                                b2 = fb()
                                MM(ps(b2)[0:64, 0:128], [(Ktok3[d][:, tc, :], vT3[:, tc, h * 128:(h + 1) * 128])], [KTB[d], VTB], [PSB[b2]])
                            flags.append((tc, first_in_seq, last_in_seq, need_state, b2))
                        for d in range(2):
                            tc, first_in_seq, last_in_seq, need_state, b2 = flags[d]
                            tcs = slice(tc * 128, (tc + 1) * 128)
                            sidx = d * 4 + h
                            k_ = (step % 2) * 2 + d
                            pairs = [(vT3[:, tc, h * 128:(h + 1) * 128], Am[k_])]
                            rd = [VTB, AMB[k_]]
                            if sample or not first_in_seq:
                                pairs.append((Sbf3[0:64, sidx, :], Qt[d][0:64, tcs]))
                                rd += [SBB[sidx], QKB[d]]
                            ob = op_[d]
                            MM(psall[:, ob * 512 + tc * 128: ob * 512 + (tc + 1) * 128], pairs, rd, [PSB[ob], PSB[ob + 1]])
                        for d in range(2):
                            tc, first_in_seq, last_in_seq, need_state, b2 = flags[d]
                            sidx = d * 4 + h
                            if need_state:
                                col = tc * 128 + 127 if d == 0 else tc * 128
                                ebc = eb[d][0:64, col:col + 1]
                                final_out = last_in_seq and not sample
                                if final_out:
                                    dst = stage3[0:64, (tc // cps) * 2 + d, :]
                                    dB = [STGB]
                                else:
                                    dst = Sst3[0:64, sidx, :]
                                    dB = [SSB[sidx]]
                                if first_in_seq and not sample:
                                    TS(dst, ps(b2)[0:64, 0:128], ebc, ALU.mult, [PSB[b2], EBB[d]], dB)
                                else:
                                    TS(Sst3[0:64, sidx, :], Sst3[0:64, sidx, :], ebc, ALU.mult, [SSB[sidx], EBB[d]], [SSB[sidx]])
                                    STT(dst, ps(b2)[0:64, 0:128], ebc, Sst3[0:64, sidx, :], ALU.mult, ALU.add,
                                        [PSB[b2], EBB[d], SSB[sidx]], dB)
                                if not last_in_seq:
                                    ACT(Sbf3[0:64, sidx, :], Sst3[0:64, sidx, :], AF.Copy, [SSB[sidx]], [SBB[sidx]])
                    if not sample:
                        for sq_ in range(4):
                            for d in range(2):
                                o_ = (((sq_ * 2 + i) * 2 + d) * 4 + h) * 128
                                OUT_DMA(gsod[:, o_:o_ + 128], stage3[0:64, sq_ * 2 + d, :], STGB)
                    ACT(oT, psall[:, op_[0] * 512: op_[0] * 512 + 1024], AF.Copy, [PSB[op_[0]], PSB[op_[0] + 1]], [OTB])
                    TTo(oT, psall[:, op_[1] * 512: op_[1] * 512 + 1024], oT, ALU.add, [PSB[op_[1]], PSB[op_[1] + 1], OTB], [OTB])
                    for th in range(2):
                        tsl = slice(th * 512, (th + 1) * 512)
                        rs = rms_stats(lambda c, th: oT[:, th * 512:(th + 1) * 512], lambda c, th: [OTB], 1, th, 1.0 / 128, ntmp)
                        b = nextbank()
                        MM(ps(b), [(sg3[:, kc, h * 128:(h + 1) * 128], hT3[:, kc, tsl]) for kc in range(8)], [sbg, HB], [PSB[b]])
                        ACT(gsb, ps(b), AF.Silu, [PSB[b]], [GSB])
                        TTo(t3, oT[:, tsl], rs, ALU.mult, [OTB, ntmp[3]], [T3B])
                        STT(mixT3[:, h, tsl], t3, VEC[:, VOFF["glan"] + i:VOFF["glan"] + i + 1], gsb, ALU.mult, ALU.mult,
                            [T3B, VECB, GSB], [MIXB[h]])
                S.barrier()
                AR.top = m0
                S.marks.append((gname + "%d.mla" % l, S.npe))
                SK = 1536 if sample else 1024
                KOFF = 512 if sample else 0
                sl5, sb5 = wload(("e5", i))
                s53 = w3(sl5, 8)
                sl4, sb4 = wload(("e4", i))
                s43 = w3(sl4, 8)
                cbuf = AR.f32(2048)
                cb3 = cbuf.rearrange("p (c t) -> p c t", c=2)
                CBB = Buf()
                cqn = AR.bf16(2048)
                cqn3 = cqn.rearrange("p (c t) -> p c t", c=2)
                CQB = Buf()
                ckn = AR.bf16(2 * SK)
                ckn3 = ckn.rearrange("p (c t) -> p c t", c=2)
                CKB = Buf()
                krT = AR.bf16(SK)
                KRB = Buf()
                ntmp1 = mk_tmp()
                ntmps = [ntmp1, ntmp1]
                stg = [AR.f32(512) for _ in range(2 if sample else 4)]
                STB = [Buf() for _ in range(4)]
                sk = [0]

                def nstg():
                    k_ = sk[0] % 4
                    sk[0] += 1
                    return stg[k_], STB[k_]
                rtmp = (stg[0], STB[0], stg[1], STB[1])
                if sample:
                    S.dma_multi("pool", "i6", [(ckn3[:, c, 0:512], ckvTd[i, c * 128:(c + 1) * 128, :]) for c in range(2)]
                                + [(krT[0:64, 0:512], krcTd[i])], writes=[CKB, KRB], join=True)
                for kind in range(2):
                    coloff = kind * 256
                    gname_ = "qn" if kind == 0 else "kvn"
                    for th in range(2):
                        tsl = slice(th * 512, (th + 1) * 512)
                        for c in range(2):
                            b = nextbank()
                            MM(ps(b), [(s43[:, kc, coloff + c * 128: coloff + (c + 1) * 128], hT3[:, kc, tsl]) for kc in range(8)],
                               [sb4, HB], [PSB[b]])
                            COPY(cb3[:, c, tsl], ps(b), [PSB[b]], [CBB])
                    for th in range(2):
                        tsl = slice(th * 512, (th + 1) * 512)
                        rs = rms_stats(lambda c, th: cb3[:, c, th * 512:(th + 1) * 512], lambda c, th: [CBB], 2, th, 1.0 / 256, ntmps[th])
                        rsB = ntmps[th][3]
                        for c in range(2):
                            gcol = VEC[:, VOFF[gname_] + 2 * i + c: VOFF[gname_] + 2 * i + c + 1]
                            if kind == 0:
                                STT(cqn3[:, c, tsl], cb3[:, c, tsl], gcol, rs, ALU.mult, ALU.mult, [CBB, VECB, rsB], [CQB])
                            elif sample:
                                STT(ckn3[:, c, 512 + th * 512: 512 + (th + 1) * 512], cb3[:, c, tsl], gcol, rs, ALU.mult, ALU.mult,
                                    [CBB, VECB, rsB], [CKB])
                            else:
                                sg_, sgB = nstg()
                                STT(sg_, cb3[:, c, tsl], gcol, rs, ALU.mult, ALU.mult, [CBB, VECB, rsB], [sgB])
                                ACT(ckn3[:, c, tsl], sg_, AF.Copy, [sgB], [CKB])
                                OUT_DMA(ckvod[i, c * 128:(c + 1) * 128, tsl], sg_, sgB)
                for th in range(2):
                    tsl = slice(th * 512, (th + 1) * 512)
                    b = nextbank()
                    MM(ps(b)[0:64, :], [(s53[:, kc, 32:96], hT3[:, kc, tsl]) for kc in range(8)], [sb5, HB], [PSB[b]])
                    if sample:
                        b1 = nextbank()
                        MM(ps(b1)[0:64, :], [(s53[:, kc, 96:160], hT3[:, kc, tsl]) for kc in range(8)], [sb5, HB], [PSB[b1]])
                        rope_evac(krT[0:64, 512 + th * 512: 512 + (th + 1) * 512], b, b1, 64, tsl, [KRB], rtmp)
                    else:
                        sg_, sgB = nstg()
                        ACT(sg_[0:64, :], ps(b)[0:64, :], AF.Copy, [PSB[b]], [sgB])
                        VCOPY(krT[0:64, tsl], sg_[0:64, :], [sgB], [KRB])
                        OUT_DMA(krod[i, :, tsl], sg_[0:64, :], sgB)
                sluq, sbuq = wload(("uq", i))
                suq3 = w3(sluq, 2)
                qn = AR.bf16(4096)
                qn3 = qn.rearrange("p (h t) -> p h t", h=4)
                QNB = Buf()
                qr = AR.bf16(4096)
                qr3 = qr.rearrange("p (h t) -> p h t", h=4)
                QRB = Buf()
                for h in range(4):
                    for th in range(2):
                        tsl = slice(th * 512, (th + 1) * 512)
                        b = nextbank()
                        MM(ps(b), [(suq3[:, kc, h * 128:(h + 1) * 128], cqn3[:, kc, tsl]) for kc in range(2)], [sbuq, CQB], [PSB[b]])
                        COPY(qn3[:, h, tsl], ps(b), [PSB[b]], [QNB])
                        b = nextbank()
                        MM(ps(b)[0:64, :], [(suq3[:, kc, 512 + h * 64: 512 + (h + 1) * 64], cqn3[:, kc, tsl]) for kc in range(2)],
                           [sbuq, CQB], [PSB[b]])
                        if sample:
                            b1 = nextbank()
                            MM(ps(b1)[0:64, :], [(suq3[:, kc, 768 + h * 64: 768 + (h + 1) * 64], cqn3[:, kc, tsl]) for kc in range(2)],
                               [sbuq, CQB], [PSB[b1]])
                            rope_evac(qr3[0:64, h, tsl], b, b1, 64, tsl, [QRB], rtmp)
                        else:
                            COPY(qr3[0:64, h, tsl], ps(b)[0:64, :], [PSB[b]], [QRB])
                slkv, sbkv = wload(("ukv", i))
                skv3 = w3(slkv, 2)
                kn = AR.bf16(4 * SK)
                kn3 = kn.rearrange("p (h t) -> p h t", h=4)
                KNB = Buf()
                nkc = SK // 128
                vml = AR.bf16(nkc * 512)
                vml3 = vml.rearrange("p (c f) -> p c f", c=nkc)
                VMB = Buf()
                for h in range(4):
                    for kt in range(SK // 512):
                        ksl = slice(kt * 512, (kt + 1) * 512)
                        b = nextbank()
                        MM(ps(b), [(skv3[:, kc, h * 128:(h + 1) * 128], ckn3[:, kc, ksl]) for kc in range(2)], [sbkv, CKB], [PSB[b]])
                        COPY(kn3[:, h, ksl], ps(b), [PSB[b]], [KNB])
                for c2 in range(nkc):
                    b = nextbank()
                    MM(ps(b), [(ckn3[:, kc, c2 * 128:(c2 + 1) * 128], skv3[:, kc, 512:1024]) for kc in range(2)], [sbkv, CKB], [PSB[b]])
                    COPY(vml3[:, c2, :], ps(b), [PSB[b]], [VMB])
                PT = [AR.bf16(512) for _ in range(4)]
                PTB = [Buf() for _ in range(4)]
                rden = [AR.f32(512), AR.f32(512)]
                RDB = [Buf(), Buf()]
                it = 0
                for (_sq, q0, nq, kcs) in attn_blocks():
                    qsl = slice(q0, q0 + nq)
                    for h in range(4):
                        bo, bd = (4, 5) if it % 2 == 0 else (6, 7)
                        n = len(kcs)

                        def emitS(j):
                            sb_ = j % 4
                            ks = slice(kcs[j] * 128, (kcs[j] + 1) * 128)
                            MM(ps(sb_)[:, 0:nq], [(kn3[:, h, ks], qn3[:, h, qsl]), (krT[0:64, ks], qr3[0:64, h, qsl])],
                               [KNB, QNB, KRB, QRB], [PSB[sb_]])
                        for j in range(min(2, n)):
                            emitS(j)
                        for j in range(n):
                            sb_ = j % 4
                            ACT(PT[sb_][:, 0:nq], ps(sb_)[:, 0:nq], AF.Exp, [PSB[sb_]], [PTB[sb_]], scale=MLA_SCALE)
                            MM(ps(bo)[:, 0:nq], [(vml3[:, kcs[j], h * 128:(h + 1) * 128], PT[sb_][:, 0:nq])], [VMB, PTB[sb_]], [PSB[bo]],
                               first=(j == 0), last=(j == n - 1))
                            MM(ps(bd)[:, 0:nq], [(ones_b, PT[sb_][:, 0:nq])], [CONSTB, PTB[sb_]], [PSB[bd]], first=(j == 0), last=(j == n - 1))
                            if j + 2 < n:
                                emitS(j + 2)
                        r_ = it % 2
                        ACT(rden[r_][:, 0:nq], ps(bd)[:, 0:nq], AF.Ln, [PSB[bd]], [RDB[r_]])
                        ACT(rden[r_][:, 0:nq], rden[r_][:, 0:nq], AF.Exp, [RDB[r_]], [RDB[r_]], scale=-1.0)
                        TTo(mixT3[:, 4 + h, qsl], ps(bo)[:, 0:nq], rden[r_][:, 0:nq], ALU.mult, [PSB[bo], RDB[r_]], [MIXB[4 + h]])
                        it += 1
                out_proj("oe", i, l)
                S.barrier()
                AR.top = m0

            def mixer_odd(i, l):
                m0 = AR.top
                SK = 1536 if sample else 1024
                nkc = SK // 128
                qT_ = AR.bf16(8192)
                q3 = qT_.rearrange("p (h t) -> p h t", h=8)
                QB = Buf()
                kT_ = AR.bf16(8 * SK)
                k3 = kT_.rearrange("p (h t) -> p h t", h=8)
                KB = Buf()
                vTok = AR.bf16(nkc * 1024)
                v3 = vTok.rearrange("p (c f) -> p c f", c=nkc)
                VB = Buf()
                PT = [AR.bf16(512) for _ in range(4 if sample else 0)]
                PTB = [Buf() for _ in range(4)]
                tA = [AR.f32(512) for _ in range(2)]
                tAB = [Buf() for _ in range(2)]
                rtmp = (tA[0], tAB[0], tA[1], tAB[1])
                ntmp = mk_tmp()
                if not sample:
                    stg = [AR.f32(512) for _ in range(4)]
                    STB = [Buf() for _ in range(4)]
                sk = [0]
                rawk = [0]
                KOFF = 512 if sample else 0
                for nm, dst3, dB in (("oq", q3, QB), ("ok", k3, KB)):
                    toff = KOFF if nm == "ok" else 0
                    if sample and nm == "ok":
                        S.dma_multi("pool", "i6", [(k3[:, c, 0:512], dkTd[i, c * 128:(c + 1) * 128, :]) for c in range(8)],
                                    writes=[KB], join=True)
                        S.dma_multi("pool", "i7", [(v3[:, c, :], dvd[i, c * 128:(c + 1) * 128, :]) for c in range(4)],
                                    writes=[VB], join=True)
                    for b2 in range(2):
                        sl, sb = wload((nm, i, b2))
                        s3 = w3(sl, 8)
                        if sample:
                            slp, sbp = wload((nm + "p", i, b2))
                            sp3 = w3(slp, 8)
                        for mo in range(4):
                            hc = b2 * 4 + mo
                            for th in range(2):
                                tsl = slice(th * 512, (th + 1) * 512)
                                dsl = slice(toff + th * 512, toff + (th + 1) * 512)
                                b = nextbank()
                                MM(ps(b), [(s3[:, kc, mo * 128:(mo + 1) * 128], hT3[:, kc, tsl]) for kc in range(8)], [sb, HB], [PSB[b]])
                                if sample:
                                    b1 = nextbank()
                                    MM(ps(b1), [(sp3[:, kc, mo * 128:(mo + 1) * 128], hT3[:, kc, tsl]) for kc in range(8)], [sbp, HB], [PSB[b1]])
                                    rope_evac(dst3[:, hc, dsl], b, b1, 128, tsl, [dB], rtmp)
                                elif nm == "oq":
                                    COPY(dst3[:, hc, dsl], ps(b), [PSB[b]], [dB])
                                else:
                                    k_ = sk[0] % 4
                                    sk[0] += 1
                                    ACT(stg[k_], ps(b), AF.Copy, [PSB[b]], [STB[k_]])
                                    VCOPY(dst3[:, hc, dsl], stg[k_], [STB[k_]], [dB])
                                    OUT_DMA(dkod[i, hc * 128:(hc + 1) * 128, tsl], stg[k_], STB[k_])
                for b2 in range(2):
                    sl, sb = wload(("ov", i, b2))
                    s3 = w3(sl, 8)
                    for tc in range(8):
                        b = nextbank()
                        MM(ps(b), [(hT3[:, kc, tc * 128:(tc + 1) * 128], s3[:, kc, :]) for kc in range(8)], [sb, HB], [PSB[b]])
                        if sample:
                            COPY(v3[:, 4 + tc, b2 * 512:(b2 + 1) * 512], ps(b), [PSB[b]], [VB])
                        else:
                            k_ = sk[0] % 4
                            sk[0] += 1
                            ACT(stg[k_], ps(b), AF.Copy, [PSB[b]], [STB[k_]])
                            VCOPY(v3[:, tc, b2 * 512:(b2 + 1) * 512], stg[k_], [STB[k_]], [VB])
                            OUT_DMA(dvod[i, tc * 128:(tc + 1) * 128, b2 * 512:(b2 + 1) * 512], stg[k_], STB[k_])
                nset = 1 if sample else 2
                osbs = [AR.f32(1024) for _ in range(nset)]
                OSBs = [Buf() for _ in range(nset)]
                dsbs = [AR.f32(1024) for _ in range(nset)]
                DSBs = [Buf() for _ in range(nset)]
                def mk_part2(h, qsl, nq, o2v, d2v, OSB, DSB):
                    def part2(bap, bbuf):
                        TTo(o2v[:, 0, 0:nq], o2v[:, 0, 0:nq], d2v[:, 1, 0:nq], ALU.mult, [OSB, DSB], [OSB])
                        TTo(o2v[:, 1, 0:nq], o2v[:, 1, 0:nq], d2v[:, 0, 0:nq], ALU.mult, [OSB, DSB], [OSB])
                        STT(o2v[:, 0, 0:nq], o2v[:, 1, 0:nq], misc[:, i:i + 1], o2v[:, 0, 0:nq], ALU.mult, ALU.add, [OSB, MISCB], [OSB])
                        TTo(d2v[:, 0, 0:nq], d2v[:, 0, 0:nq], d2v[:, 1, 0:nq], ALU.mult, [DSB], [DSB])
                        sq, sqB, rs, rsB = ntmp
                        ACT(sq[0][:, 0:nq], o2v[:, 0, 0:nq], AF.Square, [OSB], [sqB[0]])
                        ACT(sq[1][:, 0:nq], d2v[:, 0, 0:nq], AF.Square, [DSB], [sqB[1]], scale=math.sqrt(EPS))
                        MM(bap, [(ones_b, sq[0][:, 0:nq]), (ones_b, sq[1][:, 0:nq])], [sqB[0], sqB[1], CONSTB], [bbuf])
                        ACT(rs[:, 0:nq], bap, AF.Ln, [bbuf], [rsB], scale=1.0 / 128)
                        ACT(rs[:, 0:nq], rs[:, 0:nq], AF.Exp, [rsB], [rsB], scale=-0.5)
                        STT(mixT3[:, h, qsl], o2v[:, 0, 0:nq], misc[:, 2 + i:3 + i], rs[:, 0:nq], ALU.mult, ALU.mult, [OSB, MISCB, rsB], [MIXB[h]])
                    return part2
                nb = 0
                pend = [None]
                if not sample:
                    acc = [(4, 6), (5, 7)]
                    PT2 = [AR.bf16(2048), AR.bf16(2048)]
                    PT2B = [Buf(), Buf()]
                    items = [(q0, nq, kcs, h) for (_sq, q0, nq, kcs) in attn_blocks() for h in range(8)]
                    ps4 = psall[:, 0:2048].rearrange("p (t c) -> p t c", t=4)

                    def emitS_all(idx):
                        q0, nq, kcs, h = items[idx]
                        for j in range(2):
                            ks = slice(kcs[j] * 128, (kcs[j] + 1) * 128)
                            for n_ in range(2):
                                t_ = 2 * j + n_
                                MM(ps(t_)[:, 0:nq], [(k3[n_ * 64:(n_ + 1) * 64, h, ks], q3[n_ * 64:(n_ + 1) * 64, h, q0:q0 + nq])],
                                   [KB, QB], [PSB[t_]])
                    emitS_all(0)
                    for idx, (q0, nq, kcs, h) in enumerate(items):
                        qsl = slice(q0, q0 + nq)
                        pk = idx % 2
                        ptv = PT2[pk].rearrange("p (t c) -> p t c", t=4)
                        ACT(ptv[:, :, 0:nq], ps4[:, :, 0:nq], AF.Exp, [PSB[0], PSB[1], PSB[2], PSB[3]], [PT2B[pk]], scale=DIFF_SCALE)
                        if idx + 1 < len(items):
                            emitS_all(idx + 1)
                        for j in range(2):
                            for n_ in range(2):
                                t_ = 2 * j + n_
                                bo, bd = acc[n_]
                                MM(ps(bo)[:, 0:nq], [(v3[:, kcs[j], h * 128:(h + 1) * 128], ptv[:, t_, 0:nq])], [VB, PT2B[pk]], [PSB[bo]],
                                   first=(j == 0), last=(j == 1))
                                MM(ps(bd)[:, 0:nq], [(ones_b, ptv[:, t_, 0:nq])], [CONSTB, PT2B[pk]], [PSB[bd]], first=(j == 0), last=(j == 1))
                        if pend[0] is not None:
                            pend[0](ps(4)[:, 256:256 + nq], PSB[4])
                            pend[0] = None
                        osb, OSB, dsb, DSB = osbs[nb % nset], OSBs[nb % nset], dsbs[nb % nset], DSBs[nb % nset]
                        nb += 1
                        o2v = osb.rearrange("p (b n) -> p b n", b=2)
                        d2v = dsb.rearrange("p (b n) -> p b n", b=2)
                        pso = psall[:, 4 * 512:6 * 512].rearrange("p (b n) -> p b n", b=2)
                        psd = psall[:, 6 * 512:8 * 512].rearrange("p (b n) -> p b n", b=2)
                        ACT(d2v[:, :, 0:nq], psd[:, :, 0:nq], AF.Copy, [PSB[6], PSB[7]], [DSB])
                        VCOPY(o2v[:, :, 0:nq], pso[:, :, 0:nq], [PSB[4], PSB[5]], [OSB])
                        pend[0] = mk_part2(h, qsl, nq, o2v, d2v, OSB, DSB)
                for (_sq, q0, nq, kcs) in (attn_blocks() if sample else []):
                    qsl = slice(q0, q0 + nq)
                    for h in range(8):
                        n = len(kcs)
                        acc = [(4, 6), (5, 7)]

                        def emitS(j):
                            ks = slice(kcs[j] * 128, (kcs[j] + 1) * 128)
                            for n_ in range(2):
                                sb_ = 2 * (j % 2) + n_
                                MM(ps(sb_)[:, 0:nq], [(k3[n_ * 64:(n_ + 1) * 64, h, ks], q3[n_ * 64:(n_ + 1) * 64, h, qsl])], [KB, QB], [PSB[sb_]])
                        emitS(0)
                        jpt = min(2, n - 1)
                        for j in range(n):
                            for n_ in range(2):
                                sb_ = 2 * (j % 2) + n_
                                ACT(PT[sb_][:, 0:nq], ps(sb_)[:, 0:nq], AF.Exp, [PSB[sb_]], [PTB[sb_]], scale=DIFF_SCALE)
                            if j + 1 < n:
                                emitS(j + 1)
                            for n_ in range(2):
                                sb_ = 2 * (j % 2) + n_
                                bo, bd = acc[n_]
                                MM(ps(bo)[:, 0:nq], [(v3[:, kcs[j], h * 128:(h + 1) * 128], PT[sb_][:, 0:nq])], [VB, PTB[sb_]], [PSB[bo]],
                                   first=(j == 0), last=(j == n - 1))
                                MM(ps(bd)[:, 0:nq], [(ones_b, PT[sb_][:, 0:nq])], [CONSTB, PTB[sb_]], [PSB[bd]], first=(j == 0), last=(j == n - 1))
                            if pend[0] is not None and j == jpt:
                                pend[0](ps(2 * (j % 2))[:, 0:nq], PSB[2 * (j % 2)])
                                pend[0] = None
                        if os.environ.get("KDBG_NOFIN"):
                            VCOPY(mixT3[:, h, qsl], ps(4)[:, 0:nq], [PSB[4], PSB[5], PSB[6], PSB[7]], [MIXB[h]])
                            continue
                        osb, OSB, dsb, DSB = osbs[nb % nset], OSBs[nb % nset], dsbs[nb % nset], DSBs[nb % nset]
                        nb += 1
                        o2v = osb.rearrange("p (b n) -> p b n", b=2)
                        d2v = dsb.rearrange("p (b n) -> p b n", b=2)
                        pso = psall[:, 4 * 512:6 * 512].rearrange("p (b n) -> p b n", b=2)
                        psd = psall[:, 6 * 512:8 * 512].rearrange("p (b n) -> p b n", b=2)
                        ACT(d2v[:, :, 0:nq], psd[:, :, 0:nq], AF.Copy, [PSB[6], PSB[7]], [DSB])
                        VCOPY(o2v[:, :, 0:nq], pso[:, :, 0:nq], [PSB[4], PSB[5]], [OSB])

                        pend[0] = mk_part2(h, qsl, nq, o2v, d2v, OSB, DSB)
                if pend[0] is not None:
                    pend[0](ps(0)[:, 0:256 if not sample else 512], PSB[0])
                    pend[0] = None
                out_proj("oo", i, l)
                S.barrier()
                AR.top = m0

            def ffn(l):
                m0 = AR.top
                actT = AR.bf16(NJ * 1024)
                act3 = actT.rearrange("p (j t) -> p j t", j=NJ)
                ACB = [Buf() for _ in range(NJ)]
                ca = [AR.f32(1024), AR.f32(1024)]
                cg = [AR.f32(1024), AR.f32(1024)]
                CAB = [Buf(), Buf()]
                CGB = [Buf(), Buf()]

                def v3d(ap):
                    return ap.rearrange("p (s t) -> p s t", s=nseq)
                do_ada = ADA_INTERLEAVE and sample and (l + 1 < DEPTH) and (groups[0] == "s")
                for jb in range(11):
                    sl, sb = wload(("up", l, jb))
                    s3 = w3(sl, 8)
                    if do_ada:
                        ada_block(l + 1, jb)
                    for jj in range(2):
                        j = 2 * jb + jj
                        r = j % 2
                        for ag in range(2):
                            bp = nextpair()
                            for th in range(2):
                                MM(ps(bp + th), [(s3[:, kc, (jj * 2 + ag) * 128:(jj * 2 + ag + 1) * 128], hT3[:, kc, th * 512:(th + 1) * 512]) for kc in range(8)],
                                   [sb, HB], [PSB[bp + th]])
                            pp = psall[:, bp * 512:bp * 512 + 1024]
                            cidx = j + ag * NJ
                            w0 = VEC[:, VOFF["cw"] + (l * 3 + 0) * 44 + cidx: VOFF["cw"] + (l * 3 + 0) * 44 + cidx + 1]
                            w1 = VEC[:, VOFF["cw"] + (l * 3 + 1) * 44 + cidx: VOFF["cw"] + (l * 3 + 1) * 44 + cidx + 1]
                            w2 = VEC[:, VOFF["cw"] + (l * 3 + 2) * 44 + cidx: VOFF["cw"] + (l * 3 + 2) * 44 + cidx + 1]
                            bb = VEC[:, VOFF["cb"] + l * 44 + cidx: VOFF["cb"] + l * 44 + cidx + 1]
                            cdst, cB = (ca[r], CAB[r]) if ag == 0 else (cg[r], CGB[r])
                            pbufs = [PSB[bp], PSB[bp + 1]]
                            ACT(cdst, pp, AF.Identity, pbufs + [VECB], [cB], bias=bb, scale=w1)
                            STT(v3d(cdst)[:, :, 1:], v3d(pp)[:, :, :-1], w0, v3d(cdst)[:, :, 1:], ALU.mult, ALU.add, pbufs + [VECB, cB], [cB])
                            STT(v3d(cdst)[:, :, :-1], v3d(pp)[:, :, 1:], w2, v3d(cdst)[:, :, :-1], ALU.mult, ALU.add, pbufs + [VECB, cB], [cB])
                        ACT(ca[r], ca[r], AF.Silu, [CAB[r]], [CAB[r]])
                        TTo(act3[:, j, :], ca[r], cg[r], ALU.mult, [CAB[r], CGB[r]], [ACB[j]])
                for m in range(8):
                    sl, sb = wload(("dn", l, m))
                    s3 = w3(sl, NJ)
                    if do_ada and m == 0:
                        ada_block(l + 1, 11)
                    for th in range(2):
                        tsl = slice(th * 512, (th + 1) * 512)
                        b = nextbank()
                        MM(ps(b), [(s3[:, j, :], act3[:, j, tsl]) for j in range(NJ)], [sb] + ACB, [PSB[b]])
                        STT(xT3[:, m, tsl], ps(b), modcol(l, 5, m), xT3[:, m, tsl], ALU.mult, ALU.add, [PSB[b], MODL[l], XB[m][th]], [XB[m][th]])
                S.barrier()
                AR.top = m0

            for l in range(nlayers):
                S.marks.append((gname + "%d.norm1" % l, S.npe))
                mk_ab(l)
                norm_mod(l, 0)
                S.marks.append((gname + "%d.mixer" % l, S.npe))
                if l % 2 == 0:
                    mixer_even(l // 2, l)
                else:
                    mixer_odd(l // 2, l)
                S.marks.append((gname + "%d.norm2" % l, S.npe))
                norm_mod(l, 1)
                S.marks.append((gname + "%d.ffn" % l, S.npe))
                ffn(l)
            S.marks.append((gname + ".final", S.npe))
            m0 = AR.top
            tmps = [mk_tmp(), mk_tmp()]
            stg = [AR.f32(512) for _ in range(4)]
            STB = [Buf() for _ in range(4)]
            k_ = 0
            for th in range(2):
                tsl = slice(th * 512, (th + 1) * 512)
                rs = rms_stats(lambda c, th: xT3[:, c, th * 512:(th + 1) * 512], lambda c, th: [XB[c][th]], 8, th, 1.0 / D, tmps[th])
                for kc in range(8):
                    STT(stg[k_ % 4], xT3[:, kc, tsl], VEC[:, VOFF["fgain"] + kc:VOFF["fgain"] + kc + 1], rs, ALU.mult, ALU.mult,
                        [XB[kc][th], VECB, tmps[th][3]], [STB[k_ % 4]])
                    OUT_DMA(yd[gname][kc * 128:(kc + 1) * 128, tsl], stg[k_ % 4], STB[k_ % 4])
                    k_ += 1
            S.barrier()
            AR.top = m0

        for gname in groups:
            run_group(gname)
        S.barrier(final=True)
        S.marks.append(("end", S.npe))
        global _MARKS, _LOG
        _MARKS = S.marks
        _LOG = S.log
        print("program: inst=%d waits=%d arena_peak=%d/%d wtotal=%d" % (S.ninst, S.nwait, AR.peak, ARENA_WORDS, WTOTAL))
    return nc, WL, WTOTAL


_CACHE = {}
_MARKS = []
_LOG = {}


def _get_program(nlayers=DEPTH, groups=("s", "p")):
    k = (nlayers, groups)
    if k not in _CACHE:
        _CACHE[k] = build_program(nlayers, groups)
    return _CACHE[k]


def make_in_maps(inp, WL, WTOTAL):
    inp = {k: np.asarray(v) for k, v in inp.items()}
    W = np.empty((128, WTOTAL), np.float32)
    off = 0
    for key, F, fn in WL:
        W[:, off:off + F] = fn(inp)
        off += F
    consts = build_consts()
    vecs = build_vecs(inp)
    wdec = np.zeros((32, 4, 256), np.float32)
    for i in range(2):
        for d in range(2):
            wdec[0:16, i * 2 + d] = inp["gla_w_decay"][i, d]
            wdec[16, i * 2 + d] = inp["gla_b_decay"][i, d]
    wdec = wdec.reshape(32, 1024)
    lamp = np.ascontiguousarray(inp["diff_lambda"].reshape(1, 512)).astype(np.float32)
    maps = []
    for c in range(NCORES):
        cvec = np.stack([inp["c_ctx"], inp["c"][c]], 0)
        cT = np.ascontiguousarray(cvec.reshape(2, 8, 128).transpose(2, 1, 0).reshape(128, 16))
        xs = np.ascontiguousarray(inp["x_sample"][c].T)
        xp = np.ascontiguousarray(inp["x_prompt"][4 * c:4 * c + 4].reshape(1024, 1024).T)
        gst = np.ascontiguousarray(inp["state_gla"][c].transpose(3, 0, 1, 2, 4).reshape(64, 2048))
        ckvT = np.ascontiguousarray(inp["cache_mla_ckv"][c].transpose(0, 2, 1))
        krcT = np.ascontiguousarray(inp["cache_mla_krope"][c].transpose(0, 2, 1))
        dkT = np.ascontiguousarray(inp["cache_diff_k"][c].reshape(2, 512, 1024).transpose(0, 2, 1))
        dv = np.ascontiguousarray(inp["cache_diff_v"][c].reshape(2, 512, 1024))
        maps.append({"wts": W, "consts": consts, "vecs": vecs, "cT": cT, "xsT": xs, "xpT": xp, "gst": gst, "ckvT": ckvT,
                     "krcT": krcT, "dkT": dkT, "dv": dv, "wdec": wdec, "lamp": lamp})
    return maps


def assemble(results):
    ys = np.stack([r["ysT"].T for r in results], 0)
    yp = np.concatenate([r["ypT"].T.reshape(4, 256, 1024) for r in results], 0)
    gs = np.concatenate([r["gso"].reshape(64, 4, 2, 2, 4, 128).transpose(1, 2, 3, 4, 0, 5) for r in results], 0)
    ckv = np.concatenate([r["ckvo"].reshape(2, 256, 4, 256).transpose(2, 0, 3, 1) for r in results], 0)
    kr = np.concatenate([r["kro"].reshape(2, 64, 4, 256).transpose(2, 0, 3, 1) for r in results], 0)
    dk = np.concatenate([r["dko"].reshape(2, 1024, 4, 256).transpose(2, 0, 3, 1).reshape(4, 2, 256, 8, 128) for r in results], 0)
    dv = np.concatenate([r["dvo"].reshape(2, 4, 256, 8, 128).transpose(1, 0, 2, 3, 4) for r in results], 0)
    f = lambda a: np.ascontiguousarray(a, dtype=np.float32)
    return (f(yp), f(ys), f(gs), f(ckv), f(kr), f(dk), f(dv))


def kernel(**inputs):
    nc, WL, WTOTAL = _get_program()
    maps = make_in_maps(inputs, WL, WTOTAL)
    res = run_bass_kernel_spmd(nc, maps, core_ids=list(range(NCORES)))
    return assemble(res.results)
```
